# Optimizing a Trainium2 kernel written in Bass

```python
import math
import jax, jax.numpy as jnp
from jax import lax
import numpy as np

D_MODEL = 1024
BATCH = 32
SEQ = 256
DEPTH = 2
DEC_BATCH = 2
DEC_SEQ = 2048
PAST_LEN = 512

GRID_W = 64
Q_BLOCK = 128
ROPE_THETA = 10000.0
EPS = 1e-6

MLA_HEADS = 8
MLA_NOPE = 64
MLA_ROPE = 32
MLA_V = 64
MLA_Q_LORA = 256
MLA_KV_LORA = 128
MLA_WIDTH = MLA_HEADS * MLA_V
MLA_SCALE = (MLA_NOPE + MLA_ROPE) ** -0.5
DIFF_HEADS = 4
DIFF_HEAD_DIM = 64
DIFF_V = 2 * DIFF_HEAD_DIM
DIFF_WIDTH = DIFF_HEADS * DIFF_V
DIFF_SCALE = DIFF_HEAD_DIM ** -0.5
GQA_HEADS = 8
GQA_KV_HEADS = 2
GQA_HEAD_DIM = 64
GQA_GROUP = GQA_HEADS // GQA_KV_HEADS
GQA_WIDTH = GQA_HEADS * GQA_HEAD_DIM
GQA_SCALE = GQA_HEAD_DIM ** -0.5
SSM_WIDTH = 512
SSM_GROUP = 16
SSM_GROUPS = SSM_WIDTH // SSM_GROUP
SSM_STATE = 64
SSM_DT_MIN = 1e-3
SSM_DT_MAX = 1e-1

N_BRANCH = 4
BRANCH_WIDTH = 512
IN_SPLITS = (MLA_Q_LORA, MLA_KV_LORA, MLA_ROPE,
             DIFF_HEADS * 2 * DIFF_HEAD_DIM, DIFF_HEADS * 2 * DIFF_HEAD_DIM, DIFF_HEADS * DIFF_V,
             GQA_HEADS * GQA_HEAD_DIM, GQA_KV_HEADS * GQA_HEAD_DIM, GQA_KV_HEADS * GQA_HEAD_DIM,
             SSM_WIDTH,
             N_BRANCH * BRANCH_WIDTH,
             N_BRANCH * D_MODEL)
D_IN = sum(IN_SPLITS)

kernel_name = 'hybrid_prefix_dit_mla_diff_gqa_s5'


def _rms_norm(x, g):
    xf = x.astype(jnp.float32)
    y = xf * lax.rsqrt(jnp.mean(xf * xf, axis=-1, keepdims=True) + EPS)
    return (y * g.astype(jnp.float32)).astype(x.dtype)


def _modulate(x, g, shift, scale):
    return _rms_norm(x, g) * (1.0 + scale) + shift


def _rope_tables(n_tok, rot_dim):
    rows = n_tok // GRID_W
    row = jnp.repeat(jnp.arange(rows, dtype=jnp.float32), GRID_W)
    col = jnp.tile(jnp.arange(GRID_W, dtype=jnp.float32), rows)
    quarter = rot_dim // 4
    inv = ROPE_THETA ** (-jnp.arange(quarter, dtype=jnp.float32) / quarter)
    ang = jnp.concatenate([row[:, None] * inv, col[:, None] * inv], axis=-1)
    return jnp.cos(ang), jnp.sin(ang)


def _apply_rope(x, cos, sin):
    half = x.shape[-1] // 2
    xf = x.astype(jnp.float32)
    x1, x2 = xf[..., :half], xf[..., half:]
    c, s = cos[:, None, :], sin[:, None, :]
    return jnp.concatenate([x1 * c - x2 * s, x1 * s + x2 * c], axis=-1).astype(x.dtype)


def _sweep_query_blocks(fn, qs):
    b, t = qs[0].shape[0], qs[0].shape[1]
    nb = t // Q_BLOCK
    blocked = tuple(jnp.moveaxis(q.reshape((b, nb, Q_BLOCK) + q.shape[2:]), 1, 0) for q in qs)
    out = lax.map(lambda a: fn(*a), blocked)
    out = jnp.moveaxis(out, 0, 1)
    return out.reshape((b, t) + out.shape[3:])


def _softmax_attn(q, k, v, scale):
    s = jnp.einsum('bqhgd,bshd->bhgqs', q, k).astype(jnp.float32) * scale
    p = jax.nn.softmax(s, axis=-1)
    return jnp.einsum('bhgqs,bshe->bqhge', p.astype(v.dtype), v)


def _lin_combine(left, right):
    a1, b1 = left
    a2, b2 = right
    return a2 * a1, a2 * b1 + b2


def _ssm_direction(u, s0, a_re, a_im, log_dt, b_re, b_im, c_re, c_im):
    f32 = jnp.float32
    lam = lax.complex(a_re.astype(f32), a_im.astype(f32))
    dt = jnp.exp(log_dt.astype(f32))[:, None]
    abar = jnp.exp(lam * dt)
    bbar = ((abar - 1.0) / lam)[..., None] * lax.complex(b_re.astype(f32), b_im.astype(f32))
    cmat = lax.complex(c_re.astype(f32), c_im.astype(f32))
    bu = jnp.einsum('btgn,gpn->btgp', u.astype(jnp.complex64), bbar)
    if s0 is not None:
        bu = bu.at[:, 0].add(abar * s0)
    a = jnp.broadcast_to(abar, bu.shape)
    _, hs = lax.associative_scan(_lin_combine, (a, bu), axis=1)
    y = jnp.real(jnp.einsum('btgp,gnp->btgn', hs, cmat))
    return y, hs[:, -1]


def _mixer(h, p, lam_init, rope, ctx):
    b, t, _ = h.shape
    points, acc = [], 0
    for width in IN_SPLITS[:-1]:
        acc += width
        points.append(acc)
    (q_a, kv_a, k_pe, dq, dk, dv, gq, gk, gv, u, gate_in, merge_in) = jnp.split(h @ p['w_in'], points, axis=-1)
    latent = ctx is not None

    q = (_rms_norm(q_a, p['mla_q_norm']) @ p['w_mla_q_b']).reshape(b, t, MLA_HEADS, MLA_NOPE + MLA_ROPE)
    q_nope, q_pe = q[..., :MLA_NOPE], q[..., MLA_NOPE:]
    c_kv = _rms_norm(kv_a, p['mla_kv_norm'])
    k_pe = k_pe[:, :, None, :]
    if latent:
        q_pe = _apply_rope(q_pe, *rope['mla'])
        ckv_all = jnp.concatenate([ctx['mla_ckv'], c_kv], axis=1)
        kpe_all = jnp.concatenate([ctx['mla_krope'][:, :, None, :], _apply_rope(k_pe, *rope['mla'])], axis=1)
    else:
        ckv_all, kpe_all = c_kv, k_pe
    s_len = ckv_all.shape[1]
    kv = (ckv_all @ p['w_mla_kv_b']).reshape(b, s_len, MLA_HEADS, MLA_NOPE + MLA_V)
    k_a = jnp.concatenate([kv[..., :MLA_NOPE], jnp.broadcast_to(kpe_all, (b, s_len, MLA_HEADS, MLA_ROPE))], axis=-1)
    v_a = kv[..., MLA_NOPE:]
    q_full = jnp.concatenate([q_nope, q_pe], axis=-1)[:, :, :, None, :]
    o_a = _sweep_query_blocks(lambda qb: _softmax_attn(qb, k_a, v_a, MLA_SCALE), (q_full,)).reshape(b, t, MLA_WIDTH)

    dq = dq.reshape(b, t, DIFF_HEADS * 2, DIFF_HEAD_DIM)
    dk = dk.reshape(b, t, DIFF_HEADS * 2, DIFF_HEAD_DIM)
    dv = dv.reshape(b, t, DIFF_HEADS, DIFF_V)
    if latent:
        dq = _apply_rope(dq, *rope['diff'])
        dk_all = jnp.concatenate([ctx['diff_k'], _apply_rope(dk, *rope['diff']).reshape(b, t, DIFF_HEADS, 2, DIFF_HEAD_DIM)], axis=1)
        dv_all = jnp.concatenate([ctx['diff_v'], dv], axis=1)
    else:
        dk_all, dv_all = dk.reshape(b, t, DIFF_HEADS, 2, DIFF_HEAD_DIM), dv
    dq = dq.reshape(b, t, DIFF_HEADS, 2, DIFF_HEAD_DIM)
    f32 = jnp.float32
    lam = (jnp.exp(jnp.sum(p['diff_lq1'].astype(f32) * p['diff_lk1'].astype(f32)))
           - jnp.exp(jnp.sum(p['diff_lq2'].astype(f32) * p['diff_lk2'].astype(f32))) + lam_init)

    def diff_block(qb):
        s = jnp.einsum('bqhmd,bshmd->bhmqs', qb, dk_all).astype(f32) * DIFF_SCALE
        pr = jax.nn.softmax(s, axis=-1)
        w = pr[:, :, 0] - lam * pr[:, :, 1]
        return jnp.einsum('bhqs,bshe->bqhe', w.astype(dv_all.dtype), dv_all)

    o_b = _sweep_query_blocks(diff_block, (dq,))
    o_b = (_rms_norm(o_b, p['diff_subln']) * (1.0 - lam_init)).reshape(b, t, DIFF_WIDTH)

    gq = _rms_norm(gq.reshape(b, t, GQA_HEADS, GQA_HEAD_DIM), p['gqa_q_norm'])
    gk = _rms_norm(gk.reshape(b, t, GQA_KV_HEADS, GQA_HEAD_DIM), p['gqa_k_norm'])
    gv = gv.reshape(b, t, GQA_KV_HEADS, GQA_HEAD_DIM)
    if latent:
        gq = _apply_rope(gq, *rope['gqa'])
        gk_all = jnp.concatenate([ctx['gqa_k'], _apply_rope(gk, *rope['gqa'])], axis=1)
        gv_all = jnp.concatenate([ctx['gqa_v'], gv], axis=1)
    else:
        gk_all, gv_all = gk, gv
    gq5 = gq.reshape(b, t, GQA_KV_HEADS, GQA_GROUP, GQA_HEAD_DIM)
    o_c = _sweep_query_blocks(lambda qb: _softmax_attn(qb, gk_all, gv_all, GQA_SCALE), (gq5,)).reshape(b, t, GQA_WIDTH)

    uf = u.astype(f32)
    ug = uf.reshape(b, t, SSM_GROUPS, SSM_GROUP)
    if latent:
        st = ctx['ssm'].astype(f32)
        s0_f = lax.complex(st[:, 0, ..., 0], st[:, 0, ..., 1])
        s0_b = lax.complex(st[:, 1, ..., 0], st[:, 1, ..., 1])
    else:
        s0_f = s0_b = None
    y_f, hf_last = _ssm_direction(ug, s0_f, p['ssm_a_re'][0], p['ssm_a_im'][0], p['ssm_log_dt'][0],
                                  p['ssm_b_re'][0], p['ssm_b_im'][0], p['ssm_c_re'][0], p['ssm_c_im'][0])
    y_b, hb_last = _ssm_direction(jnp.flip(ug, axis=1), s0_b, p['ssm_a_re'][1], p['ssm_a_im'][1], p['ssm_log_dt'][1],
                                  p['ssm_b_re'][1], p['ssm_b_im'][1], p['ssm_c_re'][1], p['ssm_c_im'][1])
    y = (y_f + jnp.flip(y_b, axis=1)).reshape(b, t, SSM_WIDTH) + p['ssm_d'].astype(f32) * uf
    y = jax.nn.gelu(y).astype(h.dtype)
    o_d = y * jax.nn.sigmoid(y @ p['ssm_glu_w'] + p['ssm_glu_b'])

    br = jnp.stack([o_a, o_b, o_c, o_d], axis=2) * jax.nn.silu(gate_in.reshape(b, t, N_BRANCH, BRANCH_WIDTH))
    br = jnp.einsum('btnw,nwd->btnd', br, p['w_branch_out'])
    merged = jnp.sum(jax.nn.sigmoid(merge_in.reshape(b, t, N_BRANCH, D_MODEL)) * br, axis=2)
    out = merged @ p['w_out']
    if latent:
        return out, None
    ssm_state = jnp.stack([hf_last, hb_last], axis=1)
    side = (c_kv, k_pe[:, :, 0, :], dk_all, dv_all, gk_all, gv_all,
            jnp.stack([jnp.real(ssm_state), jnp.imag(ssm_state)], axis=-1))
    return out, side


def setup_inputs(seed: int = 0) -> dict:
    key = jax.random.key(seed)
    ks = iter(jax.random.split(key, 64))
    f32 = jnp.float32

    def nrm(shape, s=1.0):
        return jax.random.normal(next(ks), shape, f32) * s

    hw = (DEPTH, 2, SSM_GROUPS, SSM_STATE)
    return {
        'x_prompt': nrm((BATCH, SEQ, D_MODEL)),
        'x_sample': nrm((DEC_BATCH, DEC_SEQ, D_MODEL)),
        'cache_mla_ckv': nrm((DEC_BATCH, DEPTH, PAST_LEN, MLA_KV_LORA)),
        'cache_mla_krope': nrm((DEC_BATCH, DEPTH, PAST_LEN, MLA_ROPE)),
        'cache_diff_k': nrm((DEC_BATCH, DEPTH, PAST_LEN, DIFF_HEADS, 2, DIFF_HEAD_DIM)),
        'cache_diff_v': nrm((DEC_BATCH, DEPTH, PAST_LEN, DIFF_HEADS, DIFF_V)),
        'cache_gqa_k': nrm((DEC_BATCH, DEPTH, PAST_LEN, GQA_KV_HEADS, GQA_HEAD_DIM)),
        'cache_gqa_v': nrm((DEC_BATCH, DEPTH, PAST_LEN, GQA_KV_HEADS, GQA_HEAD_DIM)),
        'state_ssm': nrm((DEC_BATCH, DEPTH, 2, SSM_GROUPS, SSM_STATE, 2), 0.5),
        'c': nrm((DEC_BATCH, D_MODEL)),
        'c_ctx': nrm((D_MODEL,)),
        'norm_g': 1.0 + nrm((DEPTH, D_MODEL), 0.01),
        'w_mod': nrm((DEPTH, D_MODEL, 3 * D_MODEL), D_MODEL ** -0.5),
        'b_mod': nrm((DEPTH, 3 * D_MODEL), 0.01),
        'w_in': nrm((DEPTH, D_MODEL, D_IN), D_MODEL ** -0.5),
        'mla_q_norm': 1.0 + nrm((DEPTH, MLA_Q_LORA), 0.01),
        'w_mla_q_b': nrm((DEPTH, MLA_Q_LORA, MLA_HEADS * (MLA_NOPE + MLA_ROPE)), MLA_Q_LORA ** -0.5),
        'mla_kv_norm': 1.0 + nrm((DEPTH, MLA_KV_LORA), 0.01),
        'w_mla_kv_b': nrm((DEPTH, MLA_KV_LORA, MLA_HEADS * (MLA_NOPE + MLA_V)), MLA_KV_LORA ** -0.5),
        'diff_lq1': nrm((DEPTH, DIFF_HEAD_DIM), 0.1),
        'diff_lk1': nrm((DEPTH, DIFF_HEAD_DIM), 0.1),
        'diff_lq2': nrm((DEPTH, DIFF_HEAD_DIM), 0.1),
        'diff_lk2': nrm((DEPTH, DIFF_HEAD_DIM), 0.1),
        'diff_subln': 1.0 + nrm((DEPTH, DIFF_V), 0.01),
        'gqa_q_norm': 1.0 + nrm((DEPTH, GQA_HEAD_DIM), 0.01),
        'gqa_k_norm': 1.0 + nrm((DEPTH, GQA_HEAD_DIM), 0.01),
        'ssm_a_re': -0.5 + nrm(hw, 0.01),
        'ssm_a_im': jnp.broadcast_to(math.pi * jnp.arange(SSM_STATE, dtype=f32), hw) + nrm(hw, 0.01),
        'ssm_log_dt': jax.random.uniform(next(ks), (DEPTH, 2, SSM_GROUPS), f32,
                                         minval=math.log(SSM_DT_MIN), maxval=math.log(SSM_DT_MAX)),
        'ssm_b_re': nrm((DEPTH, 2, SSM_GROUPS, SSM_STATE, SSM_GROUP), (2 * SSM_GROUP) ** -0.5),
        'ssm_b_im': nrm((DEPTH, 2, SSM_GROUPS, SSM_STATE, SSM_GROUP), (2 * SSM_GROUP) ** -0.5),
        'ssm_c_re': nrm((DEPTH, 2, SSM_GROUPS, SSM_GROUP, SSM_STATE), (2 * SSM_STATE) ** -0.5),
        'ssm_c_im': nrm((DEPTH, 2, SSM_GROUPS, SSM_GROUP, SSM_STATE), (2 * SSM_STATE) ** -0.5),
        'ssm_d': 1.0 + nrm((DEPTH, SSM_WIDTH), 0.1),
        'ssm_glu_w': nrm((DEPTH, SSM_WIDTH, SSM_WIDTH), SSM_WIDTH ** -0.5),
        'ssm_glu_b': nrm((DEPTH, SSM_WIDTH), 0.01),
        'w_branch_out': nrm((DEPTH, N_BRANCH, BRANCH_WIDTH, D_MODEL), BRANCH_WIDTH ** -0.5),
        'w_out': nrm((DEPTH, D_MODEL, D_MODEL), D_MODEL ** -0.5),
        'final_norm': 1.0 + nrm((D_MODEL,), 0.01),
    }


def reference(x_prompt, x_sample, cache_mla_ckv, cache_mla_krope, cache_diff_k, cache_diff_v,
              cache_gqa_k, cache_gqa_v, state_ssm, c, c_ctx, norm_g, w_mod, b_mod, w_in,
              mla_q_norm, w_mla_q_b, mla_kv_norm, w_mla_kv_b, diff_lq1, diff_lk1, diff_lq2, diff_lk2,
              diff_subln, gqa_q_norm, gqa_k_norm, ssm_a_re, ssm_a_im, ssm_log_dt, ssm_b_re, ssm_b_im,
              ssm_c_re, ssm_c_im, ssm_d, ssm_glu_w, ssm_glu_b, w_branch_out, w_out, final_norm):
    n_lat = x_sample.shape[1]
    rope = {'mla': _rope_tables(n_lat, MLA_ROPE),
            'diff': _rope_tables(n_lat, DIFF_HEAD_DIM),
            'gqa': _rope_tables(n_lat, GQA_HEAD_DIM)}
    xp, xs = x_prompt, x_sample
    sides = []
    for l in range(DEPTH):
        p = dict(w_in=w_in[l], mla_q_norm=mla_q_norm[l], w_mla_q_b=w_mla_q_b[l], mla_kv_norm=mla_kv_norm[l],
                 w_mla_kv_b=w_mla_kv_b[l], diff_lq1=diff_lq1[l], diff_lk1=diff_lk1[l], diff_lq2=diff_lq2[l],
                 diff_lk2=diff_lk2[l], diff_subln=diff_subln[l], gqa_q_norm=gqa_q_norm[l],
                 gqa_k_norm=gqa_k_norm[l], ssm_a_re=ssm_a_re[l], ssm_a_im=ssm_a_im[l],
                 ssm_log_dt=ssm_log_dt[l], ssm_b_re=ssm_b_re[l], ssm_b_im=ssm_b_im[l],
                 ssm_c_re=ssm_c_re[l], ssm_c_im=ssm_c_im[l], ssm_d=ssm_d[l], ssm_glu_w=ssm_glu_w[l],
                 ssm_glu_b=ssm_glu_b[l], w_branch_out=w_branch_out[l], w_out=w_out[l])
        lam_init = 0.8 - 0.6 * math.exp(-0.3 * l)
        sh_c, sc_c, ga_c = jnp.split(jax.nn.silu(c_ctx) @ w_mod[l] + b_mod[l], 3, axis=-1)
        out_p, side = _mixer(_modulate(xp, norm_g[l], sh_c, sc_c), p, lam_init, None, None)
        xp = xp + ga_c * out_p
        sides.append(side)
        sh_s, sc_s, ga_s = jnp.split((jax.nn.silu(c) @ w_mod[l] + b_mod[l])[:, None, :], 3, axis=-1)
        ctx = {'mla_ckv': cache_mla_ckv[:, l], 'mla_krope': cache_mla_krope[:, l],
               'diff_k': cache_diff_k[:, l], 'diff_v': cache_diff_v[:, l],
               'gqa_k': cache_gqa_k[:, l], 'gqa_v': cache_gqa_v[:, l], 'ssm': state_ssm[:, l]}
        out_s, _ = _mixer(_modulate(xs, norm_g[l], sh_s, sc_s), p, lam_init, rope, ctx)
        xs = xs + ga_s * out_s
    y_prompt = _rms_norm(xp, final_norm)
    y_sample = _rms_norm(xs, final_norm)
    new_mla_ckv = jnp.stack([s[0] for s in sides], axis=1)
    new_mla_krope = jnp.stack([s[1] for s in sides], axis=1)
    new_diff_k = jnp.stack([s[2] for s in sides], axis=1)
    new_diff_v = jnp.stack([s[3] for s in sides], axis=1)
    new_gqa_k = jnp.stack([s[4] for s in sides], axis=1)
    new_gqa_v = jnp.stack([s[5] for s in sides], axis=1)
    new_state_ssm = jnp.stack([s[6] for s in sides], axis=1)
    return (y_prompt, y_sample, new_mla_ckv, new_mla_krope, new_diff_k, new_diff_v, new_gqa_k, new_gqa_v, new_state_ssm)
```

```python
import math
from contextlib import ExitStack
import numpy as np
import concourse.bass as bass
import concourse.mybir as mybir
from concourse.bass_utils import run_bass_kernel_spmd

F32 = mybir.dt.float32
BF16 = mybir.dt.bfloat16
I32 = mybir.dt.int32
AF = mybir.ActivationFunctionType
ALU = mybir.AluOpType
AX = mybir.AxisListType
ENG = ['pe', 'act', 'dve', 'pool', 'sp']
NDSEM = 12
DEBUG = False

D = 1024
DEPTH = 2
NP_SEQ = 4
TSEQ = 256
TS = 2048
PAST = 512
SMAX = PAST + TS
EPS = 1e-6
D_IN = 9376
C_QA, C_KVA, C_KPE, C_DQ, C_DK, C_DV, C_GQ, C_GK, C_GV, C_U, C_GATE, C_MERGE = (
    0, 256, 384, 416, 928, 1440, 1952, 2464, 2592, 2720, 3232, 5280)
MLA_SCALE = 96 ** -0.5
SCALE64 = 64 ** -0.5
PAGE = 512
NPOW = 11


def keys_of(b):
    k = b.key if hasattr(b, 'key') else b
    if isinstance(k, list):
        return k
    return [k]


class Buf:
    def __init__(self, t, key):
        self.t = t
        self.key = key

    def __getitem__(self, idx):
        return self.t[idx]


class Ring:
    def __init__(self, bufs):
        self.bufs = bufs
        self.i = 0

    def next(self):
        b = self.bufs[self.i % len(self.bufs)]
        self.i += 1
        return b


class _Rec:
    def __getattr__(self, name):
        def f(*a, **kw):
            self.call = (name, a, kw)
            return None
        return f


class Prog:
    def __init__(self, nc):
        self.nc = nc
        self.stream = {e: [] for e in ENG}
        self.count = {}
        self.waited = {e: {} for e in ENG}
        self.wr = {}
        self.rd = {}
        self.dsem_i = {e: 0 for e in ENG}
        self.nops = 0

    def _wait(self, eng, s, v):
        if v > 0 and self.waited[eng].get(s, 0) < v:
            self.stream[eng].append(('wait', s, v))
            self.waited[eng][s] = v

    def op(self, eng, fn, reads=(), writes=(), dma=False, milestone=True):
        if getattr(self, 'maxops', None) is not None and self.nops >= self.maxops:
            return
        self.nops += 1
        rec = _Rec()
        fn(rec)
        fn = rec.call
        if not hasattr(self, 'log'):
            self.log = []
        self.log.append((eng, rec.call[0], dma))
        needs = {}

        def need(d):
            if d:
                for s, v in d.items():
                    if v > needs.get(s, 0):
                        needs[s] = v
        rk = [k for b in reads for k in keys_of(b)]
        wk = [k for b in writes for k in keys_of(b)]
        for k in rk:
            need(self.wr.get(k))
        for k in wk:
            need(self.wr.get(k))
            need(self.rd.get(k))
        for s, v in needs.items():
            if eng == 'pe' and s == 'c_pe':
                continue
            self._wait(eng, s, v)
        if dma:
            sem = 'd_%s_%d' % (eng, self.dsem_i[eng] % NDSEM)
            self.dsem_i[eng] += 1
            amt = 16
            self._wait(eng, sem, self.count.get(sem, 0))
        else:
            sem, amt = 'c_' + eng, 1
        if milestone:
            self.count[sem] = self.count.get(sem, 0) + amt
            val = self.count[sem]
            self.stream[eng].append(('op', fn, sem, amt))
        else:
            val = self.count.get(sem, 0) + amt
            self.stream[eng].append(('op', fn, None, 0))
        for k in rk:
            d = self.rd.setdefault(k, {})
            if d.get(sem, 0) < val:
                d[sem] = val
        for k in wk:
            d = self.wr.setdefault(k, {})
            if d.get(sem, 0) < val:
                d[sem] = val

    def finish(self):
        for s, v in self.count.items():
            self._wait('sp', s, v)

    def emit(self, stack):
        nc = self.nc
        sems = {}
        for s in self.count:
            sems[s] = stack.enter_context(nc.semaphore(s))
        block = stack.enter_context(nc.Block())

        def run(e, name):
            for it in self.stream[name]:
                if it[0] == 'wait':
                    e.wait_ge(sems[it[1]], it[2])
                else:
                    nm, a, kw = it[1]
                    ins = getattr(e, nm)(*a, **kw)
                    if it[2] is not None:
                        ins.then_inc(sems[it[2]], it[3])

        @block.tensor
        def _(e):
            run(e, 'pe')

        @block.scalar
        def _(e):
            run(e, 'act')

        @block.vector
        def _(e):
            run(e, 'dve')

        @block.gpsimd
        def _(e):
            run(e, 'pool')

        @block.sync
        def _(e):
            run(e, 'sp')


class Arena:
    def __init__(self, tensor, words):
        self.t = tensor
        self.words = words
        self.top = 0
        self.peak = 0

    def alloc(self, shape, dt):
        free = 1
        for s in shape[1:]:
            free *= s
        w = free if dt in (F32, I32) else (free + 1) // 2
        npg = (w + PAGE - 1) // PAGE
        off = self.top
        self.top += npg * PAGE
        self.peak = max(self.peak, self.top)
        assert self.top <= self.words, ('arena overflow', self.top, self.words)
        v = self.t[:, off:off + npg * PAGE]
        if dt != F32:
            v = v.bitcast(dt)
        v = v[0:shape[0], 0:free]
        if len(shape) == 3:
            v = v.rearrange("p (a b) -> p a b", a=shape[1], b=shape[2])
        elif len(shape) == 4:
            v = v.rearrange("p (a b c) -> p a b c", a=shape[1], b=shape[2], c=shape[3])
        elif len(shape) == 5:
            v = v.rearrange("p (a b c d) -> p a b c d", a=shape[1], b=shape[2], c=shape[3], d=shape[4])
        return Buf(v, ['ar%d' % (off // PAGE + i) for i in range(npg)])


def build_program(stop=None, debug=False, maxops=None):
    global DEBUG
    DEBUG = debug
    nc = bass.Bass("TRN2", target_bir_lowering=False)
    st = ExitStack()
    P = Prog(nc)
    P.maxops = maxops
    uid = [0]

    def sb(shape, dt, name=None):
        uid[0] += 1
        nm = '%s_%d' % (name or 'sb', uid[0])
        return Buf(st.enter_context(nc.sbuf_tensor(nm, list(shape), dt)), nm)

    def din(name, shape, dt=F32):
        return nc.dram_tensor(name, list(shape), dt, kind="ExternalInput").ap()

    def dout(name, shape, dt=F32):
        return nc.dram_tensor(name, list(shape), dt, kind="ExternalOutput").ap()

    def dscr(name, shape, dt):
        if DEBUG:
            return nc.dram_tensor(name, list(shape), dt, kind="ExternalOutput").ap()
        return nc.dram_tensor(name, list(shape), dt).ap()

    I = {}
    I['xp'] = din('xp', [NP_SEQ * TSEQ, D])
    I['xs'] = din('xs', [TS, D])
    I['c_ckv'] = din('c_ckv', [DEPTH, PAST, 128])
    I['c_krope'] = din('c_krope', [DEPTH, PAST, 32])
    I['c_dk'] = din('c_dk', [DEPTH, PAST, 512])
    I['c_dv'] = din('c_dv', [DEPTH, PAST, 512])
    I['c_gk'] = din('c_gk', [DEPTH, PAST, 128])
    I['c_gv'] = din('c_gv', [DEPTH, PAST, 128])
    I['st_ssm'] = din('st_ssm', [DEPTH, 2, 16, 128, 2])
    I['cvec'] = din('cvec', [128, 8, 2])
    I['colvecs'] = din('colvecs', [128, 64])
    I['bmod'] = din('bmod', [128, 2, 24])
    I['w_mod'] = din('w_mod', [DEPTH, D, 3 * D])
    I['w_in'] = din('w_in', [DEPTH, D, D_IN])
    I['w_in_sw'] = din('w_in_sw', [DEPTH, D, 1696])
    I['w_qb'] = din('w_qb', [DEPTH, 256, 1024])
    I['w_kvb'] = din('w_kvb', [DEPTH, 128, 1024])
    I['lam_rows'] = din('lam_rows', [DEPTH, 4, 64])
    I['w_glu'] = din('w_glu', [DEPTH, 512, 512])
    I['w_bo'] = din('w_bo', [DEPTH, 4, 512, D])
    I['w_out'] = din('w_out', [DEPTH, D, D])
    I['ident'] = din('ident', [128, 128])
    I['blk64'] = din('blk64', [128, 128])
    I['rope64'] = din('rope64', [2, 128, TS])
    I['rope32'] = din('rope32', [2, 32, TS])
    I['ssm_a'] = din('ssm_a', [DEPTH, 2, 3, 128, 16])
    I['ssm_b'] = din('ssm_b', [DEPTH, 2, 2, 32, 64, 16])
    I['ssm_c'] = din('ssm_c', [DEPTH, 2, 2, 32, 16, 64])
    I['mask4'] = din('mask4', [128, 4])
    I['perm64'] = din('perm64', [128, 128])
    I['perm32'] = din('perm32', [128, 128])
    I['rope64o'] = din('rope64o', [2, 128, 512])
    I['rope32o'] = din('rope32o', [2, 32, 512])
    O = {}
    O['yp'] = dout('yp', [NP_SEQ * TSEQ, D])
    O['ys'] = dout('ys', [512, D])
    O['n_ckv'] = dout('n_ckv', [DEPTH, NP_SEQ * TSEQ, 128])
    O['n_krope'] = dout('n_krope', [DEPTH, NP_SEQ * TSEQ, 32])
    O['n_dk'] = dout('n_dk', [DEPTH, NP_SEQ * TSEQ, 512])
    O['n_dv'] = dout('n_dv', [DEPTH, NP_SEQ * TSEQ, 512])
    O['n_gk'] = dout('n_gk', [DEPTH, NP_SEQ * TSEQ, 128])
    O['n_gv'] = dout('n_gv', [DEPTH, NP_SEQ * TSEQ, 128])
    O['n_ssm'] = dout('n_ssm', [DEPTH, 128, 256])

    class Unit:
        pass
    units = []
    for ui in range(3):
        u = Unit()
        u.i = ui
        u.sample = (ui == 2)
        u.T = TS if u.sample else 512
        u.nblk = u.T // 512
        u.nss = 1 if u.sample else 2
        u.Ts = TS if u.sample else TSEQ
        u.nseq = 1 if u.sample else 2
        u.S = SMAX if u.sample else 512
        u.koff = PAST if u.sample else 0
        u.xin = I['xs'] if u.sample else I['xp'][ui * 512:(ui + 1) * 512, :]
        u.yout = O['ys'] if u.sample else O['yp'][ui * 512:(ui + 1) * 512, :]
        u.tok0 = 0 if u.sample else ui * 512
        u.xres = dscr('xres%d' % ui, [128, 8, u.T], F32)
        u.hTs = dscr('s_hT%d' % ui, [128, 8, u.T], BF16)
        u.uTs = dscr('s_uT%d' % ui, [128, 4, u.T], BF16)
        u.oTs = dscr('s_oT%d' % ui, [128, 16, u.T], BF16)
        u.xo = dscr('s_xo%d' % ui, [128, 8, 512], F32)
        u.hTo = dscr('s_hTo%d' % ui, [128, 8, 512], BF16)
        units.append(u)
    scr = {}
    scr['q_mn'] = dscr('s_qmn', [128, 4, TS], BF16)
    scr['q_mp'] = dscr('s_qmp', [32, 8, TS], BF16)
    scr['q_d'] = dscr('s_qd', [128, 4, TS], BF16)
    scr['q_g'] = dscr('s_qg', [128, 4, TS], BF16)
    scr['k_mn'] = dscr('s_kmn', [128, 4, SMAX], BF16)
    scr['k_mp'] = dscr('s_kmp', [32, SMAX], BF16)
    scr['k_d'] = dscr('s_kd', [128, 4, SMAX], BF16)
    scr['k_g'] = dscr('s_kg', [128, 1, SMAX], BF16)
    scr['v_m'] = dscr('s_vm', [SMAX, 512], BF16)
    scr['v_d'] = dscr('s_vd', [SMAX, 512], BF16)
    scr['v_g'] = dscr('s_vg', [SMAX, 128], BF16)

    ident = sb([128, 128], F32, 'ident')
    blk64 = sb([128, 128], BF16, 'blk64')
    ones = sb([128, 128], BF16, 'ones')
    colv = sb([128, 64], F32, 'colv')
    cvec = sb([128, 8, 2], F32, 'cvec')
    bmod = sb([128, 2, 24], F32, 'bmod')
    silc = sb([128, 8, 2], F32, 'silc')
    modv = sb([128, 2, 24, 2], F32, 'modv')
    gmod = sb([128, 2, 8, 2], F32, 'gmod')
    lamneg = sb([128, 2], F32, 'lamneg')
    mask4 = sb([128, 4], F32, 'mask4')
    perm64 = sb([128, 128], F32, 'perm64')
    perm32 = sb([128, 128], F32, 'perm32')
    tmpf = Ring([sb([128, 512], F32, 'tmpf') for _ in range(20)])
    tmpb = Ring([sb([128, 512], BF16, 'tmpb') for _ in range(8)])
    stage = Ring([sb([128, 512], F32, 'stage') for _ in range(4)])
    stageb = Ring([sb([128, 4, 512], BF16, 'stageb') for _ in range(4)])
    prm = sb([128, 2, 16, 12], F32, 'prm')
    pwc = sb([128, 2, 16, NPOW], F32, 'pwc')
    pws = sb([128, 2, 16, NPOW], F32, 'pws')
    ARENA_WORDS = 64 * PAGE
    arena = Arena(st.enter_context(nc.sbuf_tensor('arena', [128, ARENA_WORDS], F32)), ARENA_WORDS)

    P.op('sp', lambda e: e.dma_start(out=ident[:], in_=I['ident']), writes=[ident], dma=True)
    P.op('pool', lambda e: e.dma_start(out=blk64[:], in_=I['blk64']), writes=[blk64], dma=True)
    P.op('sp', lambda e: e.dma_start(out=colv[:], in_=I['colvecs']), writes=[colv], dma=True)
    P.op('sp', lambda e: e.dma_start(out=cvec[:], in_=I['cvec']), writes=[cvec], dma=True)
    P.op('sp', lambda e: e.dma_start(out=bmod[:], in_=I['bmod']), writes=[bmod], dma=True)
    P.op('dve', lambda e: e.memset(ones[:], 1.0), writes=[ones])
    P.op('sp', lambda e: e.dma_start(out=mask4[:], in_=I['mask4']), writes=[mask4], dma=True)
    P.op('sp', lambda e: e.dma_start(out=perm64[:], in_=I['perm64']), writes=[perm64], dma=True)
    P.op('sp', lambda e: e.dma_start(out=perm32[:], in_=I['perm32']), writes=[perm32], dma=True)
    P.op('act', lambda e: e.activation(out=silc[:], in_=cvec[:], func=AF.Sigmoid), reads=[cvec], writes=[silc])
    P.op('dve', lambda e: e.tensor_tensor(out=silc[:], in0=silc[:], in1=cvec[:], op=ALU.mult),
         reads=[silc, cvec], writes=[silc])

    def cv(l, j, n=1):
        return colv[:, 32 * l + j:32 * l + j + n]

    psb = [Buf(st.enter_context(nc.psum_tensor('ps%d' % i, [128, 512], F32)), 'ps%d' % i) for i in range(8)]
    ring_g = Ring(psb[0:2])
    RG = [ring_g]
    ring_s = Ring(psb[2:4])
    ring_acc = Ring(psb[4:8])
    ring_all = Ring(psb[0:8])

    def mm_group(ps, out_ap, pairs, reads):
        n = len(pairs)
        for i, (lhsT, rhs) in enumerate(pairs):
            P.op('pe', (lambda e, lhsT=lhsT, rhs=rhs, i=i: e.matmul(out_ap, lhsT=lhsT, rhs=rhs,
                                                                     start=(i == 0), stop=(i == n - 1))),
                 reads=reads, writes=[ps], milestone=(i == n - 1))

    def acopy(dst_ap, dst_buf, src_ap, src_buf, scale=None):
        if scale is None:
            P.op('act', lambda e: e.activation(out=dst_ap, in_=src_ap, func=AF.Copy), reads=[src_buf], writes=[dst_buf])
        else:
            P.op('act', lambda e: e.activation(out=dst_ap, in_=src_ap, func=AF.Identity, scale=scale),
                 reads=[src_buf], writes=[dst_buf])

    def vcopy(dst_ap, dst_buf, src_ap, src_buf):
        P.op('dve', lambda e: e.tensor_copy(out=dst_ap, in_=src_ap), reads=[src_buf], writes=[dst_buf])

    tcount = [0]

    def transpose_to(dst_ap, dst_buf, src_ap, src_buf, rows, cols, ring=None, scale=None):
        ps = (ring or RG[0]).next()
        P.op('pe', lambda e: e.transpose(ps[0:cols, 0:rows], src_ap, ident[0:rows, 0:rows]),
             reads=[src_buf, ident], writes=[ps])
        tcount[0] += 1
        if scale is not None or tcount[0] % 2:
            acopy(dst_ap, dst_buf, ps[0:cols, 0:rows], ps, scale)
        else:
            vcopy(dst_ap, dst_buf, ps[0:cols, 0:rows], ps)

    def rstd_of(ps, ps_ap, n, denom, mult=1.0):
        r = tmpf.next()
        P.op('act', lambda e: e.activation(out=r[:, 0:n], in_=ps_ap, func=AF.Ln, scale=1.0 / denom, bias=EPS),
             reads=[ps], writes=[r])
        P.op('act', lambda e: e.activation(out=r[:, 0:n], in_=r[:, 0:n], func=AF.Exp, scale=-0.5,
                                           bias=math.log(mult)), reads=[r], writes=[r])
        return r

    def rstd_from_sq(sq_list, denom, lhs_ones):
        ps = RG[0].next()
        mm_group(ps, ps[:, :], [(lhs_ones, s[:, :]) for s in sq_list], reads=list(sq_list) + [ones, blk64])
        return rstd_of(ps, ps[:, :], 512, denom)

    def modulation(l):
        arena.top = 0
        wm = arena.alloc([128, 8, 3072], F32)
        for k in range(8):
            P.op('sp' if k % 2 == 0 else 'pool', lambda e, k=k: e.dma_start(out=wm[:, k, :], in_=I['w_mod'][l, k * 128:(k + 1) * 128, :]),
                 writes=[wm], dma=True)
        ps = ring_acc.next()
        for j in range(24):
            for k in range(8):
                P.op('pe', (lambda e, k=k, j=j: e.matmul(ps[:, 2 * j:2 * j + 2], lhsT=wm[:, k, j * 128:(j + 1) * 128],
                                                         rhs=silc[:, k, :], start=(k == 0), stop=(k == 7))),
                     reads=[wm, silc], writes=[ps], milestone=(k == 7))
        for pth in range(2):
            P.op('dve', lambda e, pth=pth: e.tensor_tensor(out=modv[:, l, :, pth], in0=ps[:, pth:48:2],
                                                           in1=bmod[:, l, :], op=ALU.add),
                 reads=[ps, bmod], writes=[modv])
            P.op('dve', lambda e, pth=pth: e.scalar_tensor_tensor(out=gmod[:, l, :, pth], in0=modv[:, l, 8:16, pth],
                                                                  scalar=1.0, in1=cv(l, 0, 8), op0=ALU.add,
                                                                  op1=ALU.mult),
                 reads=[modv, colv], writes=[gmod])
        if DEBUG and l == 0:
            dm = dout('dbg_mod', [128, 2, 24, 2])
            P.op('sp', lambda e: e.dma_start(out=dm, in_=modv[:]), reads=[modv], dma=True)
            dg = dout('dbg_gmod', [128, 2, 8, 2])
            P.op('sp', lambda e: e.dma_start(out=dg, in_=gmod[:]), reads=[gmod], dma=True)
        rows = arena.alloc([128, 4, 64], F32)
        P.op('sp', lambda e: e.dma_start(out=rows[:], in_=I['lam_rows'][l].partition_broadcast(128)),
             writes=[rows], dma=True)
        pr = arena.alloc([128, 2, 64], F32)
        P.op('dve', lambda e: e.tensor_tensor(out=pr[:, 0, :], in0=rows[:, 0, :], in1=rows[:, 1, :], op=ALU.mult),
             reads=[rows], writes=[pr])
        P.op('dve', lambda e: e.tensor_tensor(out=pr[:, 1, :], in0=rows[:, 2, :], in1=rows[:, 3, :], op=ALU.mult),
             reads=[rows], writes=[pr])
        sm = arena.alloc([128, 2], F32)
        P.op('dve', lambda e: e.tensor_reduce(out=sm[:, :], in_=pr[:, :, :], axis=AX.X, op=ALU.add),
             reads=[pr], writes=[sm])
        P.op('act', lambda e: e.activation(out=sm[:, :], in_=sm[:, :], func=AF.Exp), reads=[sm], writes=[sm])
        lam_init = 0.8 - 0.6 * math.exp(-0.3 * l)
        P.op('dve', lambda e: e.scalar_tensor_tensor(out=lamneg[:, l:l + 1], in0=sm[:, 1:2], scalar=-lam_init,
                                                     in1=sm[:, 0:1], op0=ALU.add, op1=ALU.subtract),
             reads=[sm], writes=[lamneg])

    def stage_A_unit(l, u):
        arena.top = 0
        pth = 1 if u.sample else 0
        w_in = I['w_in'][l]
        w_sw = I['w_in_sw'][l]
        ring_g = Ring(psb[0:8])
        RG[0] = ring_g
        wring = Ring([arena.alloc([128, 8, 512], BF16) for _ in range(3)])
        xT = arena.alloc([128, 8, 512], F32)
        hT = arena.alloc([128, 8, 512], BF16)
        xtok = Ring([arena.alloc([128, 1024], F32) for _ in range(2)])
        rope64 = arena.alloc([128, 2, 512], F32)
        rope32 = arena.alloc([32, 2, 512], F32)
        wqb = arena.alloc([128, 2, 1024], BF16)
        wkvb = arena.alloc([128, 1024], BF16)
        qan = arena.alloc([128, 2, 512], BF16)
        ckvn_b = arena.alloc([128, 512], BF16)
        cache_b = arena.alloc([128, 512], BF16)

        def load_w(src2d, rows, c0, ncols):
            wb = wring.next()
            kk = rows // 128
            src = src2d[:, c0:c0 + ncols].rearrange("(k p) c -> p k c", p=128)
            P.op('pool', lambda e: e.dma_start(out=wb[:, 0:kk, 0:ncols], in_=src), writes=[wb], dma=True)
            return wb

        P.op('pool', lambda e: e.dma_start(out=wqb[:], in_=I['w_qb'][l].rearrange("(k p) c -> p k c", p=128)),
             writes=[wqb], dma=True)
        P.op('pool', lambda e: e.dma_start(out=wkvb[:, :], in_=I['w_kvb'][l]), writes=[wkvb], dma=True)

        def store(dst, src_buf, src_ap, wkeys):
            P.op('sp', lambda e: e.dma_start(out=dst, in_=src_ap), reads=[src_buf], writes=wkeys, dma=True)

        def proj_fm(wb, c0, rhs_list, rhs_bufs, consumer, M=128):
            ps = ring_g.next()
            mm_group(ps, ps[0:M, :], [(wb[:, k, c0:c0 + M], r) for k, r in enumerate(rhs_list)],
                     reads=[wb] + rhs_bufs)
            consumer(ps)

        def out_tokmajor(dst_rows, src_buf, src_ap, nfeat):
            for tt in range(4):
                sg = stage.next()
                transpose_to(sg[:, 0:nfeat], sg, src_ap[:, tt * 128:(tt + 1) * 128], src_buf, nfeat, 128)
                P.op('sp', lambda e, sg=sg, tt=tt: e.dma_start(out=dst_rows(tt), in_=sg[:, 0:nfeat]),
                     reads=[sg], dma=True)

        def mla_kv_expand(ckv_b, kc0):
            sgb = stageb.next()
            for j in range(4):
                ps = ring_g.next()
                mm_group(ps, ps[:, :], [(wkvb[:, 128 * j:128 * j + 128], ckv_b[:, :])], reads=[wkvb, ckv_b])
                acopy(sgb[:, j, :], sgb, ps[:, :], ps)
            store(scr['k_mn'][:, :, kc0:kc0 + 512], sgb, sgb[:], ['k_mn'])
            for tt in range(4):
                ps = ring_g.next()
                mm_group(ps, ps[:, :], [(ckv_b[:, tt * 128:(tt + 1) * 128], wkvb[:, 512:1024])], reads=[wkvb, ckv_b])
                sg2 = stageb.next()
                vcopy(sg2[:, 0, :], sg2, ps[:, :], ps)
                r0 = kc0 + tt * 128
                P.op('sp', lambda e, sg2=sg2, r0=r0: e.dma_start(out=scr['v_m'][r0:r0 + 128, :], in_=sg2[:, 0, :]),
                     reads=[sg2], writes=['v_m'], dma=True)

        if u.sample:
            def cache_T(src, nfeat, dst_fn):
                for tt in range(4):
                    xt = xtok.next()
                    P.op('sp', lambda e, xt=xt, tt=tt: e.dma_start(out=xt[:, 0:nfeat],
                                                                   in_=src[tt * 128:(tt + 1) * 128, :]),
                         writes=[xt], dma=True)
                    for j in range((nfeat + 127) // 128):
                        w = min(128, nfeat - 128 * j)
                        dst_ap, dst_buf = dst_fn(j, tt, w)
                        transpose_to(dst_ap, dst_buf, xt[:, 128 * j:128 * j + w], xt, 128, w)
            cache_T(I['c_ckv'][l], 128, lambda j, tt, w: (cache_b[:, tt * 128:(tt + 1) * 128], cache_b))
            mla_kv_expand(cache_b, 0)
            sgb = stageb.next()
            cache_T(I['c_krope'][l], 32, lambda j, tt, w: (sgb[0:32, 0, tt * 128:(tt + 1) * 128], sgb))
            store(scr['k_mp'][:, 0:512], sgb, sgb[0:32, 0, :], ['k_mp'])
            sgb2 = stageb.next()
            cache_T(I['c_dk'][l], 512, lambda j, tt, w: (sgb2[:, j, tt * 128:(tt + 1) * 128], sgb2))
            store(scr['k_d'][:, :, 0:512], sgb2, sgb2[:], ['k_d'])
            sgb3 = stageb.next()
            cache_T(I['c_gk'][l], 128, lambda j, tt, w: (sgb3[:, 0, tt * 128:(tt + 1) * 128], sgb3))
            store(scr['k_g'][:, :, 0:512], sgb3, sgb3[:, 0:1, :], ['k_g'])
            for name, dst, w in (('c_dv', 'v_d', 512), ('c_gv', 'v_g', 128)):
                sg = stageb.next()
                P.op('pool', lambda e, sg=sg, name=name, w=w: e.dma_start(
                    out=sg[:, :, 0:w], in_=I[name][l].rearrange("(t p) c -> p t c", p=128)), writes=[sg], dma=True)
                P.op('sp', lambda e, sg=sg, dst=dst, w=w: e.dma_start(
                    out=scr[dst][0:512, :].rearrange("(t p) c -> p t c", p=128), in_=sg[:, :, 0:w]),
                    reads=[sg], writes=[dst], dma=True)

        own_mode = (u.sample and l == DEPTH - 1)
        if own_mode:
            xown = arena.alloc([128, 8, 512], F32)
            hown = arena.alloc([128, 8, 512], BF16)
        def prelude_own():
            for k in range(8):
                acopy(hT[:, k, :], hT, hown[:, k, :], hown)
            P.op('sp', lambda e: e.dma_start(out=u.xo[:], in_=xown[:]), reads=[xown], writes=[('xo', u.i)], dma=True)
            P.op('sp', lambda e: e.dma_start(out=u.hTo[:], in_=hown[:]), reads=[hown], writes=[('hTo', u.i)], dma=True)
            P.op('sp', lambda e: e.dma_start(out=rope64[:], in_=I['rope64o'].rearrange("a p t -> p a t")),
                 writes=[rope64], dma=True)
            P.op('sp', lambda e: e.dma_start(out=rope32[:], in_=I['rope32o'].rearrange("a p t -> p a t")),
                 writes=[rope32], dma=True)

        def prelude(b, t0):
            if l == 0:
                for tt in range(4):
                    xt = xtok.next()
                    r0 = t0 + tt * 128
                    P.op('sp', lambda e, xt=xt, r0=r0: e.dma_start(out=xt[:, :], in_=u.xin[r0:r0 + 128, :]),
                         writes=[xt], dma=True)
                    for k in range(8):
                        transpose_to(xT[:, k, tt * 128:(tt + 1) * 128], xT, xt[:, k * 128:(k + 1) * 128], xt, 128, 128)
                P.op('sp', lambda e, t0=t0: e.dma_start(out=u.xres[:, :, t0:t0 + 512], in_=xT[:]),
                     reads=[xT], writes=[('xres', u.i)], dma=True)
            else:
                P.op('sp', lambda e, t0=t0: e.dma_start(out=xT[:], in_=u.xres[:, :, t0:t0 + 512]),
                     reads=[('xres', u.i)], writes=[xT], dma=True)
            sqs = []
            for k in range(8):
                s = tmpb.next()
                P.op('act', lambda e, s=s, k=k: e.activation(out=s[:, :], in_=xT[:, k, :], func=AF.Square),
                     reads=[xT], writes=[s])
                sqs.append(s)
            r = rstd_from_sq(sqs, float(D), ones[:, :])
            for k in range(8):
                t = tmpf.next()
                P.op('dve', lambda e, t=t, k=k: e.scalar_tensor_tensor(out=t[:, :], in0=xT[:, k, :],
                                                                        scalar=gmod[:, l, k:k + 1, pth], in1=r[:, :],
                                                                        op0=ALU.mult, op1=ALU.mult),
                     reads=[xT, gmod, r], writes=[t])
                P.op('act', lambda e, t=t, k=k: e.activation(out=hT[:, k, :], in_=t[:, :], func=AF.Identity,
                                                             bias=modv[:, l, k:k + 1, pth], scale=1.0),
                     reads=[t, modv], writes=[hT])
            store(u.hTs[:, :, t0:t0 + 512], hT, hT[:], [('hT', u.i)])
            hk = [hT[:, k, :] for k in range(8)]
            if own_mode:
                for k in range(8):
                    if b == 0:
                        P.op('dve', lambda e: e.tensor_scalar(out=xown[:, k, :], in0=xT[:, k, :], scalar1=mask4[:, 0:1],
                                                              scalar2=None, op0=ALU.mult), reads=[xT, mask4], writes=[xown])
                        P.op('dve', lambda e: e.tensor_scalar(out=hown[:, k, :], in0=hT[:, k, :], scalar1=mask4[:, 0:1],
                                                              scalar2=None, op0=ALU.mult), reads=[hT, mask4], writes=[hown])
                    else:
                        P.op('dve', lambda e: e.scalar_tensor_tensor(out=xown[:, k, :], in0=xT[:, k, :],
                                                                     scalar=mask4[:, b:b + 1], in1=xown[:, k, :],
                                                                     op0=ALU.mult, op1=ALU.add),
                             reads=[xT, mask4, xown], writes=[xown])
                        P.op('dve', lambda e: e.scalar_tensor_tensor(out=hown[:, k, :], in0=hT[:, k, :],
                                                                     scalar=mask4[:, b:b + 1], in1=hown[:, k, :],
                                                                     op0=ALU.mult, op1=ALU.add),
                             reads=[hT, mask4, hown], writes=[hown])
            if u.sample:
                P.op('sp', lambda e, t0=t0: e.dma_start(
                    out=rope64[:], in_=I['rope64'][:, :, t0:t0 + 512].rearrange("a p t -> p a t")),
                    writes=[rope64], dma=True)
                P.op('sp', lambda e, t0=t0: e.dma_start(
                    out=rope32[:], in_=I['rope32'][:, :, t0:t0 + 512].rearrange("a p t -> p a t")),
                    writes=[rope32], dma=True)


        passes = [(b, not own_mode, True) for b in range(u.nblk)] + ([('own', True, False)] if own_mode else [])
        for b, do_q, do_k in passes:
            if b == 'own':
                kc0, t0 = 0, 0
                prelude_own()
            else:
                kc0 = u.koff + b * 512
                t0 = b * 512
                prelude(b, t0)
            hk = [hT[:, k, :] for k in range(8)]

            def rope_apply(dst_ap, dst_buf, a_ap, a_buf, s_ap, s_buf, tab, rows):
                t1 = tmpf.next()
                P.op('dve', lambda e: e.tensor_tensor(out=t1[0:rows, :], in0=a_ap, in1=tab[0:rows, 0, :], op=ALU.mult),
                     reads=[a_buf, tab], writes=[t1])
                t2 = tmpf.next()
                P.op('pool', lambda e: e.tensor_tensor(out=t2[0:rows, :], in0=s_ap, in1=tab[0:rows, 1, :], op=ALU.mult),
                     reads=[s_buf, tab], writes=[t2])
                P.op('dve', lambda e: e.tensor_tensor(out=dst_ap, in0=t1[0:rows, :], in1=t2[0:rows, :], op=ALU.add),
                     reads=[t1, t2], writes=[dst_buf])

            def evac_f(M):
                a_f = tmpf.next()

                def cons(ps):
                    acopy(a_f[0:M, :], a_f, ps[0:M, :], ps)
                return a_f, cons

            def perm_apply(src_f, rows, perm):
                dst_f = tmpf.next()
                ps = ring_g.next()
                mm_group(ps, ps[0:rows, :], [(perm[0:rows, 0:rows], src_f[0:rows, :])], reads=[perm, src_f])
                acopy(dst_f[0:rows, :], dst_f, ps[0:rows, :], ps)
                return dst_f

            wb = load_w(w_in, D, 0, 416)
            qa_f = []
            sqs = []
            for j in (range(2) if do_q else []):
                a_f, cons = evac_f(128)
                proj_fm(wb, 128 * j, hk, [hT], cons)
                s = tmpb.next()
                P.op('act', lambda e, s=s, a_f=a_f: e.activation(out=s[:, :], in_=a_f[:, :], func=AF.Square),
                     reads=[a_f], writes=[s])
                sqs.append(s)
                qa_f.append(a_f)
            r = rstd_from_sq(sqs, 256.0, ones[:, :]) if do_q else None
            for j in (range(2) if do_q else []):
                P.op('dve', lambda e, j=j: e.scalar_tensor_tensor(out=qan[:, j, :], in0=qa_f[j][:, :],
                                                                  scalar=cv(l, 8 + j), in1=r[:, :], op0=ALU.mult,
                                                                  op1=ALU.mult),
                     reads=[qa_f[j], colv, r], writes=[qan])
            if do_k:
                ckv_f, cons = evac_f(128)
                proj_fm(wb, C_KVA, hk, [hT], cons)
                s = tmpb.next()
                P.op('act', lambda e, s=s: e.activation(out=s[:, :], in_=ckv_f[:, :], func=AF.Square),
                     reads=[ckv_f], writes=[s])
                r2 = rstd_from_sq([s], 128.0, ones[:, :])
                ckvn_f = tmpf.next()
                P.op('dve', lambda e: e.scalar_tensor_tensor(out=ckvn_f[:, :], in0=ckv_f[:, :], scalar=cv(l, 10),
                                                             in1=r2[:, :], op0=ALU.mult, op1=ALU.mult),
                     reads=[ckv_f, colv, r2], writes=[ckvn_f])
                acopy(ckvn_b[:, :], ckvn_b, ckvn_f[:, :], ckvn_f)
                if not u.sample:
                    out_tokmajor(lambda tt: O['n_ckv'][l, u.tok0 + t0 + tt * 128:u.tok0 + t0 + (tt + 1) * 128, :],
                                 ckvn_f, ckvn_f[:, :], 128)
                kpe_f, cons = evac_f(32)
                proj_fm(wb, C_KPE, hk, [hT], cons, M=32)
                sgb = stageb.next()
                if u.sample:
                    kpes_f = perm_apply(kpe_f, 32, perm32)
                    rope_apply(sgb[0:32, 0, :], sgb, kpe_f[0:32, :], kpe_f, kpes_f[0:32, :], kpes_f, rope32, 32)
                else:
                    out_tokmajor(lambda tt: O['n_krope'][l, u.tok0 + t0 + tt * 128:u.tok0 + t0 + (tt + 1) * 128, :],
                                 kpe_f, kpe_f[0:32, :], 32)
                    acopy(sgb[0:32, 0, :], sgb, kpe_f[0:32, :], kpe_f)
                store(scr['k_mp'][:, kc0:kc0 + 512], sgb, sgb[0:32, 0, :], ['k_mp'])
            if do_q:
                qk = [qan[:, 0, :], qan[:, 1, :]]
                sgb = stageb.next()
                for j in range(4):
                    def cons_q(ps, j=j, sgb=sgb):
                        acopy(sgb[:, j, :], sgb, ps[:, :], ps)
                    proj_fm(wqb, 128 * j, qk, [qan], cons_q)
                store(scr['q_mn'][:, :, t0:t0 + 512], sgb, sgb[:], ['q_mn'])
                sgq = [stageb.next(), stageb.next()]
                for h in range(8):
                    dst = sgq[h // 4]
                    if u.sample:
                        a_f, cons_a = evac_f(32)
                        proj_fm(wqb, 512 + 32 * h, qk, [qan], cons_a, M=32)
                        s_f = perm_apply(a_f, 32, perm32)
                        rope_apply(dst[0:32, h % 4, :], dst, a_f[0:32, :], a_f, s_f[0:32, :], s_f, rope32, 32)
                    else:
                        def cons_p(ps, dst=dst, h=h):
                            acopy(dst[0:32, h % 4, :], dst, ps[0:32, :], ps)
                        proj_fm(wqb, 512 + 32 * h, qk, [qan], cons_p, M=32)
                for hh in range(2):
                    store(scr['q_mp'][:, 4 * hh:4 * hh + 4, t0:t0 + 512], sgq[hh], sgq[hh][0:32, :, :], ['q_mp'])
            if do_k:
                mla_kv_expand(ckvn_b, kc0)

            def orow(name, width):
                return lambda tt, j=None: (O[name][l, u.tok0 + t0 + tt * 128:u.tok0 + t0 + (tt + 1) * 128,
                                                   (0 if j is None else 128 * j):(width if j is None else 128 * j + 128)])

            def fm_cols(c_lo, c_sw, ntile, dst_scr, dst_key, col0, outrows=None, norm=None):
                wb = load_w(w_in, D, c_lo, 128 * ntile)
                sgb = stageb.next()
                for j in range(ntile):
                    a_f, cons_a = evac_f(128)
                    proj_fm(wb, 128 * j, hk, [hT], cons_a)
                    rr = None
                    if norm is not None:
                        s = tmpb.next()
                        P.op('act', lambda e, s=s, a_f=a_f: e.activation(out=s[:, :], in_=a_f[:, :], func=AF.Square),
                             reads=[a_f], writes=[s])
                        rr = rstd_from_sq([s], 64.0, blk64[:, :])
                        P.op('dve', lambda e, a_f=a_f, rr=rr: e.scalar_tensor_tensor(
                            out=a_f[:, :], in0=a_f[:, :], scalar=cv(l, norm), in1=rr[:, :], op0=ALU.mult,
                            op1=ALU.mult), reads=[a_f, colv, rr], writes=[a_f])
                    if u.sample:
                        s_f = perm_apply(a_f, 128, perm64)
                        rope_apply(sgb[:, j, :], sgb, a_f[:, :], a_f, s_f[:, :], s_f, rope64, 128)
                    else:
                        if outrows is not None:
                            out_tokmajor(lambda tt, j=j: outrows(tt, j), a_f, a_f[:, :], 128)
                        acopy(sgb[:, j, :], sgb, a_f[:, :], a_f)
                store(dst_scr[:, :, col0:col0 + 512], sgb, sgb[:, 0:ntile, :], [dst_key])

            def tm_cols(c_lo, ncols, dst_scr, dst_key, outrows=None):
                wb = load_w(w_in, D, c_lo, ncols)
                for tt in range(4):
                    ps = ring_g.next()
                    mm_group(ps, ps[:, 0:ncols],
                             [(hT[:, k, tt * 128:(tt + 1) * 128], wb[:, k, 0:ncols]) for k in range(8)],
                             reads=[wb, hT])
                    sgb = stageb.next()
                    r0 = kc0 + tt * 128
                    if outrows is not None:
                        sg = stage.next()
                        vcopy(sg[:, 0:ncols], sg, ps[:, 0:ncols], ps)
                        P.op('sp', lambda e, sg=sg, tt=tt: e.dma_start(out=outrows(tt), in_=sg[:, 0:ncols]),
                             reads=[sg], dma=True)
                        acopy(sgb[:, 0, 0:ncols], sgb, sg[:, 0:ncols], sg)
                    else:
                        acopy(sgb[:, 0, 0:ncols], sgb, ps[:, 0:ncols], ps)
                    P.op('sp', lambda e, sgb=sgb, r0=r0: e.dma_start(out=dst_scr[r0:r0 + 128, :],
                                                                     in_=sgb[:, 0, 0:ncols]),
                         reads=[sgb], writes=[dst_key], dma=True)

            pr = not u.sample
            if do_q:
                fm_cols(C_DQ, 32, 4, scr['q_d'], 'q_d', t0)
            if do_k:
                fm_cols(C_DK, 544, 4, scr['k_d'], 'k_d', kc0, outrows=(orow('n_dk', 512) if pr else None))
                tm_cols(C_DV, 512, scr['v_d'], 'v_d', outrows=(orow('n_dv', 512) if pr else None))
            if do_q:
                fm_cols(C_GQ, 1056, 4, scr['q_g'], 'q_g', t0, norm=11)
            if do_k:
                fm_cols(C_GK, 1568, 1, scr['k_g'], 'k_g', kc0, outrows=(orow('n_gk', 128) if pr else None), norm=13)
                tm_cols(C_GV, 128, scr['v_g'], 'v_g', outrows=(orow('n_gv', 128) if pr else None))
                wb = load_w(w_in, D, C_U, 512)
                sgb = stageb.next()
                for j in range(4):
                    def cons_u(ps, j=j, sgb=sgb):
                        acopy(sgb[:, j, :], sgb, ps[:, :], ps)
                    proj_fm(wb, 128 * j, hk, [hT], cons_u)
                store(u.uTs[:, :, t0:t0 + 512], sgb, sgb[:], [('uT', u.i)])

        RG[0] = Ring(psb[0:2])

    def attention(l, u):
        arena.top = 0
        S = u.S
        nkt_seq = SMAX // 128 if u.sample else 2
        Nq = 512 if u.sample else 256
        kT = arena.alloc([128, 4, S], BF16)
        kpe = arena.alloc([128, S], BF16)
        vv = arena.alloc([128, S // 128, 512], BF16)
        qT = arena.alloc([128, 4, 2, 512], BF16)
        qpe = arena.alloc([128, 8, 512], BF16)
        P.op('pool', lambda e: e.memset(qT[:], 0.0), writes=[qT])
        P.op('pool', lambda e: e.memset(qpe[:], 0.0), writes=[qpe])
        P.op('pool', lambda e: e.memset(kpe[:], 0.0), writes=[kpe])

        def load_q(qname, t0):
            for hh in range(2):
                P.op('sp', lambda e: e.dma_start(out=qT[64 * hh:64 * hh + 64, :, hh, :],
                                                 in_=scr[qname][64 * hh:64 * hh + 64, :, t0:t0 + 512]),
                     reads=[qname], writes=[qT], dma=True)
        ptring = Ring([arena.alloc([128, 512], BF16) for _ in range(8)])
        nkt_all = S // 128
        vaug = arena.alloc([128, nkt_all, 8, 128], BF16)

        def load_ctx(kname, vname, vw, nkt4):
            P.op('sp', lambda e: e.dma_start(out=kT[:, 0:nkt4, :], in_=scr[kname][:, :, 0:S]), reads=[kname],
                 writes=[kT], dma=True)
            P.op('sp', lambda e: e.dma_start(out=vv[:, :, 0:vw],
                                             in_=scr[vname][0:S, :].rearrange("(t p) c -> p t c", p=128)),
                 reads=[vname], writes=[vv], dma=True)

        ring_s3 = Ring(psb[1:5])
        ring_acc3 = Ring(psb[5:8])
        nq = 1 if (u.sample and l == DEPTH - 1) else u.nblk
        ring_g1 = Ring(psb[0:1])
        LA = 3

        def run_heads(heads):
            items = [(hd, i) for hd in heads for i in range(nkt_seq)]
            n = len(items)
            for idx in range(n + LA):
                if idx < n:
                    hd, i = items[idx]
                    q0 = hd['ss'] * Nq
                    kt = (hd['ss'] * 2 if not u.sample else 0) + i
                    sp_ = ring_s3.next()
                    hd.setdefault('sp', {})[i] = sp_
                    pairs = [(hd['k_fn'](kt), hd['q_fn'](q0))]
                    if hd['extra'] is not None:
                        pairs.append((hd['extra'][0](kt), hd['extra'][1](q0)))
                    mm_group(sp_, sp_[:, 0:Nq], pairs, reads=[kT, kpe, qT, qpe])
                j = idx - LA
                if j >= 0:
                    hd, i = items[j]
                    kt = (hd['ss'] * 2 if not u.sample else 0) + i
                    aug = hd.get('aug') is not None
                    if i == 0:
                        hd['ops'] = ring_acc3.next()
                        hd['sps'] = None if aug else ring_acc3.next()
                    ops_, sps_ = hd['ops'], hd['sps']
                    sp_ = hd['sp'].pop(i)
                    pt = ptring.next()
                    scale, vM, v_col0 = hd['scale'], hd['vM'], hd['v_col0']
                    P.op('act', lambda e: e.activation(out=pt[:, 0:Nq], in_=sp_[:, 0:Nq], func=AF.Exp, scale=scale),
                         reads=[sp_], writes=[pt])
                    if aug:
                        P.op('pe', lambda e: e.matmul(ops_[:, 0:Nq], lhsT=vaug[:, kt, hd['aug'], :],
                                                      rhs=pt[:, 0:Nq], start=(i == 0), stop=(i == nkt_seq - 1)),
                             reads=[vaug, pt], writes=[ops_])
                    else:
                        P.op('pe', lambda e: e.matmul(ops_[0:vM, 0:Nq], lhsT=vv[:, kt, v_col0:v_col0 + vM],
                                                      rhs=pt[:, 0:Nq], start=(i == 0), stop=(i == nkt_seq - 1)),
                             reads=[vv, pt], writes=[ops_], milestone=False)
                        P.op('pe', lambda e: e.matmul(sps_[:, 0:Nq], lhsT=ones[:, :], rhs=pt[:, 0:Nq],
                                                      start=(i == 0), stop=(i == nkt_seq - 1)),
                             reads=[ones, pt], writes=[sps_, ops_])
                    if i == nkt_seq - 1:
                        hd['fin'](hd, ops_, sps_)

        def fin_aug(og, r0, j, ss):
            def fin(hd, ops_, sps_):
                q0 = ss * Nq
                o0 = 64 - r0
                rc = tmpf.next()
                acopy(rc[r0:r0 + 64, 0:Nq], rc, ops_[o0:o0 + 64, 0:Nq], ops_)
                P.op('dve', lambda e: e.reciprocal(out=rc[r0:r0 + 64, 0:Nq], in_=rc[r0:r0 + 64, 0:Nq]),
                     reads=[rc], writes=[rc])
                P.op('dve', lambda e: e.tensor_tensor(out=og[r0:r0 + 64, j, q0:q0 + Nq], in0=ops_[r0:r0 + 64, 0:Nq],
                                                      in1=rc[r0:r0 + 64, 0:Nq], op=ALU.mult),
                     reads=[ops_, rc], writes=[og])
            return fin

        load_ctx('k_mn', 'v_m', 512, 4)
        P.op('sp', lambda e: e.dma_start(out=kpe[0:32, :], in_=scr['k_mp'][:, 0:S]), reads=['k_mp'], writes=[kpe], dma=True)
        vv4 = vv[:, :, :].rearrange("p t (h d) -> p t h d", h=8)
        P.op('pool', lambda e: e.memset(vaug[:], 1.0), writes=[vaug])
        P.op('dve', lambda e: e.tensor_copy(out=vaug[:, :, 0:8:2, 0:64], in_=vv4[:, :, 0:8:2, :]), reads=[vv], writes=[vaug])
        P.op('pool', lambda e: e.tensor_copy(out=vaug[:, :, 1:8:2, 64:128], in_=vv4[:, :, 1:8:2, :]), reads=[vv],
             writes=[vaug])
        for b in range(nq):
            t0 = b * 512
            load_q('q_mn', t0)
            P.op('sp', lambda e: e.dma_start(out=qpe[0:32, :, :], in_=scr['q_mp'][:, :, t0:t0 + 512]),
                 reads=['q_mp'], writes=[qpe], dma=True)
            og = stageb.next()
            heads = []
            for h in range(8):
                r0 = 64 * (h % 2)
                j = h // 2
                for ss in range(u.nss):
                    fin = fin_aug(og, r0, j, ss)
                    heads.append(dict(
                        aug=h,
                        ss=ss, k_fn=(lambda kt, j=j: kT[:, j, kt * 128:(kt + 1) * 128]),
                        q_fn=(lambda q0_, h=h, j=j: qT[:, j, h % 2, q0_:q0_ + Nq]),
                        extra=(lambda kt: kpe[:, kt * 128:(kt + 1) * 128],
                               lambda q0_, h=h: qpe[:, h, q0_:q0_ + Nq]),
                        v_col0=128 * j, vM=128, scale=MLA_SCALE, fin=fin))
            run_heads(heads)
            P.op('sp', lambda e: e.dma_start(out=u.oTs[:, 0:4, t0:t0 + 512], in_=og[:]),
                 reads=[og], writes=[('oT', u.i)], dma=True)
        load_ctx('k_d', 'v_d', 512, 4)
        lam_init = 0.8 - 0.6 * math.exp(-0.3 * l)
        for b in range(nq):
            t0 = b * 512
            load_q('q_d', t0)
            og = stageb.next()
            heads = []
            for h in range(4):
                for ss in range(u.nss):
                    pair = {}
                    for m in range(2):
                        r0 = 64 * m

                        def fin(hd, ops_, sps_, og=og, h=h, ss=ss, m=m, pair=pair):
                            q0 = ss * Nq
                            rc, dd = tmpf.next(), tmpf.next()
                            P.op('dve', lambda e: e.reciprocal(out=rc[:, 0:Nq], in_=sps_[:, 0:Nq]),
                                 reads=[sps_], writes=[rc])
                            P.op('dve', lambda e: e.tensor_tensor(out=dd[:, 0:Nq], in0=ops_[:, 0:Nq], in1=rc[:, 0:Nq],
                                                                  op=ALU.mult), reads=[ops_, rc], writes=[dd])
                            if m == 0:
                                pair['d1'] = dd
                                return
                            d1, d2 = pair['d1'], dd
                            P.op('dve', lambda e: e.scalar_tensor_tensor(out=d1[:, 0:Nq], in0=d2[:, 0:Nq],
                                                                         scalar=lamneg[:, l:l + 1], in1=d1[:, 0:Nq],
                                                                         op0=ALU.mult, op1=ALU.add),
                                 reads=[d1, d2, lamneg], writes=[d1])
                            sq = tmpb.next()
                            P.op('pool', lambda e: e.tensor_tensor(out=sq[:, 0:Nq], in0=d1[:, 0:Nq], in1=d1[:, 0:Nq],
                                                                  op=ALU.mult), reads=[d1], writes=[sq])
                            ps = ring_g1.next()
                            mm_group(ps, ps[:, 0:Nq], [(ones[:, :], sq[:, 0:Nq])], reads=[ones, sq])
                            rr = rstd_of(ps, ps[:, 0:Nq], Nq, 128.0, mult=(1.0 - lam_init))
                            P.op('dve', lambda e: e.scalar_tensor_tensor(
                                out=og[:, h, q0:q0 + Nq], in0=d1[:, 0:Nq], scalar=cv(l, 15), in1=rr[:, 0:Nq],
                                op0=ALU.mult, op1=ALU.mult), reads=[d1, rr, colv], writes=[og])
                        heads.append(dict(
                            ss=ss, k_fn=(lambda kt, h=h: kT[:, h, kt * 128:(kt + 1) * 128]),
                            q_fn=(lambda q0_, m=m, h=h: qT[:, h, m, q0_:q0_ + Nq]),
                            extra=None, v_col0=128 * h, vM=128, scale=SCALE64, fin=fin))
            run_heads(heads)
            P.op('sp', lambda e: e.dma_start(out=u.oTs[:, 4:8, t0:t0 + 512], in_=og[:]),
                 reads=[og], writes=[('oT', u.i)], dma=True)
        load_ctx('k_g', 'v_g', 128, 1)
        P.op('sp', lambda e: e.dma_start(out=kT[0:64, 1, :], in_=scr['k_g'][64:128, 0, 0:S]), reads=['k_g'],
             writes=[kT], dma=True)
        P.op('sp', lambda e: e.dma_start(out=kT[64:128, 1, :], in_=scr['k_g'][0:64, 0, 0:S]), reads=['k_g'],
             writes=[kT], dma=True)
        P.op('pool', lambda e: e.memset(vaug[:, :, 0:4, :], 1.0), writes=[vaug])
        for kvh in range(2):
            for half in range(2):
                P.op('dve' if half == 0 else 'pool', lambda e: e.tensor_copy(
                    out=vaug[:, :, 2 * kvh + half, 64 * half:64 * half + 64], in_=vv[:, :, 64 * kvh:64 * kvh + 64]),
                    reads=[vv], writes=[vaug])
        for b in range(nq):
            t0 = b * 512
            load_q('q_g', t0)
            og = stageb.next()
            heads = []
            for h in range(8):
                r0 = 64 * (h % 2)
                j = h // 2
                k0 = 64 * (h // 4)
                for ss in range(u.nss):
                    fin = fin_aug(og, r0, j, ss)
                    heads.append(dict(
                        aug=2 * (h // 4) + (h % 2),
                        ss=ss,
                        k_fn=(lambda kt, k0=k0, r0=r0: kT[:, (0 if k0 == r0 else 1), kt * 128:(kt + 1) * 128]),
                        q_fn=(lambda q0_, h=h, j=j: qT[:, j, h % 2, q0_:q0_ + Nq]),
                        extra=None, v_col0=0, vM=128, scale=SCALE64, fin=fin))
            run_heads(heads)
            P.op('sp', lambda e: e.dma_start(out=u.oTs[:, 8:12, t0:t0 + 512], in_=og[:]),
                 reads=[og], writes=[('oT', u.i)], dma=True)

    def ssm(l):
        arena.top = 0
        BT = arena.alloc([128, 2, 16, 2, 128], BF16)
        CT = arena.alloc([128, 2, 16, 2, 128], BF16)
        NB0 = 32
        Ebr = arena.alloc([128, 2, 16, NB0], F32)
        Ebi = arena.alloc([128, 2, 16, NB0], F32)
        mark = arena.top
        A = arena.alloc([128, 3, 2, 16], F32)
        for d in range(2):
            P.op('sp', lambda e, d=d: e.dma_start(out=A[:, :, d, :], in_=I['ssm_a'][l, d].rearrange("a p s -> p a s")),
                 writes=[A], dma=True)
        W = arena.alloc([128, 16, 2, 16], F32)

        def wcol(i):
            return W[:, i, :, :]
        are, aim, ldt = A[:, 0, :, :], A[:, 1, :, :], A[:, 2, :, :]

        def dv(fn, rd=(), wrt=()):
            P.op('dve', fn, reads=[W, A, prm] + list(rd), writes=list(wrt))
        P.op('act', lambda e: e.activation(out=wcol(0), in_=ldt, func=AF.Exp), reads=[A], writes=[W])
        dv(lambda e: e.tensor_tensor(out=wcol(1), in0=are, in1=wcol(0), op=ALU.mult), wrt=[W])
        dv(lambda e: e.tensor_tensor(out=wcol(2), in0=aim, in1=wcol(0), op=ALU.mult), wrt=[W])
        P.op('act', lambda e: e.activation(out=prm[:, :, :, 0], in_=wcol(1), func=AF.Exp), reads=[W], writes=[prm])
        ki = arena.alloc([128, 2, 16], I32)

        def sin_of(dst_ap, dst_buf, src_ap, shift):
            dv(lambda e: e.tensor_scalar(out=wcol(3), in0=src_ap, scalar1=shift, scalar2=None, op0=ALU.add), wrt=[W])
            P.op('dve', lambda e: e.tensor_scalar(out=ki[:, :, :], in0=wcol(3), scalar1=1.0 / (2 * math.pi),
                                                  scalar2=None, op0=ALU.mult), reads=[W], writes=[ki])
            P.op('dve', lambda e: e.tensor_copy(out=wcol(4), in_=ki[:, :, :]), reads=[ki], writes=[W])
            dv(lambda e: e.scalar_tensor_tensor(out=wcol(5), in0=wcol(4), scalar=-2 * math.pi, in1=wcol(3),
                                                op0=ALU.mult, op1=ALU.add), wrt=[W])
            P.op('act', lambda e: e.activation(out=dst_ap, in_=wcol(5), func=AF.Sin), reads=[W],
                 writes=[dst_buf])
        sin_of(wcol(6), W, wcol(2), 0.0)
        sin_of(wcol(7), W, wcol(2), math.pi / 2)
        dv(lambda e: e.tensor_tensor(out=wcol(8), in0=prm[:, :, :, 0], in1=wcol(7), op=ALU.mult), wrt=[W])
        dv(lambda e: e.tensor_tensor(out=wcol(9), in0=prm[:, :, :, 0], in1=wcol(6), op=ALU.mult), wrt=[W])
        dv(lambda e: e.tensor_scalar(out=wcol(10), in0=wcol(8), scalar1=-1.0, scalar2=None, op0=ALU.add), wrt=[W])
        dv(lambda e: e.tensor_tensor(out=wcol(11), in0=are, in1=are, op=ALU.mult), wrt=[W])
        dv(lambda e: e.tensor_tensor(out=wcol(12), in0=aim, in1=aim, op=ALU.mult), wrt=[W])
        dv(lambda e: e.tensor_tensor(out=wcol(11), in0=wcol(11), in1=wcol(12), op=ALU.add), wrt=[W])
        dv(lambda e: e.reciprocal(out=wcol(11), in_=wcol(11)), wrt=[W])
        dv(lambda e: e.tensor_tensor(out=wcol(12), in0=wcol(10), in1=are, op=ALU.mult), wrt=[W])
        dv(lambda e: e.tensor_tensor(out=wcol(13), in0=wcol(9), in1=aim, op=ALU.mult), wrt=[W])
        dv(lambda e: e.tensor_tensor(out=wcol(12), in0=wcol(12), in1=wcol(13), op=ALU.add), wrt=[W])
        dv(lambda e: e.tensor_tensor(out=prm[:, :, :, 1], in0=wcol(12), in1=wcol(11), op=ALU.mult), wrt=[prm])
        dv(lambda e: e.tensor_tensor(out=wcol(12), in0=wcol(9), in1=are, op=ALU.mult), wrt=[W])
        dv(lambda e: e.tensor_tensor(out=wcol(13), in0=wcol(10), in1=aim, op=ALU.mult), wrt=[W])
        dv(lambda e: e.tensor_tensor(out=wcol(12), in0=wcol(12), in1=wcol(13), op=ALU.subtract), wrt=[W])
        dv(lambda e: e.tensor_tensor(out=prm[:, :, :, 2], in0=wcol(12), in1=wcol(11), op=ALU.mult), wrt=[prm])
        dv(lambda e: e.tensor_scalar(out=prm[:, :, :, 3], in0=prm[:, :, :, 2], scalar1=-1.0, scalar2=None,
                                     op0=ALU.mult), wrt=[prm])
        s0 = arena.alloc([128, 2, 16, 2], F32)
        for d in range(2):
            P.op('sp', lambda e, d=d: e.dma_start(out=s0[:, d, :, :], in_=I['st_ssm'][l, d].rearrange("s p r -> p s r")),
                 writes=[s0], dma=True)
        dv(lambda e: e.tensor_tensor(out=wcol(12), in0=wcol(7), in1=s0[:, :, :, 0], op=ALU.mult), rd=[s0], wrt=[W])
        dv(lambda e: e.tensor_tensor(out=wcol(13), in0=wcol(6), in1=s0[:, :, :, 1], op=ALU.mult), rd=[s0], wrt=[W])
        dv(lambda e: e.tensor_tensor(out=prm[:, :, :, 4], in0=wcol(12), in1=wcol(13), op=ALU.subtract), wrt=[prm])
        dv(lambda e: e.tensor_tensor(out=wcol(12), in0=wcol(7), in1=s0[:, :, :, 1], op=ALU.mult), rd=[s0], wrt=[W])
        dv(lambda e: e.tensor_tensor(out=wcol(13), in0=wcol(6), in1=s0[:, :, :, 0], op=ALU.mult), rd=[s0], wrt=[W])
        dv(lambda e: e.tensor_tensor(out=prm[:, :, :, 5], in0=wcol(12), in1=wcol(13), op=ALU.add), wrt=[prm])
        P.op('dve', lambda e: e.tensor_copy(out=pwc[:, :, :, 0], in_=wcol(7)), reads=[W], writes=[pwc])
        P.op('dve', lambda e: e.tensor_scalar(out=pws[:, :, :, 0], in0=wcol(6), scalar1=-1.0, scalar2=None,
                                              op0=ALU.mult), reads=[W], writes=[pws])
        for k in range(1, NPOW):
            dv(lambda e, k=k: e.tensor_tensor(out=wcol(12), in0=pwc[:, :, :, k - 1], in1=pwc[:, :, :, k - 1],
                                              op=ALU.mult), rd=[pwc], wrt=[W])
            dv(lambda e, k=k: e.tensor_tensor(out=wcol(13), in0=pws[:, :, :, k - 1], in1=pws[:, :, :, k - 1],
                                              op=ALU.mult), rd=[pws], wrt=[W])
            dv(lambda e, k=k: e.tensor_tensor(out=wcol(14), in0=pwc[:, :, :, k - 1], in1=pws[:, :, :, k - 1],
                                              op=ALU.mult), rd=[pwc, pws], wrt=[W])
            dv(lambda e, k=k: e.tensor_tensor(out=pwc[:, :, :, k], in0=wcol(12), in1=wcol(13), op=ALU.subtract),
               wrt=[pwc])
            dv(lambda e, k=k: e.tensor_scalar(out=pws[:, :, :, k], in0=wcol(14), scalar1=2.0, scalar2=None,
                                              op0=ALU.mult), wrt=[pws])
        M = [arena.alloc([128, 2, 16, NB0 // 2], F32) for _ in range(4)]
        P.op('pool', lambda e: e.memset(Ebr[:, :, :, 0:1], 1.0), writes=[Ebr])
        P.op('pool', lambda e: e.memset(Ebi[:, :, :, 0:1], 0.0), writes=[Ebi])
        kk = 0
        while (1 << kk) < NB0:
            n = 1 << kk
            cb = pwc[:, :, :, kk:kk + 1].to_broadcast([128, 2, 16, n])
            sbb = pws[:, :, :, kk:kk + 1].to_broadcast([128, 2, 16, n])
            er0, ei0 = Ebr[:, :, :, 0:n], Ebi[:, :, :, 0:n]
            m = [M[i][:, :, :, 0:n] for i in range(4)]
            P.op('dve', lambda e: e.tensor_tensor(out=m[0], in0=er0, in1=cb, op=ALU.mult), reads=[Ebr, pwc], writes=[M[0]])
            P.op('pool', lambda e: e.tensor_tensor(out=m[1], in0=ei0, in1=sbb, op=ALU.mult), reads=[Ebi, pws], writes=[M[1]])
            P.op('dve', lambda e: e.tensor_tensor(out=m[2], in0=ei0, in1=cb, op=ALU.mult), reads=[Ebi, pwc], writes=[M[2]])
            P.op('pool', lambda e: e.tensor_tensor(out=m[3], in0=er0, in1=sbb, op=ALU.mult), reads=[Ebr, pws], writes=[M[3]])
            P.op('dve', lambda e: e.tensor_tensor(out=Ebr[:, :, :, n:2 * n], in0=m[0], in1=m[1], op=ALU.subtract),
                 reads=[M[0], M[1]], writes=[Ebr])
            P.op('dve', lambda e: e.tensor_tensor(out=Ebi[:, :, :, n:2 * n], in0=m[2], in1=m[3], op=ALU.add),
                 reads=[M[2], M[3]], writes=[Ebi])
            kk += 1
        KK0 = kk
        Yb = arena.alloc([128, 2, 16, 128], F32)
        Yw = arena.alloc([128, 2, 128], F32)
        X = arena.alloc([128, 2, 4, 512], F32)
        for d in range(2):
            P.op('pool', lambda e: e.memset(Yb[:], 0.0), writes=[Yb])
            P.op('pool', lambda e: e.memset(X[:], 0.0), writes=[X])
            for ri in range(2):
                bsrc = I['ssm_b'][l, d, ri].rearrange("(ct k gl) p n -> k gl p ct n", ct=4, k=4, gl=2)
                csrc = I['ssm_c'][l, d, ri].rearrange("(ct k gl) n p -> k gl n ct p", ct=4, k=4, gl=2)
                for k in range(4):
                    for gl in range(2):
                        c0 = 32 * k + 16 * gl
                        P.op('sp', lambda e, ri=ri, k=k, gl=gl, c0=c0, bsrc=bsrc: e.dma_start(
                            out=Yb[64 * gl:64 * gl + 64, ri, k:16:4, c0:c0 + 16], in_=bsrc[k, gl]),
                            writes=[Yb], dma=True)
                        x0 = 64 * (2 * k + gl)
                        P.op('sp', lambda e, ri=ri, c0=c0, x0=x0, csrc=csrc, k=k, gl=gl: e.dma_start(
                            out=X[c0:c0 + 16, ri, :, x0:x0 + 64], in_=csrc[k, gl]), writes=[X], dma=True)
            for s_ in range(16):
                cr, ci, nci = prm[:, d, s_, 1:2], prm[:, d, s_, 2:3], prm[:, d, s_, 3:4]
                P.op('dve', lambda e, s_=s_, ci=ci: e.tensor_scalar(out=Yw[:, 0, :], in0=Yb[:, 1, s_, :], scalar1=ci,
                                                                    scalar2=None, op0=ALU.mult),
                     reads=[Yb, prm], writes=[Yw])
                P.op('dve', lambda e, s_=s_, cr=cr: e.scalar_tensor_tensor(out=Yw[:, 0, :], in0=Yb[:, 0, s_, :],
                                                                           scalar=cr, in1=Yw[:, 0, :], op0=ALU.mult,
                                                                           op1=ALU.subtract),
                     reads=[Yb, prm, Yw], writes=[Yw])
                P.op('dve', lambda e, s_=s_, ci=ci: e.tensor_scalar(out=Yw[:, 1, :], in0=Yb[:, 0, s_, :], scalar1=ci,
                                                                    scalar2=None, op0=ALU.mult),
                     reads=[Yb, prm], writes=[Yw])
                P.op('dve', lambda e, s_=s_, cr=cr: e.scalar_tensor_tensor(out=Yw[:, 1, :], in0=Yb[:, 1, s_, :],
                                                                           scalar=cr, in1=Yw[:, 1, :], op0=ALU.mult,
                                                                           op1=ALU.add),
                     reads=[Yb, prm, Yw], writes=[Yw])
                for ri in range(2):
                    transpose_to(BT[:, d, s_, ri, :], BT, Yw[:, ri, :], Yw, 128, 128)
                    ct, k = s_ // 4, s_ % 4
                    transpose_to(CT[:, d, s_, ri, :], CT, X[:, ri, ct, 128 * k:128 * k + 128], X, 128, 128,
                                 scale=(1.0 if ri == 0 else -1.0))
        arena.top = mark
        Eres = [arena.alloc([128, TS], F32) for _ in range(2)]
        Eims = [arena.alloc([128, TS], F32) for _ in range(2)]
        bre = arena.alloc([128, TS], F32)
        bim = arena.alloc([128, TS], F32)
        uct = [arena.alloc([128, u.T], BF16) for u in units]
        yg = [arena.alloc([128, 4, u.T], BF16) for u in units]
        stS = arena.alloc([128, 2, 128], F32)
        ring2 = Ring(psb[0:2])
        ybank = {(0, 0): psb[6], (1, 0): psb[7], (2, 0): psb[2], (2, 1): psb[3], (2, 2): psb[4], (2, 3): psb[5]}

        def pk(buf, lo, hi):
            return [buf.key[i] for i in range(lo // PAGE, (hi - 1) // PAGE + 1)]
        it = 0
        for ct in range(4):
            for ui, u in enumerate(units):
                P.op('sp', lambda e: e.dma_start(out=uct[ui][:, :], in_=u.uTs[:, ct, :]),
                     reads=[('uT', u.i)], writes=[uct[ui]], dma=True)
            for d in range(2):
                for k in range(4):
                    s_ = 4 * ct + k
                    first_dk = (d == 0 and k == 0)
                    last_dk = (d == 1 and k == 3)
                    rho = prm[:, d, s_, 0:1]
                    Ere, Eim = Eres[it % 2], Eims[it % 2]
                    it += 1
                    acopy(Ere[:, 0:NB0], pk(Ere, 0, NB0), Ebr[:, d, s_, :], Ebr)
                    acopy(Eim[:, 0:NB0], pk(Eim, 0, NB0), Ebi[:, d, s_, :], Ebi)
                    for kk in range(KK0, NPOW):
                        n = 1 << kk
                        c_, sn = pwc[:, d, s_, kk:kk + 1], pws[:, d, s_, kk:kk + 1]
                        for q0 in range(0, n, 512):
                            q1 = min(n, q0 + 512)
                            w = q1 - q0
                            t1, t2 = tmpf.next(), tmpf.next()
                            P.op('act', lambda e: e.activation(out=t1[:, 0:w], in_=Eim[:, q0:q1], func=AF.Identity, scale=sn),
                                 reads=pk(Eim, q0, q1) + [pws], writes=[t1])
                            P.op('act', lambda e: e.activation(out=t2[:, 0:w], in_=Ere[:, q0:q1], func=AF.Identity, scale=sn),
                                 reads=pk(Ere, q0, q1) + [pws], writes=[t2])
                            P.op('dve', lambda e: e.scalar_tensor_tensor(
                                out=Ere[:, n + q0:n + q1], in0=Ere[:, q0:q1], scalar=c_, in1=t1[:, 0:w], op0=ALU.mult,
                                op1=ALU.subtract), reads=pk(Ere, q0, q1) + [pwc, t1], writes=pk(Ere, n + q0, n + q1))
                            P.op('dve', lambda e: e.scalar_tensor_tensor(
                                out=Eim[:, n + q0:n + q1], in0=Eim[:, q0:q1], scalar=c_, in1=t2[:, 0:w], op0=ALU.mult,
                                op1=ALU.add), reads=pk(Eim, q0, q1) + [pwc, t2], writes=pk(Eim, n + q0, n + q1))
                    for ui, u in enumerate(units):
                        Ts = u.Ts

                        def Ev(E, a, b_):
                            if d == 0:
                                return E[:, a:b_], pk(E, a, b_)
                            return E[:, 0:Ts][:, ::-1][:, a:b_], pk(E, Ts - b_, Ts - a)
                        for b in range(u.nblk):
                            psr, psi = ring2.next(), ring2.next()
                            mm_group(psr, psr[:, :], [(BT[:, d, s_, 0, :], uct[ui][:, b * 512:(b + 1) * 512])],
                                     reads=[BT, uct[ui]])
                            mm_group(psi, psi[:, :], [(BT[:, d, s_, 1, :], uct[ui][:, b * 512:(b + 1) * 512])],
                                     reads=[BT, uct[ui]])
                            ar, ai = tmpf.next(), tmpf.next()
                            acopy(ar[:, :], ar, psr[:, :], psr)
                            acopy(ai[:, :], ai, psi[:, :], psi)
                            for ss in range(u.nss):
                                w = 512 // u.nss
                                c0 = b * 512 + ss * w
                                a = c0 % Ts
                                (er, erk), (ei, eik) = Ev(Ere, a, a + w), Ev(Eim, a, a + w)
                                bk_r, bk_i = pk(bre, c0, c0 + w), pk(bim, c0, c0 + w)
                                p0 = ss * w
                                t1, t2, t3, t4 = tmpf.next(), tmpf.next(), tmpf.next(), tmpf.next()
                                P.op('dve', lambda e: e.tensor_tensor(out=t1[:, 0:w], in0=ar[:, p0:p0 + w], in1=er,
                                                                      op=ALU.mult), reads=[ar] + erk, writes=[t1])
                                P.op('dve', lambda e: e.tensor_tensor(out=t2[:, 0:w], in0=ai[:, p0:p0 + w], in1=ei,
                                                                      op=ALU.mult), reads=[ai] + eik, writes=[t2])
                                P.op('dve', lambda e: e.tensor_tensor(out=bre[:, c0:c0 + w], in0=t1[:, 0:w],
                                                                      in1=t2[:, 0:w], op=ALU.subtract),
                                     reads=[t1, t2], writes=bk_r)
                                P.op('pool', lambda e: e.tensor_tensor(out=t3[:, 0:w], in0=ar[:, p0:p0 + w], in1=ei,
                                                                       op=ALU.mult), reads=[ar] + eik, writes=[t3])
                                P.op('pool', lambda e: e.tensor_tensor(out=t4[:, 0:w], in0=ai[:, p0:p0 + w], in1=er,
                                                                       op=ALU.mult), reads=[ai] + erk, writes=[t4])
                                P.op('dve', lambda e: e.tensor_tensor(out=bim[:, c0:c0 + w], in0=t3[:, 0:w],
                                                                      in1=t4[:, 0:w], op=ALU.add),
                                     reads=[t3, t4], writes=bk_i)
                        for sq_ in range(u.T // Ts):
                            c0 = sq_ * Ts
                            for bb, col in ((bre, 4), (bim, 5)):
                                v = bb[:, c0:c0 + Ts]
                                if d == 1:
                                    v = v[:, ::-1]
                                init = prm[:, d, s_, col:col + 1] if u.sample else 0.0
                                P.op('dve', lambda e: e.tensor_tensor_scan(
                                    out=v, data0=rho.to_broadcast([128, Ts]), data1=v, initial=init, op0=ALU.mult,
                                    op1=ALU.add), reads=pk(bb, c0, c0 + Ts) + [prm], writes=pk(bb, c0, c0 + Ts))
                            if not u.sample:
                                gcol = (c0 + Ts - 1) if d == 0 else c0
                                seq = 2 * ui + sq_
                                oc = (seq * 2 + d) * 16 + s_
                                eR, eI = Ere[:, Ts - 1:Ts], Eim[:, Ts - 1:Ts]
                                ekr, eki = pk(Ere, Ts - 1, Ts), pk(Eim, Ts - 1, Ts)
                                gr, gi = bre[:, gcol:gcol + 1], bim[:, gcol:gcol + 1]
                                gkr, gki = pk(bre, gcol, gcol + 1), pk(bim, gcol, gcol + 1)
                                t1 = tmpf.next()
                                P.op('dve', lambda e: e.tensor_tensor(out=t1[:, 0:1], in0=gi, in1=eI, op=ALU.mult),
                                     reads=gki + eki, writes=[t1])
                                P.op('dve', lambda e: e.scalar_tensor_tensor(
                                    out=stS[:, 0, oc:oc + 1], in0=gr, scalar=eR, in1=t1[:, 0:1], op0=ALU.mult,
                                    op1=ALU.add), reads=gkr + ekr + [t1], writes=[stS])
                                t2 = tmpf.next()
                                P.op('dve', lambda e: e.tensor_tensor(out=t2[:, 0:1], in0=gr, in1=eI, op=ALU.mult),
                                     reads=gkr + eki, writes=[t2])
                                P.op('dve', lambda e: e.scalar_tensor_tensor(
                                    out=stS[:, 1, oc:oc + 1], in0=gi, scalar=eR, in1=t2[:, 0:1], op0=ALU.mult,
                                    op1=ALU.subtract), reads=gki + ekr + [t2], writes=[stS])
                        for b in range(u.nblk):
                            hr, hi = tmpb.next(), tmpb.next()
                            for ss in range(u.nss):
                                w = 512 // u.nss
                                c0 = b * 512 + ss * w
                                a = c0 % Ts
                                (er, erk), (ei, eik) = Ev(Ere, a, a + w), Ev(Eim, a, a + w)
                                bk_r, bk_i = pk(bre, c0, c0 + w), pk(bim, c0, c0 + w)
                                p0 = ss * w
                                t1, t2, t3, t4 = tmpf.next(), tmpf.next(), tmpf.next(), tmpf.next()
                                P.op('pool', lambda e: e.tensor_tensor(out=t1[:, 0:w], in0=bre[:, c0:c0 + w], in1=er,
                                                                       op=ALU.mult), reads=bk_r + erk, writes=[t1])
                                P.op('dve', lambda e: e.tensor_tensor(out=t2[:, 0:w], in0=bim[:, c0:c0 + w], in1=ei,
                                                                      op=ALU.mult), reads=bk_i + eik, writes=[t2])
                                P.op('dve', lambda e: e.tensor_tensor(out=hr[:, p0:p0 + w], in0=t1[:, 0:w],
                                                                      in1=t2[:, 0:w], op=ALU.add),
                                     reads=[t1, t2], writes=[hr])
                                P.op('pool', lambda e: e.tensor_tensor(out=t3[:, 0:w], in0=bim[:, c0:c0 + w], in1=er,
                                                                       op=ALU.mult), reads=bk_i + erk, writes=[t3])
                                P.op('dve', lambda e: e.tensor_tensor(out=t4[:, 0:w], in0=bre[:, c0:c0 + w], in1=ei,
                                                                      op=ALU.mult), reads=bk_r + eik, writes=[t4])
                                P.op('dve', lambda e: e.tensor_tensor(out=hi[:, p0:p0 + w], in0=t3[:, 0:w],
                                                                      in1=t4[:, 0:w], op=ALU.subtract),
                                     reads=[t3, t4], writes=[hi])
                            psy = ybank[(ui, b)]
                            P.op('pe', lambda e: e.matmul(psy[:, :], lhsT=CT[:, d, s_, 0, :], rhs=hr[:, :],
                                                          start=first_dk, stop=False),
                                 reads=[CT, hr], writes=[psy], milestone=False)
                            P.op('pe', lambda e: e.matmul(psy[:, :], lhsT=CT[:, d, s_, 1, :], rhs=hi[:, :],
                                                          start=False, stop=last_dk),
                                 reads=[CT, hi], writes=[psy])
            for ui, u in enumerate(units):
                for b in range(u.nblk):
                    sl = slice(b * 512, (b + 1) * 512)
                    psy = ybank[(ui, b)]
                    t = tmpf.next()
                    P.op('dve', lambda e: e.scalar_tensor_tensor(
                        out=t[:, :], in0=uct[ui][:, sl], scalar=cv(l, 16 + ct), in1=psy[:, :], op0=ALU.mult,
                        op1=ALU.add), reads=[uct[ui], colv, psy], writes=[t])
                    t2 = tmpf.next()
                    P.op('pool', lambda e: e.tensor_tensor(out=t2[:, :], in0=t[:, :], in1=t[:, :], op=ALU.mult),
                         reads=[t], writes=[t2])
                    P.op('dve', lambda e: e.tensor_scalar(out=t2[:, :], in0=t2[:, :], scalar1=0.044715, scalar2=1.0,
                                                          op0=ALU.mult, op1=ALU.add), reads=[t2], writes=[t2])
                    P.op('dve', lambda e: e.tensor_tensor(out=t2[:, :], in0=t2[:, :], in1=t[:, :], op=ALU.mult),
                         reads=[t, t2], writes=[t2])
                    P.op('act', lambda e: e.activation(out=t2[:, :], in_=t2[:, :], func=AF.Sigmoid,
                                                       scale=1.5957691216057308), reads=[t2], writes=[t2])
                    P.op('dve', lambda e: e.tensor_tensor(out=yg[ui][:, ct, sl], in0=t[:, :], in1=t2[:, :], op=ALU.mult),
                         reads=[t, t2], writes=[yg[ui]])
        wglu = arena.alloc([128, 4, 512], BF16)
        P.op('pool', lambda e: e.dma_start(out=wglu[:], in_=I['w_glu'][l].rearrange("(k p) c -> p k c", p=128)),
             writes=[wglu], dma=True)
        for ui, u in enumerate(units):
            for b in range(u.nblk):
                sl = slice(b * 512, (b + 1) * 512)
                og = stageb.next()
                for jj in range(4):
                    ps = ring_all.next()
                    mm_group(ps, ps[:, :], [(wglu[:, k, 128 * jj:128 * jj + 128], yg[ui][:, k, sl]) for k in range(4)],
                             reads=[wglu, yg[ui]])
                    sg = tmpf.next()
                    P.op('act', lambda e, sg=sg, ps=ps, jj=jj: e.activation(out=sg[:, :], in_=ps[:, :], func=AF.Sigmoid,
                                                                            bias=cv(l, 20 + jj), scale=1.0),
                         reads=[ps, colv], writes=[sg])
                    P.op('dve', lambda e, sg=sg, og=og, jj=jj, ui=ui, sl=sl: e.tensor_tensor(
                        out=og[:, jj, :], in0=yg[ui][:, jj, sl], in1=sg[:, :], op=ALU.mult),
                        reads=[yg[ui], sg], writes=[og])
                P.op('sp', lambda e, og=og, u=u, b=b: e.dma_start(out=u.oTs[:, 12:16, b * 512:(b + 1) * 512], in_=og[:]),
                     reads=[og], writes=[('oT', u.i)], dma=True)
        zz = arena.alloc([128, 128, 2], F32)
        for ri in range(2):
            transpose_to(zz[:, :, ri], zz, stS[:, ri, :], stS, 128, 128)
        P.op('sp', lambda e: e.dma_start(out=O['n_ssm'][l], in_=zz[:].rearrange("p a b -> p (a b)")),
             reads=[zz], dma=True)

    def stage_C_unit(l, u):
        arena.top = 0
        pth = 1 if u.sample else 0
        w_in = I['w_in'][l]
        wring = Ring([arena.alloc([128, 8, 512], BF16) for _ in range(3)])
        xT = arena.alloc([128, 8, 512], F32)
        hT = arena.alloc([128, 8, 512], BF16)
        oT = arena.alloc([128, 16, 512], BF16)
        acc = arena.alloc([128, 8, 512], F32)
        mg = arena.alloc([128, 8, 512], BF16)
        ytok = Ring([arena.alloc([128, 1024], F32) for _ in range(2)])
        ring_g = Ring(psb[0:4])
        ring_acc = Ring(psb[4:8])
        RG[0] = ring_g

        def load_w(src2d, rows, c0, ncols):
            wb = wring.next()
            kk = rows // 128
            src = src2d[:, c0:c0 + ncols].rearrange("(k p) c -> p k c", p=128)
            P.op('pool', lambda e: e.dma_start(out=wb[:, 0:kk, 0:ncols], in_=src), writes=[wb], dma=True)
            return wb

        own_mode = (u.sample and l == DEPTH - 1)
        for b in range(1 if own_mode else u.nblk):
            t0 = b * 512
            if own_mode:
                P.op('sp', lambda e: e.dma_start(out=hT[:], in_=u.hTo[:]), reads=[('hTo', u.i)], writes=[hT], dma=True)
                P.op('sp', lambda e: e.dma_start(out=xT[:], in_=u.xo[:]), reads=[('xo', u.i)], writes=[xT], dma=True)
                P.op('sp', lambda e: e.dma_start(out=oT[:, 0:12, :], in_=u.oTs[:, 0:12, 0:512]), reads=[('oT', u.i)],
                     writes=[oT], dma=True)
                for s_ in range(4):
                    od = stageb.next()
                    P.op('sp', lambda e: e.dma_start(out=od[:], in_=u.oTs[:, 12:16, 512 * s_:512 * s_ + 512]),
                         reads=[('oT', u.i)], writes=[od], dma=True)
                    for jj in range(4):
                        if s_ == 0:
                            P.op('dve', lambda e: e.tensor_scalar(out=oT[:, 12 + jj, :], in0=od[:, jj, :],
                                                                  scalar1=mask4[:, 0:1], scalar2=None, op0=ALU.mult),
                                 reads=[od, mask4], writes=[oT])
                        else:
                            P.op('dve', lambda e: e.scalar_tensor_tensor(out=oT[:, 12 + jj, :], in0=od[:, jj, :],
                                                                         scalar=mask4[:, s_:s_ + 1], in1=oT[:, 12 + jj, :],
                                                                         op0=ALU.mult, op1=ALU.add),
                                 reads=[od, mask4, oT], writes=[oT])
            else:
                P.op('sp', lambda e, t0=t0: e.dma_start(out=hT[:], in_=u.hTs[:, :, t0:t0 + 512]), reads=[('hT', u.i)],
                     writes=[hT], dma=True)
                P.op('sp', lambda e, t0=t0: e.dma_start(out=oT[:], in_=u.oTs[:, :, t0:t0 + 512]), reads=[('oT', u.i)],
                     writes=[oT], dma=True)
                P.op('sp', lambda e, t0=t0: e.dma_start(out=xT[:], in_=u.xres[:, :, t0:t0 + 512]),
                     reads=[('xres', u.i)], writes=[xT], dma=True)
            for i in range(4):
                wb = load_w(w_in, D, C_GATE + 512 * i, 512)
                for jj in range(4):
                    j = 4 * i + jj
                    ps = ring_g.next()
                    mm_group(ps, ps[:, :], [(wb[:, k, 128 * jj:128 * jj + 128], hT[:, k, :]) for k in range(8)],
                             reads=[wb, hT])
                    sg = tmpf.next()
                    P.op('act', lambda e, sg=sg, ps=ps: e.activation(out=sg[:, :], in_=ps[:, :], func=AF.Sigmoid),
                         reads=[ps], writes=[sg])
                    P.op('dve', lambda e, sg=sg, ps=ps: e.tensor_tensor(out=sg[:, :], in0=sg[:, :], in1=ps[:, :],
                                                                       op=ALU.mult), reads=[ps, sg], writes=[sg])
                    P.op('pool', lambda e, sg=sg, j=j: e.tensor_tensor(out=oT[:, j, :], in0=oT[:, j, :], in1=sg[:, :],
                                                                      op=ALU.mult), reads=[oT, sg], writes=[oT])
            for n in range(4):
                for h in range(2):
                    wbo = load_w(I['w_bo'][l, n], 512, 512 * h, 512)
                    wmg = load_w(w_in, D, C_MERGE + 1024 * n + 512 * h, 512)
                    for jj in range(4):
                        jd = 4 * h + jj
                        ps1 = ring_acc.next()
                        mm_group(ps1, ps1[:, :], [(wbo[:, k, 128 * jj:128 * jj + 128], oT[:, 4 * n + k, :])
                                                  for k in range(4)], reads=[wbo, oT])
                        ps2 = ring_g.next()
                        mm_group(ps2, ps2[:, :], [(wmg[:, k, 128 * jj:128 * jj + 128], hT[:, k, :]) for k in range(8)],
                                 reads=[wmg, hT])
                        sg = tmpf.next()
                        P.op('act', lambda e, sg=sg, ps2=ps2: e.activation(out=sg[:, :], in_=ps2[:, :], func=AF.Sigmoid),
                             reads=[ps2], writes=[sg])
                        if n == 0:
                            P.op('dve', lambda e, sg=sg, ps1=ps1, jd=jd: e.tensor_tensor(
                                out=acc[:, jd, :], in0=ps1[:, :], in1=sg[:, :], op=ALU.mult),
                                reads=[ps1, sg], writes=[acc])
                        else:
                            P.op('dve', lambda e, sg=sg, ps1=ps1: e.tensor_tensor(
                                out=sg[:, :], in0=ps1[:, :], in1=sg[:, :], op=ALU.mult), reads=[ps1, sg], writes=[sg])
                            P.op('pool', lambda e, sg=sg, jd=jd: e.tensor_tensor(
                                out=acc[:, jd, :], in0=acc[:, jd, :], in1=sg[:, :], op=ALU.add),
                                reads=[acc, sg], writes=[acc])
            for k in range(8):
                acopy(mg[:, k, :], mg, acc[:, k, :], acc)
            for h in range(2):
                wb = load_w(I['w_out'][l], D, 512 * h, 512)
                for jj in range(4):
                    jd = 4 * h + jj
                    ps = ring_g.next()
                    mm_group(ps, ps[:, :], [(wb[:, k, 128 * jj:128 * jj + 128], mg[:, k, :]) for k in range(8)],
                             reads=[wb, mg])
                    P.op('dve', lambda e, ps=ps, jd=jd: e.scalar_tensor_tensor(
                        out=xT[:, jd, :], in0=ps[:, :], scalar=modv[:, l, 16 + jd:17 + jd, pth], in1=xT[:, jd, :],
                        op0=ALU.mult, op1=ALU.add), reads=[ps, modv, xT], writes=[xT])
            if l < DEPTH - 1:
                P.op('sp', lambda e, t0=t0: e.dma_start(out=u.xres[:, :, t0:t0 + 512], in_=xT[:]),
                     reads=[xT], writes=[('xres', u.i)], dma=True)
            else:
                sqs = []
                for k in range(8):
                    s = tmpb.next()
                    P.op('act', lambda e, s=s, k=k: e.activation(out=s[:, :], in_=xT[:, k, :], func=AF.Square),
                         reads=[xT], writes=[s])
                    sqs.append(s)
                r = rstd_from_sq(sqs, float(D), ones[:, :])
                for k in range(8):
                    P.op('dve', lambda e, k=k, r=r: e.scalar_tensor_tensor(out=xT[:, k, :], in0=xT[:, k, :],
                                                                          scalar=cv(0, 24 + k), in1=r[:, :],
                                                                          op0=ALU.mult, op1=ALU.mult),
                         reads=[xT, colv, r], writes=[xT])
                for tt in range(4):
                    yt = ytok.next()
                    for k in range(8):
                        transpose_to(yt[:, 128 * k:128 * k + 128], yt, xT[:, k, tt * 128:(tt + 1) * 128], xT, 128, 128)
                    r0 = t0 + tt * 128
                    P.op('sp', lambda e, yt=yt, r0=r0: e.dma_start(out=u.yout[r0:r0 + 128, :], in_=yt[:, :]),
                         reads=[yt], dma=True)
        RG[0] = Ring(psb[0:2])

    phases = []
    for l in range(DEPTH):
        phases.append(('mod%d' % l, lambda l=l: modulation(l)))
        for u in units:
            phases.append(('A%du%d' % (l, u.i), lambda l=l, u=u: stage_A_unit(l, u)))
            phases.append(('B%du%d' % (l, u.i), lambda l=l, u=u: attention(l, u)))
        phases.append(('S%d' % l, lambda l=l: ssm(l)))
        for u in units:
            phases.append(('C%du%d' % (l, u.i), lambda l=l, u=u: stage_C_unit(l, u)))
    for name, fn in phases:
        fn()
        if stop is not None and name == stop:
            break
    P.finish()
    P.emit(st)
    st.close()
    return nc, P, arena


def _rope_tables(rot_dim):
    rows = TS // 64
    row = np.repeat(np.arange(rows, dtype=np.float32), 64)
    col = np.tile(np.arange(64, dtype=np.float32), rows)
    quarter = rot_dim // 4
    inv = (np.float32(10000.0) ** (-np.arange(quarter, dtype=np.float32) / np.float32(quarter))).astype(np.float32)
    ang = np.concatenate([row[:, None] * inv, col[:, None] * inv], axis=-1).astype(np.float32)
    return np.cos(ang).astype(np.float32), np.sin(ang).astype(np.float32)


def _host_prep(inp):
    f = np.float32
    shared = {}
    shared['w_mod'] = np.ascontiguousarray(inp['w_mod'], dtype=f)
    w_in = np.ascontiguousarray(inp['w_in'], dtype=f)
    shared['w_in'] = w_in

    def sw_idx(base, nheads, hd):
        idx = []
        for h in range(nheads):
            for d in range(hd):
                idx.append(base + h * hd + (d + hd // 2) % hd)
        return idx
    idx = sw_idx(C_KPE, 1, 32) + sw_idx(C_DQ, 8, 64) + sw_idx(C_DK, 8, 64) + sw_idx(C_GQ, 8, 64) + sw_idx(C_GK, 2, 64)
    shared['w_in_sw'] = np.ascontiguousarray(w_in[:, :, idx])
    wq = np.asarray(inp['w_mla_q_b'], dtype=f)
    cols = []
    for j in range(4):
        for hh in range(2):
            cols += [(2 * j + hh) * 96 + d for d in range(64)]
    for h in range(8):
        cols += [h * 96 + 64 + d for d in range(32)]
    for h in range(8):
        cols += [h * 96 + 64 + (d + 16) % 32 for d in range(32)]
    shared['w_qb'] = np.ascontiguousarray(wq[:, :, cols])
    wkv = np.asarray(inp['w_mla_kv_b'], dtype=f)
    cols = [h * 128 + d for h in range(8) for d in range(64)] + [h * 128 + 64 + e for h in range(8) for e in range(64)]
    shared['w_kvb'] = np.ascontiguousarray(wkv[:, :, cols])
    shared['lam_rows'] = np.ascontiguousarray(np.stack([inp['diff_lq1'], inp['diff_lk1'], inp['diff_lq2'],
                                                        inp['diff_lk2']], axis=1), dtype=f)
    shared['w_glu'] = np.ascontiguousarray(inp['ssm_glu_w'], dtype=f)
    shared['w_bo'] = np.ascontiguousarray(inp['w_branch_out'], dtype=f)
    shared['w_out'] = np.ascontiguousarray(inp['w_out'], dtype=f)
    shared['ident'] = np.eye(128, dtype=f)
    blk = np.zeros((128, 128), f)
    blk[:64, :64] = 1
    blk[64:, 64:] = 1
    shared['blk64'] = blk
    c64, s64 = _rope_tables(64)
    c32, s32 = _rope_tables(32)
    r64 = np.zeros((2, 128, TS), f)
    for p in range(128):
        d = p % 64
        r64[0, p] = c64[:, d % 32]
        r64[1, p] = -s64[:, d] if d < 32 else s64[:, d - 32]
    r32 = np.zeros((2, 32, TS), f)
    for p in range(32):
        r32[0, p] = c32[:, p % 16]
        r32[1, p] = -s32[:, p] if p < 16 else s32[:, p - 16]
    shared['rope64'] = r64
    shared['rope32'] = r32
    colv = np.zeros((128, 64), f)
    for l in range(DEPTH):
        o = 32 * l
        colv[:, o:o + 8] = np.asarray(inp['norm_g'][l]).reshape(8, 128).T
        colv[:, o + 8:o + 10] = np.asarray(inp['mla_q_norm'][l]).reshape(2, 128).T
        colv[:, o + 10] = np.asarray(inp['mla_kv_norm'][l])
        gq = np.asarray(inp['gqa_q_norm'][l])
        gk = np.asarray(inp['gqa_k_norm'][l])
        pp = np.arange(128) % 64
        colv[:, o + 11] = gq[pp]
        colv[:, o + 12] = gq[(pp + 32) % 64]
        colv[:, o + 13] = gk[pp]
        colv[:, o + 14] = gk[(pp + 32) % 64]
        colv[:, o + 15] = np.asarray(inp['diff_subln'][l])
        colv[:, o + 16:o + 20] = np.asarray(inp['ssm_d'][l]).reshape(4, 128).T
        colv[:, o + 20:o + 24] = np.asarray(inp['ssm_glu_b'][l]).reshape(4, 128).T
        colv[:, o + 24:o + 32] = np.asarray(inp['final_norm']).reshape(8, 128).T
    shared['colvecs'] = colv
    shared['bmod'] = np.ascontiguousarray(np.asarray(inp['b_mod'], dtype=f).reshape(DEPTH, 24, 128).transpose(2, 0, 1))
    a = np.zeros((DEPTH, 2, 3, 128, 16), f)
    for l in range(DEPTH):
        for d in range(2):
            a[l, d, 0] = np.asarray(inp['ssm_a_re'][l, d]).reshape(16, 128).T
            a[l, d, 1] = np.asarray(inp['ssm_a_im'][l, d]).reshape(16, 128).T
            a[l, d, 2] = np.repeat(np.asarray(inp['ssm_log_dt'][l, d]), 64).reshape(16, 128).T
    shared['ssm_a'] = a
    shared['ssm_b'] = np.ascontiguousarray(np.stack([inp['ssm_b_re'], inp['ssm_b_im']], axis=2), dtype=f)
    shared['ssm_c'] = np.ascontiguousarray(np.stack([inp['ssm_c_re'], inp['ssm_c_im']], axis=2), dtype=f)
    in_maps = []
    for i in range(8):
        b = i // 4
        m = dict(shared)
        m['xp'] = np.ascontiguousarray(np.asarray(inp['x_prompt'][4 * i:4 * i + 4], dtype=f).reshape(1024, D))
        m['xs'] = np.ascontiguousarray(inp['x_sample'][b], dtype=f)
        m['c_ckv'] = np.ascontiguousarray(inp['cache_mla_ckv'][b], dtype=f)
        m['c_krope'] = np.ascontiguousarray(inp['cache_mla_krope'][b], dtype=f)
        m['c_dk'] = np.ascontiguousarray(np.asarray(inp['cache_diff_k'][b], dtype=f).reshape(DEPTH, PAST, 512))
        m['c_dv'] = np.ascontiguousarray(np.asarray(inp['cache_diff_v'][b], dtype=f).reshape(DEPTH, PAST, 512))
        m['c_gk'] = np.ascontiguousarray(np.asarray(inp['cache_gqa_k'][b], dtype=f).reshape(DEPTH, PAST, 128))
        m['c_gv'] = np.ascontiguousarray(np.asarray(inp['cache_gqa_v'][b], dtype=f).reshape(DEPTH, PAST, 128))
        m['st_ssm'] = np.ascontiguousarray(np.asarray(inp['state_ssm'][b], dtype=f).reshape(DEPTH, 2, 16, 128, 2))
        cvec = np.zeros((128, 8, 2), f)
        cvec[:, :, 0] = np.asarray(inp['c_ctx']).reshape(8, 128).T
        cvec[:, :, 1] = np.asarray(inp['c'][b]).reshape(8, 128).T
        m['cvec'] = cvec
        j = i % 4
        mk = np.zeros((128, 4), f)
        mk[:, j] = 1.0
        m['mask4'] = mk
        p64 = np.zeros((128, 128), f)
        for mm_ in range(128):
            p64[64 * (mm_ // 64) + ((mm_ % 64) + 32) % 64, mm_] = 1.0
        p32 = np.zeros((128, 128), f)
        for mm_ in range(32):
            p32[(mm_ + 16) % 32, mm_] = 1.0
        m['perm64'] = p64
        m['perm32'] = p32
        m['rope64o'] = np.ascontiguousarray(shared['rope64'][:, :, 512 * j:512 * j + 512])
        m['rope32o'] = np.ascontiguousarray(shared['rope32'][:, :, 512 * j:512 * j + 512])
        in_maps.append(m)
    return in_maps


_CACHE = {}


def kernel(**inputs):
    inp = {k: np.asarray(v) for k, v in inputs.items()}
    if 'nc' not in _CACHE:
        _CACHE['nc'] = build_program()[0]
    nc = _CACHE['nc']
    in_maps = _host_prep(inp)
    res = run_bass_kernel_spmd(nc, in_maps, core_ids=list(range(8))).results
    f = np.float32
    y_prompt = np.zeros((32, TSEQ, D), f)
    y_sample = np.zeros((2, TS, D), f)
    n_ckv = np.zeros((32, DEPTH, TSEQ, 128), f)
    n_krope = np.zeros((32, DEPTH, TSEQ, 32), f)
    n_dk = np.zeros((32, DEPTH, TSEQ, 4, 2, 64), f)
    n_dv = np.zeros((32, DEPTH, TSEQ, 4, 128), f)
    n_gk = np.zeros((32, DEPTH, TSEQ, 2, 64), f)
    n_gv = np.zeros((32, DEPTH, TSEQ, 2, 64), f)
    n_ssm = np.zeros((32, DEPTH, 2, 32, 64, 2), f)
    for i in range(8):
        r = res[i]
        b, j = i // 4, i % 4
        sl = slice(4 * i, 4 * i + 4)
        y_prompt[sl] = r['yp'].reshape(4, TSEQ, D)
        y_sample[b, 512 * j:512 * j + 512] = r['ys']
        n_ckv[sl] = r['n_ckv'].reshape(DEPTH, 4, TSEQ, 128).transpose(1, 0, 2, 3)
        n_krope[sl] = r['n_krope'].reshape(DEPTH, 4, TSEQ, 32).transpose(1, 0, 2, 3)
        n_dk[sl] = r['n_dk'].reshape(DEPTH, 4, TSEQ, 4, 2, 64).transpose(1, 0, 2, 3, 4, 5)
        n_dv[sl] = r['n_dv'].reshape(DEPTH, 4, TSEQ, 4, 128).transpose(1, 0, 2, 3, 4)
        n_gk[sl] = r['n_gk'].reshape(DEPTH, 4, TSEQ, 2, 64).transpose(1, 0, 2, 3, 4)
        n_gv[sl] = r['n_gv'].reshape(DEPTH, 4, TSEQ, 2, 64).transpose(1, 0, 2, 3, 4)
        s = r['n_ssm'].reshape(DEPTH, 4, 2, 16, 2, 64, 2)
        n_ssm[sl] = s.reshape(DEPTH, 4, 2, 32, 64, 2).transpose(1, 0, 2, 3, 4, 5)
    return (y_prompt, y_sample, n_ckv, n_krope, n_dk, n_dv, n_gk, n_gv, n_ssm)
```

```python
import math
from contextlib import ExitStack
import numpy as np
import concourse.bass as bass
import concourse.mybir as mybir
from concourse.bass_utils import run_bass_kernel_spmd

F32 = mybir.dt.float32
BF16 = mybir.dt.bfloat16
I32 = mybir.dt.int32
AF = mybir.ActivationFunctionType
ALU = mybir.AluOpType
AX = mybir.AxisListType
ENG = ['pe', 'act', 'dve', 'pool', 'sp']
NDSEM = 12
DEBUG = False

D = 1024
DEPTH = 2
NP_SEQ = 4
TSEQ = 256
TS = 2048
PAST = 512
SMAX = PAST + TS
EPS = 1e-6
D_IN = 9376
C_QA, C_KVA, C_KPE, C_DQ, C_DK, C_DV, C_GQ, C_GK, C_GV, C_U, C_GATE, C_MERGE = (
    0, 256, 384, 416, 928, 1440, 1952, 2464, 2592, 2720, 3232, 5280)
MLA_SCALE = 96 ** -0.5
SCALE64 = 64 ** -0.5
PAGE = 512
NPOW = 11


def keys_of(b):
    k = b.key if hasattr(b, 'key') else b
    if isinstance(k, list):
        return k
    return [k]


class Buf:
    def __init__(self, t, key):
        self.t = t
        self.key = key

    def __getitem__(self, idx):
        return self.t[idx]


class Ring:
    def __init__(self, bufs):
        self.bufs = bufs
        self.i = 0

    def next(self):
        b = self.bufs[self.i % len(self.bufs)]
        self.i += 1
        return b


class _Rec:
    def __getattr__(self, name):
        def f(*a, **kw):
            self.call = (name, a, kw)
            return None
        return f


class Prog:
    def __init__(self, nc):
        self.nc = nc
        self.stream = {e: [] for e in ENG}
        self.count = {}
        self.waited = {e: {} for e in ENG}
        self.wr = {}
        self.rd = {}
        self.dsem_i = {e: 0 for e in ENG}
        self.nops = 0

    def _wait(self, eng, s, v):
        if v > 0 and self.waited[eng].get(s, 0) < v:
            self.stream[eng].append(('wait', s, v))
            self.waited[eng][s] = v

    def op(self, eng, fn, reads=(), writes=(), dma=False, milestone=True):
        if getattr(self, 'maxops', None) is not None and self.nops >= self.maxops:
            return
        self.nops += 1
        rec = _Rec()
        fn(rec)
        fn = rec.call
        if not hasattr(self, 'log'):
            self.log = []
        self.log.append((eng, rec.call[0], dma))
        needs = {}

        def need(d):
            if d:
                for s, v in d.items():
                    if v > needs.get(s, 0):
                        needs[s] = v
        rk = [k for b in reads for k in keys_of(b)]
        wk = [k for b in writes for k in keys_of(b)]
        for k in rk:
            need(self.wr.get(k))
        for k in wk:
            need(self.wr.get(k))
            need(self.rd.get(k))
        for s, v in needs.items():
            if eng == 'pe' and s == 'c_pe':
                continue
            self._wait(eng, s, v)
        if dma:
            sem = 'd_%s_%d' % (eng, self.dsem_i[eng] % NDSEM)
            self.dsem_i[eng] += 1
            amt = 16
            self._wait(eng, sem, self.count.get(sem, 0))
        else:
            sem, amt = 'c_' + eng, 1
        if milestone:
            self.count[sem] = self.count.get(sem, 0) + amt
            val = self.count[sem]
            self.stream[eng].append(('op', fn, sem, amt))
        else:
            val = self.count.get(sem, 0) + amt
            self.stream[eng].append(('op', fn, None, 0))
        for k in rk:
            d = self.rd.setdefault(k, {})
            if d.get(sem, 0) < val:
                d[sem] = val
        for k in wk:
            d = self.wr.setdefault(k, {})
            if d.get(sem, 0) < val:
                d[sem] = val

    def finish(self):
        for s, v in self.count.items():
            self._wait('sp', s, v)

    def emit(self, stack):
        nc = self.nc
        sems = {}
        for s in self.count:
            sems[s] = stack.enter_context(nc.semaphore(s))
        block = stack.enter_context(nc.Block())

        def run(e, name):
            for it in self.stream[name]:
                if it[0] == 'wait':
                    e.wait_ge(sems[it[1]], it[2])
                else:
                    nm, a, kw = it[1]
                    ins = getattr(e, nm)(*a, **kw)
                    if it[2] is not None:
                        ins.then_inc(sems[it[2]], it[3])

        @block.tensor
        def _(e):
            run(e, 'pe')

        @block.scalar
        def _(e):
            run(e, 'act')

        @block.vector
        def _(e):
            run(e, 'dve')

        @block.gpsimd
        def _(e):
            run(e, 'pool')

        @block.sync
        def _(e):
            run(e, 'sp')


class Arena:
    def __init__(self, tensor, words):
        self.t = tensor
        self.words = words
        self.top = 0
        self.peak = 0

    def alloc(self, shape, dt):
        free = 1
        for s in shape[1:]:
            free *= s
        w = free if dt in (F32, I32) else (free + 1) // 2
        npg = (w + PAGE - 1) // PAGE
        off = self.top
        self.top += npg * PAGE
        self.peak = max(self.peak, self.top)
        assert self.top <= self.words, ('arena overflow', self.top, self.words)
        v = self.t[:, off:off + npg * PAGE]
        if dt != F32:
            v = v.bitcast(dt)
        v = v[0:shape[0], 0:free]
        if len(shape) == 3:
            v = v.rearrange("p (a b) -> p a b", a=shape[1], b=shape[2])
        elif len(shape) == 4:
            v = v.rearrange("p (a b c) -> p a b c", a=shape[1], b=shape[2], c=shape[3])
        elif len(shape) == 5:
            v = v.rearrange("p (a b c d) -> p a b c d", a=shape[1], b=shape[2], c=shape[3], d=shape[4])
        return Buf(v, ['ar%d' % (off // PAGE + i) for i in range(npg)])


def build_program(stop=None, debug=False, maxops=None):
    global DEBUG
    DEBUG = debug
    nc = bass.Bass("TRN2", target_bir_lowering=False)
    st = ExitStack()
    P = Prog(nc)
    P.maxops = maxops
    uid = [0]

    def sb(shape, dt, name=None):
        uid[0] += 1
        nm = '%s_%d' % (name or 'sb', uid[0])
        return Buf(st.enter_context(nc.sbuf_tensor(nm, list(shape), dt)), nm)

    def din(name, shape, dt=F32):
        return nc.dram_tensor(name, list(shape), dt, kind="ExternalInput").ap()

    def dout(name, shape, dt=F32):
        return nc.dram_tensor(name, list(shape), dt, kind="ExternalOutput").ap()

    def dscr(name, shape, dt):
        if DEBUG:
            return nc.dram_tensor(name, list(shape), dt, kind="ExternalOutput").ap()
        return nc.dram_tensor(name, list(shape), dt).ap()

    I = {}
    I['xp'] = din('xp', [NP_SEQ * TSEQ, D])
    I['xs'] = din('xs', [TS, D])
    I['c_ckv'] = din('c_ckv', [DEPTH, PAST, 128])
    I['c_krope'] = din('c_krope', [DEPTH, PAST, 32])
    I['c_dk'] = din('c_dk', [DEPTH, PAST, 512])
    I['c_dv'] = din('c_dv', [DEPTH, PAST, 512])
    I['c_gk'] = din('c_gk', [DEPTH, PAST, 128])
    I['c_gv'] = din('c_gv', [DEPTH, PAST, 128])
    I['st_ssm'] = din('st_ssm', [DEPTH, 2, 16, 128, 2])
    I['cvec'] = din('cvec', [128, 8, 2])
    I['colvecs'] = din('colvecs', [128, 64])
    I['bmod'] = din('bmod', [128, 2, 24])
    I['w_mod'] = din('w_mod', [DEPTH, D, 3 * D])
    I['w_in'] = din('w_in', [DEPTH, D, D_IN])
    I['w_in_sw'] = din('w_in_sw', [DEPTH, D, 1696])
    I['w_qb'] = din('w_qb', [DEPTH, 256, 1024])
    I['w_kvb'] = din('w_kvb', [DEPTH, 128, 1024])
    I['lam_rows'] = din('lam_rows', [DEPTH, 4, 64])
    I['w_glu'] = din('w_glu', [DEPTH, 512, 512])
    I['w_bo'] = din('w_bo', [DEPTH, 4, 512, D])
    I['w_out'] = din('w_out', [DEPTH, D, D])
    I['ident'] = din('ident', [128, 128])
    I['blk64'] = din('blk64', [128, 128])
    I['rope64'] = din('rope64', [2, 128, TS])
    I['rope32'] = din('rope32', [2, 32, TS])
    I['ssm_a'] = din('ssm_a', [DEPTH, 2, 3, 128, 16])
    I['ssm_b'] = din('ssm_b', [DEPTH, 2, 2, 32, 64, 16])
    I['ssm_c'] = din('ssm_c', [DEPTH, 2, 2, 32, 16, 64])
    I['mask4'] = din('mask4', [128, 4])
    I['rope64o'] = din('rope64o', [2, 128, 512])
    I['rope32o'] = din('rope32o', [2, 32, 512])
    O = {}
    O['yp'] = dout('yp', [NP_SEQ * TSEQ, D])
    O['ys'] = dout('ys', [512, D])
    O['n_ckv'] = dout('n_ckv', [DEPTH, NP_SEQ * TSEQ, 128])
    O['n_krope'] = dout('n_krope', [DEPTH, NP_SEQ * TSEQ, 32])
    O['n_dk'] = dout('n_dk', [DEPTH, NP_SEQ * TSEQ, 512])
    O['n_dv'] = dout('n_dv', [DEPTH, NP_SEQ * TSEQ, 512])
    O['n_gk'] = dout('n_gk', [DEPTH, NP_SEQ * TSEQ, 128])
    O['n_gv'] = dout('n_gv', [DEPTH, NP_SEQ * TSEQ, 128])
    O['n_ssm'] = dout('n_ssm', [DEPTH, 128, 256])

    class Unit:
        pass
    units = []
    for ui in range(3):
        u = Unit()
        u.i = ui
        u.sample = (ui == 2)
        u.T = TS if u.sample else 512
        u.nblk = u.T // 512
        u.nss = 1 if u.sample else 2
        u.Ts = TS if u.sample else TSEQ
        u.nseq = 1 if u.sample else 2
        u.S = SMAX if u.sample else 512
        u.koff = PAST if u.sample else 0
        u.xin = I['xs'] if u.sample else I['xp'][ui * 512:(ui + 1) * 512, :]
        u.yout = O['ys'] if u.sample else O['yp'][ui * 512:(ui + 1) * 512, :]
        u.tok0 = 0 if u.sample else ui * 512
        u.xres = dscr('xres%d' % ui, [128, 8, u.T], F32)
        u.hTs = dscr('s_hT%d' % ui, [128, 8, u.T], BF16)
        u.uTs = dscr('s_uT%d' % ui, [128, 4, u.T], BF16)
        u.oTs = dscr('s_oT%d' % ui, [128, 16, u.T], BF16)
        u.xo = dscr('s_xo%d' % ui, [128, 8, 512], F32)
        u.hTo = dscr('s_hTo%d' % ui, [128, 8, 512], BF16)
        units.append(u)
    scr = {}
    scr['q_mn'] = dscr('s_qmn', [128, 4, TS], BF16)
    scr['q_mp'] = dscr('s_qmp', [32, 8, TS], BF16)
    scr['q_d'] = dscr('s_qd', [128, 4, TS], BF16)
    scr['q_g'] = dscr('s_qg', [128, 4, TS], BF16)
    scr['k_mn'] = dscr('s_kmn', [128, 4, SMAX], BF16)
    scr['k_mp'] = dscr('s_kmp', [32, SMAX], BF16)
    scr['k_d'] = dscr('s_kd', [128, 4, SMAX], BF16)
    scr['k_g'] = dscr('s_kg', [128, 1, SMAX], BF16)
    scr['v_m'] = dscr('s_vm', [SMAX, 512], BF16)
    scr['v_d'] = dscr('s_vd', [SMAX, 512], BF16)
    scr['v_g'] = dscr('s_vg', [SMAX, 128], BF16)

    ident = sb([128, 128], F32, 'ident')
    blk64 = sb([128, 128], BF16, 'blk64')
    ones = sb([128, 128], BF16, 'ones')
    colv = sb([128, 64], F32, 'colv')
    cvec = sb([128, 8, 2], F32, 'cvec')
    bmod = sb([128, 2, 24], F32, 'bmod')
    silc = sb([128, 8, 2], F32, 'silc')
    modv = sb([128, 2, 24, 2], F32, 'modv')
    gmod = sb([128, 2, 8, 2], F32, 'gmod')
    lamneg = sb([128, 2], F32, 'lamneg')
    mask4 = sb([128, 4], F32, 'mask4')
    tmpf = Ring([sb([128, 512], F32, 'tmpf') for _ in range(20)])
    tmpb = Ring([sb([128, 512], BF16, 'tmpb') for _ in range(8)])
    stage = Ring([sb([128, 512], F32, 'stage') for _ in range(4)])
    stageb = Ring([sb([128, 4, 512], BF16, 'stageb') for _ in range(4)])
    prm = sb([128, 2, 16, 12], F32, 'prm')
    pwc = sb([128, 2, 16, NPOW], F32, 'pwc')
    pws = sb([128, 2, 16, NPOW], F32, 'pws')
    ARENA_WORDS = 64 * PAGE
    arena = Arena(st.enter_context(nc.sbuf_tensor('arena', [128, ARENA_WORDS], F32)), ARENA_WORDS)

    P.op('sp', lambda e: e.dma_start(out=ident[:], in_=I['ident']), writes=[ident], dma=True)
    P.op('pool', lambda e: e.dma_start(out=blk64[:], in_=I['blk64']), writes=[blk64], dma=True)
    P.op('sp', lambda e: e.dma_start(out=colv[:], in_=I['colvecs']), writes=[colv], dma=True)
    P.op('sp', lambda e: e.dma_start(out=cvec[:], in_=I['cvec']), writes=[cvec], dma=True)
    P.op('sp', lambda e: e.dma_start(out=bmod[:], in_=I['bmod']), writes=[bmod], dma=True)
    P.op('dve', lambda e: e.memset(ones[:], 1.0), writes=[ones])
    P.op('sp', lambda e: e.dma_start(out=mask4[:], in_=I['mask4']), writes=[mask4], dma=True)
    P.op('act', lambda e: e.activation(out=silc[:], in_=cvec[:], func=AF.Sigmoid), reads=[cvec], writes=[silc])
    P.op('dve', lambda e: e.tensor_tensor(out=silc[:], in0=silc[:], in1=cvec[:], op=ALU.mult),
         reads=[silc, cvec], writes=[silc])

    def cv(l, j, n=1):
        return colv[:, 32 * l + j:32 * l + j + n]

    psb = [Buf(st.enter_context(nc.psum_tensor('ps%d' % i, [128, 512], F32)), 'ps%d' % i) for i in range(8)]
    ring_g = Ring(psb[0:2])
    RG = [ring_g]
    ring_s = Ring(psb[2:4])
    ring_acc = Ring(psb[4:8])
    ring_all = Ring(psb[0:8])

    def mm_group(ps, out_ap, pairs, reads):
        n = len(pairs)
        for i, (lhsT, rhs) in enumerate(pairs):
            P.op('pe', (lambda e, lhsT=lhsT, rhs=rhs, i=i: e.matmul(out_ap, lhsT=lhsT, rhs=rhs,
                                                                     start=(i == 0), stop=(i == n - 1))),
                 reads=reads, writes=[ps], milestone=(i == n - 1))

    def acopy(dst_ap, dst_buf, src_ap, src_buf, scale=None):
        if scale is None:
            P.op('act', lambda e: e.activation(out=dst_ap, in_=src_ap, func=AF.Copy), reads=[src_buf], writes=[dst_buf])
        else:
            P.op('act', lambda e: e.activation(out=dst_ap, in_=src_ap, func=AF.Identity, scale=scale),
                 reads=[src_buf], writes=[dst_buf])

    def vcopy(dst_ap, dst_buf, src_ap, src_buf):
        P.op('dve', lambda e: e.tensor_copy(out=dst_ap, in_=src_ap), reads=[src_buf], writes=[dst_buf])

    tcount = [0]

    def transpose_to(dst_ap, dst_buf, src_ap, src_buf, rows, cols, ring=None, scale=None):
        ps = (ring or RG[0]).next()
        P.op('pe', lambda e: e.transpose(ps[0:cols, 0:rows], src_ap, ident[0:rows, 0:rows]),
             reads=[src_buf, ident], writes=[ps])
        tcount[0] += 1
        if scale is not None or tcount[0] % 2:
            acopy(dst_ap, dst_buf, ps[0:cols, 0:rows], ps, scale)
        else:
            vcopy(dst_ap, dst_buf, ps[0:cols, 0:rows], ps)

    def rstd_of(ps, ps_ap, n, denom, mult=1.0):
        r = tmpf.next()
        P.op('act', lambda e: e.activation(out=r[:, 0:n], in_=ps_ap, func=AF.Ln, scale=1.0 / denom, bias=EPS),
             reads=[ps], writes=[r])
        P.op('act', lambda e: e.activation(out=r[:, 0:n], in_=r[:, 0:n], func=AF.Exp, scale=-0.5,
                                           bias=math.log(mult)), reads=[r], writes=[r])
        return r

    def rstd_from_sq(sq_list, denom, lhs_ones):
        ps = RG[0].next()
        mm_group(ps, ps[:, :], [(lhs_ones, s[:, :]) for s in sq_list], reads=list(sq_list) + [ones, blk64])
        return rstd_of(ps, ps[:, :], 512, denom)

    def modulation(l):
        arena.top = 0
        wm = arena.alloc([128, 8, 3072], F32)
        for k in range(8):
            P.op('sp' if k % 2 == 0 else 'pool', lambda e, k=k: e.dma_start(out=wm[:, k, :], in_=I['w_mod'][l, k * 128:(k + 1) * 128, :]),
                 writes=[wm], dma=True)
        ps = ring_acc.next()
        for j in range(24):
            for k in range(8):
                P.op('pe', (lambda e, k=k, j=j: e.matmul(ps[:, 2 * j:2 * j + 2], lhsT=wm[:, k, j * 128:(j + 1) * 128],
                                                         rhs=silc[:, k, :], start=(k == 0), stop=(k == 7))),
                     reads=[wm, silc], writes=[ps], milestone=(k == 7))
        for pth in range(2):
            P.op('dve', lambda e, pth=pth: e.tensor_tensor(out=modv[:, l, :, pth], in0=ps[:, pth:48:2],
                                                           in1=bmod[:, l, :], op=ALU.add),
                 reads=[ps, bmod], writes=[modv])
            P.op('dve', lambda e, pth=pth: e.scalar_tensor_tensor(out=gmod[:, l, :, pth], in0=modv[:, l, 8:16, pth],
                                                                  scalar=1.0, in1=cv(l, 0, 8), op0=ALU.add,
                                                                  op1=ALU.mult),
                 reads=[modv, colv], writes=[gmod])
        if DEBUG and l == 0:
            dm = dout('dbg_mod', [128, 2, 24, 2])
            P.op('sp', lambda e: e.dma_start(out=dm, in_=modv[:]), reads=[modv], dma=True)
            dg = dout('dbg_gmod', [128, 2, 8, 2])
            P.op('sp', lambda e: e.dma_start(out=dg, in_=gmod[:]), reads=[gmod], dma=True)
        rows = arena.alloc([128, 4, 64], F32)
        P.op('sp', lambda e: e.dma_start(out=rows[:], in_=I['lam_rows'][l].partition_broadcast(128)),
             writes=[rows], dma=True)
        pr = arena.alloc([128, 2, 64], F32)
        P.op('dve', lambda e: e.tensor_tensor(out=pr[:, 0, :], in0=rows[:, 0, :], in1=rows[:, 1, :], op=ALU.mult),
             reads=[rows], writes=[pr])
        P.op('dve', lambda e: e.tensor_tensor(out=pr[:, 1, :], in0=rows[:, 2, :], in1=rows[:, 3, :], op=ALU.mult),
             reads=[rows], writes=[pr])
        sm = arena.alloc([128, 2], F32)
        P.op('dve', lambda e: e.tensor_reduce(out=sm[:, :], in_=pr[:, :, :], axis=AX.X, op=ALU.add),
             reads=[pr], writes=[sm])
        P.op('act', lambda e: e.activation(out=sm[:, :], in_=sm[:, :], func=AF.Exp), reads=[sm], writes=[sm])
        lam_init = 0.8 - 0.6 * math.exp(-0.3 * l)
        P.op('dve', lambda e: e.scalar_tensor_tensor(out=lamneg[:, l:l + 1], in0=sm[:, 1:2], scalar=-lam_init,
                                                     in1=sm[:, 0:1], op0=ALU.add, op1=ALU.subtract),
             reads=[sm], writes=[lamneg])

    def stage_A_unit(l, u):
        arena.top = 0
        pth = 1 if u.sample else 0
        w_in = I['w_in'][l]
        w_sw = I['w_in_sw'][l]
        ring_g = Ring(psb[0:8])
        RG[0] = ring_g
        wring = Ring([arena.alloc([128, 8, 512], BF16) for _ in range(5)])
        xT = arena.alloc([128, 8, 512], F32)
        hT = arena.alloc([128, 8, 512], BF16)
        xtok = Ring([arena.alloc([128, 1024], F32) for _ in range(2)])
        rope64 = arena.alloc([128, 2, 512], F32)
        rope32 = arena.alloc([32, 2, 512], F32)
        wqb = arena.alloc([128, 2, 1024], BF16)
        wkvb = arena.alloc([128, 1024], BF16)
        qan = arena.alloc([128, 2, 512], BF16)
        ckvn_b = arena.alloc([128, 512], BF16)
        cache_b = arena.alloc([128, 512], BF16)

        def load_w(src2d, rows, c0, ncols):
            wb = wring.next()
            kk = rows // 128
            src = src2d[:, c0:c0 + ncols].rearrange("(k p) c -> p k c", p=128)
            P.op('pool', lambda e: e.dma_start(out=wb[:, 0:kk, 0:ncols], in_=src), writes=[wb], dma=True)
            return wb

        P.op('pool', lambda e: e.dma_start(out=wqb[:], in_=I['w_qb'][l].rearrange("(k p) c -> p k c", p=128)),
             writes=[wqb], dma=True)
        P.op('pool', lambda e: e.dma_start(out=wkvb[:, :], in_=I['w_kvb'][l]), writes=[wkvb], dma=True)

        def store(dst, src_buf, src_ap, wkeys):
            P.op('sp', lambda e: e.dma_start(out=dst, in_=src_ap), reads=[src_buf], writes=wkeys, dma=True)

        def proj_fm(wb, c0, rhs_list, rhs_bufs, consumer, M=128):
            ps = ring_g.next()
            mm_group(ps, ps[0:M, :], [(wb[:, k, c0:c0 + M], r) for k, r in enumerate(rhs_list)],
                     reads=[wb] + rhs_bufs)
            consumer(ps)

        def out_tokmajor(dst_rows, src_buf, src_ap, nfeat):
            for tt in range(4):
                sg = stage.next()
                transpose_to(sg[:, 0:nfeat], sg, src_ap[:, tt * 128:(tt + 1) * 128], src_buf, nfeat, 128)
                P.op('sp', lambda e, sg=sg, tt=tt: e.dma_start(out=dst_rows(tt), in_=sg[:, 0:nfeat]),
                     reads=[sg], dma=True)

        def mla_kv_expand(ckv_b, kc0):
            sgb = stageb.next()
            for j in range(4):
                ps = ring_g.next()
                mm_group(ps, ps[:, :], [(wkvb[:, 128 * j:128 * j + 128], ckv_b[:, :])], reads=[wkvb, ckv_b])
                acopy(sgb[:, j, :], sgb, ps[:, :], ps)
            store(scr['k_mn'][:, :, kc0:kc0 + 512], sgb, sgb[:], ['k_mn'])
            for tt in range(4):
                ps = ring_g.next()
                mm_group(ps, ps[:, :], [(ckv_b[:, tt * 128:(tt + 1) * 128], wkvb[:, 512:1024])], reads=[wkvb, ckv_b])
                sg2 = stageb.next()
                vcopy(sg2[:, 0, :], sg2, ps[:, :], ps)
                r0 = kc0 + tt * 128
                P.op('sp', lambda e, sg2=sg2, r0=r0: e.dma_start(out=scr['v_m'][r0:r0 + 128, :], in_=sg2[:, 0, :]),
                     reads=[sg2], writes=['v_m'], dma=True)

        if u.sample:
            def cache_T(src, nfeat, dst_fn):
                for tt in range(4):
                    xt = xtok.next()
                    P.op('sp', lambda e, xt=xt, tt=tt: e.dma_start(out=xt[:, 0:nfeat],
                                                                   in_=src[tt * 128:(tt + 1) * 128, :]),
                         writes=[xt], dma=True)
                    for j in range((nfeat + 127) // 128):
                        w = min(128, nfeat - 128 * j)
                        dst_ap, dst_buf = dst_fn(j, tt, w)
                        transpose_to(dst_ap, dst_buf, xt[:, 128 * j:128 * j + w], xt, 128, w)
            cache_T(I['c_ckv'][l], 128, lambda j, tt, w: (cache_b[:, tt * 128:(tt + 1) * 128], cache_b))
            mla_kv_expand(cache_b, 0)
            sgb = stageb.next()
            cache_T(I['c_krope'][l], 32, lambda j, tt, w: (sgb[0:32, 0, tt * 128:(tt + 1) * 128], sgb))
            store(scr['k_mp'][:, 0:512], sgb, sgb[0:32, 0, :], ['k_mp'])
            sgb2 = stageb.next()
            cache_T(I['c_dk'][l], 512, lambda j, tt, w: (sgb2[:, j, tt * 128:(tt + 1) * 128], sgb2))
            store(scr['k_d'][:, :, 0:512], sgb2, sgb2[:], ['k_d'])
            sgb3 = stageb.next()
            cache_T(I['c_gk'][l], 128, lambda j, tt, w: (sgb3[:, 0, tt * 128:(tt + 1) * 128], sgb3))
            store(scr['k_g'][:, :, 0:512], sgb3, sgb3[:, 0:1, :], ['k_g'])
            for name, dst, w in (('c_dv', 'v_d', 512), ('c_gv', 'v_g', 128)):
                sg = stageb.next()
                P.op('pool', lambda e, sg=sg, name=name, w=w: e.dma_start(
                    out=sg[:, :, 0:w], in_=I[name][l].rearrange("(t p) c -> p t c", p=128)), writes=[sg], dma=True)
                P.op('sp', lambda e, sg=sg, dst=dst, w=w: e.dma_start(
                    out=scr[dst][0:512, :].rearrange("(t p) c -> p t c", p=128), in_=sg[:, :, 0:w]),
                    reads=[sg], writes=[dst], dma=True)

        own_mode = (u.sample and l == DEPTH - 1)
        if own_mode:
            xown = arena.alloc([128, 8, 512], F32)
            hown = arena.alloc([128, 8, 512], BF16)
        def prelude_own():
            for k in range(8):
                acopy(hT[:, k, :], hT, hown[:, k, :], hown)
            P.op('sp', lambda e: e.dma_start(out=u.xo[:], in_=xown[:]), reads=[xown], writes=[('xo', u.i)], dma=True)
            P.op('sp', lambda e: e.dma_start(out=u.hTo[:], in_=hown[:]), reads=[hown], writes=[('hTo', u.i)], dma=True)
            P.op('sp', lambda e: e.dma_start(out=rope64[:], in_=I['rope64o'].rearrange("a p t -> p a t")),
                 writes=[rope64], dma=True)
            P.op('sp', lambda e: e.dma_start(out=rope32[:], in_=I['rope32o'].rearrange("a p t -> p a t")),
                 writes=[rope32], dma=True)

        def prelude(b, t0):
            if l == 0:
                for tt in range(4):
                    xt = xtok.next()
                    r0 = t0 + tt * 128
                    P.op('sp', lambda e, xt=xt, r0=r0: e.dma_start(out=xt[:, :], in_=u.xin[r0:r0 + 128, :]),
                         writes=[xt], dma=True)
                    for k in range(8):
                        transpose_to(xT[:, k, tt * 128:(tt + 1) * 128], xT, xt[:, k * 128:(k + 1) * 128], xt, 128, 128)
                P.op('sp', lambda e, t0=t0: e.dma_start(out=u.xres[:, :, t0:t0 + 512], in_=xT[:]),
                     reads=[xT], writes=[('xres', u.i)], dma=True)
            else:
                P.op('sp', lambda e, t0=t0: e.dma_start(out=xT[:], in_=u.xres[:, :, t0:t0 + 512]),
                     reads=[('xres', u.i)], writes=[xT], dma=True)
            sqs = []
            for k in range(8):
                s = tmpb.next()
                P.op('act', lambda e, s=s, k=k: e.activation(out=s[:, :], in_=xT[:, k, :], func=AF.Square),
                     reads=[xT], writes=[s])
                sqs.append(s)
            r = rstd_from_sq(sqs, float(D), ones[:, :])
            for k in range(8):
                t = tmpf.next()
                P.op('dve', lambda e, t=t, k=k: e.scalar_tensor_tensor(out=t[:, :], in0=xT[:, k, :],
                                                                        scalar=gmod[:, l, k:k + 1, pth], in1=r[:, :],
                                                                        op0=ALU.mult, op1=ALU.mult),
                     reads=[xT, gmod, r], writes=[t])
                P.op('act', lambda e, t=t, k=k: e.activation(out=hT[:, k, :], in_=t[:, :], func=AF.Identity,
                                                             bias=modv[:, l, k:k + 1, pth], scale=1.0),
                     reads=[t, modv], writes=[hT])
            store(u.hTs[:, :, t0:t0 + 512], hT, hT[:], [('hT', u.i)])
            hk = [hT[:, k, :] for k in range(8)]
            if own_mode:
                for k in range(8):
                    if b == 0:
                        P.op('dve', lambda e: e.tensor_scalar(out=xown[:, k, :], in0=xT[:, k, :], scalar1=mask4[:, 0:1],
                                                              scalar2=None, op0=ALU.mult), reads=[xT, mask4], writes=[xown])
                        P.op('dve', lambda e: e.tensor_scalar(out=hown[:, k, :], in0=hT[:, k, :], scalar1=mask4[:, 0:1],
                                                              scalar2=None, op0=ALU.mult), reads=[hT, mask4], writes=[hown])
                    else:
                        P.op('dve', lambda e: e.scalar_tensor_tensor(out=xown[:, k, :], in0=xT[:, k, :],
                                                                     scalar=mask4[:, b:b + 1], in1=xown[:, k, :],
                                                                     op0=ALU.mult, op1=ALU.add),
                             reads=[xT, mask4, xown], writes=[xown])
                        P.op('dve', lambda e: e.scalar_tensor_tensor(out=hown[:, k, :], in0=hT[:, k, :],
                                                                     scalar=mask4[:, b:b + 1], in1=hown[:, k, :],
                                                                     op0=ALU.mult, op1=ALU.add),
                             reads=[hT, mask4, hown], writes=[hown])
            if u.sample:
                P.op('sp', lambda e, t0=t0: e.dma_start(
                    out=rope64[:], in_=I['rope64'][:, :, t0:t0 + 512].rearrange("a p t -> p a t")),
                    writes=[rope64], dma=True)
                P.op('sp', lambda e, t0=t0: e.dma_start(
                    out=rope32[:], in_=I['rope32'][:, :, t0:t0 + 512].rearrange("a p t -> p a t")),
                    writes=[rope32], dma=True)


        passes = [(b, not own_mode, True) for b in range(u.nblk)] + ([('own', True, False)] if own_mode else [])
        for b, do_q, do_k in passes:
            if b == 'own':
                kc0, t0 = 0, 0
                prelude_own()
            else:
                kc0 = u.koff + b * 512
                t0 = b * 512
                prelude(b, t0)
            hk = [hT[:, k, :] for k in range(8)]

            def rope_apply(dst_ap, dst_buf, a_ap, a_buf, s_ap, s_buf, tab, rows):
                t1 = tmpf.next()
                P.op('dve', lambda e: e.tensor_tensor(out=t1[0:rows, :], in0=a_ap, in1=tab[0:rows, 0, :], op=ALU.mult),
                     reads=[a_buf, tab], writes=[t1])
                t2 = tmpf.next()
                P.op('pool', lambda e: e.tensor_tensor(out=t2[0:rows, :], in0=s_ap, in1=tab[0:rows, 1, :], op=ALU.mult),
                     reads=[s_buf, tab], writes=[t2])
                P.op('dve', lambda e: e.tensor_tensor(out=dst_ap, in0=t1[0:rows, :], in1=t2[0:rows, :], op=ALU.add),
                     reads=[t1, t2], writes=[dst_buf])

            def evac_f(M):
                a_f = tmpf.next()

                def cons(ps):
                    acopy(a_f[0:M, :], a_f, ps[0:M, :], ps)
                return a_f, cons

            wb = load_w(w_in, D, 0, 416)
            wsw = load_w(w_sw, D, 0, 32) if u.sample else None
            qa_f = []
            sqs = []
            for j in (range(2) if do_q else []):
                a_f, cons = evac_f(128)
                proj_fm(wb, 128 * j, hk, [hT], cons)
                s = tmpb.next()
                P.op('act', lambda e, s=s, a_f=a_f: e.activation(out=s[:, :], in_=a_f[:, :], func=AF.Square),
                     reads=[a_f], writes=[s])
                sqs.append(s)
                qa_f.append(a_f)
            r = rstd_from_sq(sqs, 256.0, ones[:, :]) if do_q else None
            for j in (range(2) if do_q else []):
                P.op('dve', lambda e, j=j: e.scalar_tensor_tensor(out=qan[:, j, :], in0=qa_f[j][:, :],
                                                                  scalar=cv(l, 8 + j), in1=r[:, :], op0=ALU.mult,
                                                                  op1=ALU.mult),
                     reads=[qa_f[j], colv, r], writes=[qan])
            if do_k:
                ckv_f, cons = evac_f(128)
                proj_fm(wb, C_KVA, hk, [hT], cons)
                s = tmpb.next()
                P.op('act', lambda e, s=s: e.activation(out=s[:, :], in_=ckv_f[:, :], func=AF.Square),
                     reads=[ckv_f], writes=[s])
                r2 = rstd_from_sq([s], 128.0, ones[:, :])
                ckvn_f = tmpf.next()
                P.op('dve', lambda e: e.scalar_tensor_tensor(out=ckvn_f[:, :], in0=ckv_f[:, :], scalar=cv(l, 10),
                                                             in1=r2[:, :], op0=ALU.mult, op1=ALU.mult),
                     reads=[ckv_f, colv, r2], writes=[ckvn_f])
                acopy(ckvn_b[:, :], ckvn_b, ckvn_f[:, :], ckvn_f)
                if not u.sample:
                    out_tokmajor(lambda tt: O['n_ckv'][l, u.tok0 + t0 + tt * 128:u.tok0 + t0 + (tt + 1) * 128, :],
                                 ckvn_f, ckvn_f[:, :], 128)
                kpe_f, cons = evac_f(32)
                proj_fm(wb, C_KPE, hk, [hT], cons, M=32)
                sgb = stageb.next()
                if u.sample:
                    kpes_f, cons = evac_f(32)
                    proj_fm(wsw, 0, hk, [hT], cons, M=32)
                    rope_apply(sgb[0:32, 0, :], sgb, kpe_f[0:32, :], kpe_f, kpes_f[0:32, :], kpes_f, rope32, 32)
                else:
                    out_tokmajor(lambda tt: O['n_krope'][l, u.tok0 + t0 + tt * 128:u.tok0 + t0 + (tt + 1) * 128, :],
                                 kpe_f, kpe_f[0:32, :], 32)
                    acopy(sgb[0:32, 0, :], sgb, kpe_f[0:32, :], kpe_f)
                store(scr['k_mp'][:, kc0:kc0 + 512], sgb, sgb[0:32, 0, :], ['k_mp'])
            if do_q:
                qk = [qan[:, 0, :], qan[:, 1, :]]
                sgb = stageb.next()
                for j in range(4):
                    def cons_q(ps, j=j, sgb=sgb):
                        acopy(sgb[:, j, :], sgb, ps[:, :], ps)
                    proj_fm(wqb, 128 * j, qk, [qan], cons_q)
                store(scr['q_mn'][:, :, t0:t0 + 512], sgb, sgb[:], ['q_mn'])
                sgq = [stageb.next(), stageb.next()]
                for h in range(8):
                    dst = sgq[h // 4]
                    if u.sample:
                        a_f, cons_a = evac_f(32)
                        proj_fm(wqb, 512 + 32 * h, qk, [qan], cons_a, M=32)
                        s_f, cons_s = evac_f(32)
                        proj_fm(wqb, 768 + 32 * h, qk, [qan], cons_s, M=32)
                        rope_apply(dst[0:32, h % 4, :], dst, a_f[0:32, :], a_f, s_f[0:32, :], s_f, rope32, 32)
                    else:
                        def cons_p(ps, dst=dst, h=h):
                            acopy(dst[0:32, h % 4, :], dst, ps[0:32, :], ps)
                        proj_fm(wqb, 512 + 32 * h, qk, [qan], cons_p, M=32)
                for hh in range(2):
                    store(scr['q_mp'][:, 4 * hh:4 * hh + 4, t0:t0 + 512], sgq[hh], sgq[hh][0:32, :, :], ['q_mp'])
            if do_k:
                mla_kv_expand(ckvn_b, kc0)

            def orow(name, width):
                return lambda tt, j=None: (O[name][l, u.tok0 + t0 + tt * 128:u.tok0 + t0 + (tt + 1) * 128,
                                                   (0 if j is None else 128 * j):(width if j is None else 128 * j + 128)])

            def fm_cols(c_lo, c_sw, ntile, dst_scr, dst_key, col0, outrows=None, norm=None):
                wb = load_w(w_in, D, c_lo, 128 * ntile)
                wsw = load_w(w_sw, D, c_sw, 128 * ntile) if u.sample else None
                sgb = stageb.next()
                for j in range(ntile):
                    a_f, cons_a = evac_f(128)
                    proj_fm(wb, 128 * j, hk, [hT], cons_a)
                    rr = None
                    if norm is not None:
                        s = tmpb.next()
                        P.op('act', lambda e, s=s, a_f=a_f: e.activation(out=s[:, :], in_=a_f[:, :], func=AF.Square),
                             reads=[a_f], writes=[s])
                        rr = rstd_from_sq([s], 64.0, blk64[:, :])
                        P.op('dve', lambda e, a_f=a_f, rr=rr: e.scalar_tensor_tensor(
                            out=a_f[:, :], in0=a_f[:, :], scalar=cv(l, norm), in1=rr[:, :], op0=ALU.mult,
                            op1=ALU.mult), reads=[a_f, colv, rr], writes=[a_f])
                    if u.sample:
                        s_f, cons_s = evac_f(128)
                        proj_fm(wsw, 128 * j, hk, [hT], cons_s)
                        if norm is not None:
                            P.op('dve', lambda e, s_f=s_f, rr=rr: e.scalar_tensor_tensor(
                                out=s_f[:, :], in0=s_f[:, :], scalar=cv(l, norm + 1), in1=rr[:, :], op0=ALU.mult,
                                op1=ALU.mult), reads=[s_f, colv, rr], writes=[s_f])
                        rope_apply(sgb[:, j, :], sgb, a_f[:, :], a_f, s_f[:, :], s_f, rope64, 128)
                    else:
                        if outrows is not None:
                            out_tokmajor(lambda tt, j=j: outrows(tt, j), a_f, a_f[:, :], 128)
                        acopy(sgb[:, j, :], sgb, a_f[:, :], a_f)
                store(dst_scr[:, :, col0:col0 + 512], sgb, sgb[:, 0:ntile, :], [dst_key])

            def tm_cols(c_lo, ncols, dst_scr, dst_key, outrows=None):
                wb = load_w(w_in, D, c_lo, ncols)
                for tt in range(4):
                    ps = ring_g.next()
                    mm_group(ps, ps[:, 0:ncols],
                             [(hT[:, k, tt * 128:(tt + 1) * 128], wb[:, k, 0:ncols]) for k in range(8)],
                             reads=[wb, hT])
                    sgb = stageb.next()
                    r0 = kc0 + tt * 128
                    if outrows is not None:
                        sg = stage.next()
                        vcopy(sg[:, 0:ncols], sg, ps[:, 0:ncols], ps)
                        P.op('sp', lambda e, sg=sg, tt=tt: e.dma_start(out=outrows(tt), in_=sg[:, 0:ncols]),
                             reads=[sg], dma=True)
                        acopy(sgb[:, 0, 0:ncols], sgb, sg[:, 0:ncols], sg)
                    else:
                        acopy(sgb[:, 0, 0:ncols], sgb, ps[:, 0:ncols], ps)
                    P.op('sp', lambda e, sgb=sgb, r0=r0: e.dma_start(out=dst_scr[r0:r0 + 128, :],
                                                                     in_=sgb[:, 0, 0:ncols]),
                         reads=[sgb], writes=[dst_key], dma=True)

            pr = not u.sample
            if do_q:
                fm_cols(C_DQ, 32, 4, scr['q_d'], 'q_d', t0)
            if do_k:
                fm_cols(C_DK, 544, 4, scr['k_d'], 'k_d', kc0, outrows=(orow('n_dk', 512) if pr else None))
                tm_cols(C_DV, 512, scr['v_d'], 'v_d', outrows=(orow('n_dv', 512) if pr else None))
            if do_q:
                fm_cols(C_GQ, 1056, 4, scr['q_g'], 'q_g', t0, norm=11)
            if do_k:
                fm_cols(C_GK, 1568, 1, scr['k_g'], 'k_g', kc0, outrows=(orow('n_gk', 128) if pr else None), norm=13)
                tm_cols(C_GV, 128, scr['v_g'], 'v_g', outrows=(orow('n_gv', 128) if pr else None))
                wb = load_w(w_in, D, C_U, 512)
                sgb = stageb.next()
                for j in range(4):
                    def cons_u(ps, j=j, sgb=sgb):
                        acopy(sgb[:, j, :], sgb, ps[:, :], ps)
                    proj_fm(wb, 128 * j, hk, [hT], cons_u)
                store(u.uTs[:, :, t0:t0 + 512], sgb, sgb[:], [('uT', u.i)])

        RG[0] = Ring(psb[0:2])

    def attention(l, u):
        arena.top = 0
        S = u.S
        nkt_seq = SMAX // 128 if u.sample else 2
        Nq = 512 if u.sample else 256
        kT = arena.alloc([128, 4, S], BF16)
        kpe = arena.alloc([128, S], BF16)
        vv = arena.alloc([128, S // 128, 512], BF16)
        qT = arena.alloc([128, 4, 2, 512], BF16)
        qpe = arena.alloc([128, 8, 512], BF16)
        P.op('pool', lambda e: e.memset(qT[:], 0.0), writes=[qT])
        P.op('pool', lambda e: e.memset(qpe[:], 0.0), writes=[qpe])
        P.op('pool', lambda e: e.memset(kpe[:], 0.0), writes=[kpe])

        def load_q(qname, t0):
            for hh in range(2):
                P.op('sp', lambda e: e.dma_start(out=qT[64 * hh:64 * hh + 64, :, hh, :],
                                                 in_=scr[qname][64 * hh:64 * hh + 64, :, t0:t0 + 512]),
                     reads=[qname], writes=[qT], dma=True)
        ptring = Ring([arena.alloc([128, 512], BF16) for _ in range(8)])
        nkt_all = S // 128
        vaug = arena.alloc([128, nkt_all, 8, 128], BF16)

        def load_ctx(kname, vname, vw, nkt4):
            P.op('sp', lambda e: e.dma_start(out=kT[:, 0:nkt4, :], in_=scr[kname][:, :, 0:S]), reads=[kname],
                 writes=[kT], dma=True)
            P.op('sp', lambda e: e.dma_start(out=vv[:, :, 0:vw],
                                             in_=scr[vname][0:S, :].rearrange("(t p) c -> p t c", p=128)),
                 reads=[vname], writes=[vv], dma=True)

        ring_s3 = Ring(psb[1:5])
        ring_acc3 = Ring(psb[5:8])
        nq = 1 if (u.sample and l == DEPTH - 1) else u.nblk
        ring_g1 = Ring(psb[0:1])
        LA = 3

        def run_heads(heads):
            items = [(hd, i) for hd in heads for i in range(nkt_seq)]
            n = len(items)
            for idx in range(n + LA):
                if idx < n:
                    hd, i = items[idx]
                    q0 = hd['ss'] * Nq
                    kt = (hd['ss'] * 2 if not u.sample else 0) + i
                    sp_ = ring_s3.next()
                    hd.setdefault('sp', {})[i] = sp_
                    pairs = [(hd['k_fn'](kt), hd['q_fn'](q0))]
                    if hd['extra'] is not None:
                        pairs.append((hd['extra'][0](kt), hd['extra'][1](q0)))
                    mm_group(sp_, sp_[:, 0:Nq], pairs, reads=[kT, kpe, qT, qpe])
                j = idx - LA
                if j >= 0:
                    hd, i = items[j]
                    kt = (hd['ss'] * 2 if not u.sample else 0) + i
                    aug = hd.get('aug') is not None
                    if i == 0:
                        hd['ops'] = ring_acc3.next()
                        hd['sps'] = None if aug else ring_acc3.next()
                    ops_, sps_ = hd['ops'], hd['sps']
                    sp_ = hd['sp'].pop(i)
                    pt = ptring.next()
                    scale, vM, v_col0 = hd['scale'], hd['vM'], hd['v_col0']
                    P.op('act', lambda e: e.activation(out=pt[:, 0:Nq], in_=sp_[:, 0:Nq], func=AF.Exp, scale=scale),
                         reads=[sp_], writes=[pt])
                    if aug:
                        P.op('pe', lambda e: e.matmul(ops_[:, 0:Nq], lhsT=vaug[:, kt, hd['aug'], :],
                                                      rhs=pt[:, 0:Nq], start=(i == 0), stop=(i == nkt_seq - 1)),
                             reads=[vaug, pt], writes=[ops_])
                    else:
                        P.op('pe', lambda e: e.matmul(ops_[0:vM, 0:Nq], lhsT=vv[:, kt, v_col0:v_col0 + vM],
                                                      rhs=pt[:, 0:Nq], start=(i == 0), stop=(i == nkt_seq - 1)),
                             reads=[vv, pt], writes=[ops_], milestone=False)
                        P.op('pe', lambda e: e.matmul(sps_[:, 0:Nq], lhsT=ones[:, :], rhs=pt[:, 0:Nq],
                                                      start=(i == 0), stop=(i == nkt_seq - 1)),
                             reads=[ones, pt], writes=[sps_, ops_])
                    if i == nkt_seq - 1:
                        hd['fin'](hd, ops_, sps_)

        def fin_aug(og, r0, j, ss):
            def fin(hd, ops_, sps_):
                q0 = ss * Nq
                o0 = 64 - r0
                rc = tmpf.next()
                acopy(rc[r0:r0 + 64, 0:Nq], rc, ops_[o0:o0 + 64, 0:Nq], ops_)
                P.op('dve', lambda e: e.reciprocal(out=rc[r0:r0 + 64, 0:Nq], in_=rc[r0:r0 + 64, 0:Nq]),
                     reads=[rc], writes=[rc])
                P.op('dve', lambda e: e.tensor_tensor(out=og[r0:r0 + 64, j, q0:q0 + Nq], in0=ops_[r0:r0 + 64, 0:Nq],
                                                      in1=rc[r0:r0 + 64, 0:Nq], op=ALU.mult),
                     reads=[ops_, rc], writes=[og])
            return fin

        load_ctx('k_mn', 'v_m', 512, 4)
        P.op('sp', lambda e: e.dma_start(out=kpe[0:32, :], in_=scr['k_mp'][:, 0:S]), reads=['k_mp'], writes=[kpe], dma=True)
        vv4 = vv[:, :, :].rearrange("p t (h d) -> p t h d", h=8)
        P.op('pool', lambda e: e.memset(vaug[:], 1.0), writes=[vaug])
        P.op('dve', lambda e: e.tensor_copy(out=vaug[:, :, 0:8:2, 0:64], in_=vv4[:, :, 0:8:2, :]), reads=[vv], writes=[vaug])
        P.op('pool', lambda e: e.tensor_copy(out=vaug[:, :, 1:8:2, 64:128], in_=vv4[:, :, 1:8:2, :]), reads=[vv],
             writes=[vaug])
        for b in range(nq):
            t0 = b * 512
            load_q('q_mn', t0)
            P.op('sp', lambda e: e.dma_start(out=qpe[0:32, :, :], in_=scr['q_mp'][:, :, t0:t0 + 512]),
                 reads=['q_mp'], writes=[qpe], dma=True)
            og = stageb.next()
            heads = []
            for h in range(8):
                r0 = 64 * (h % 2)
                j = h // 2
                for ss in range(u.nss):
                    fin = fin_aug(og, r0, j, ss)
                    heads.append(dict(
                        aug=h,
                        ss=ss, k_fn=(lambda kt, j=j: kT[:, j, kt * 128:(kt + 1) * 128]),
                        q_fn=(lambda q0_, h=h, j=j: qT[:, j, h % 2, q0_:q0_ + Nq]),
                        extra=(lambda kt: kpe[:, kt * 128:(kt + 1) * 128],
                               lambda q0_, h=h: qpe[:, h, q0_:q0_ + Nq]),
                        v_col0=128 * j, vM=128, scale=MLA_SCALE, fin=fin))
            run_heads(heads)
            P.op('sp', lambda e: e.dma_start(out=u.oTs[:, 0:4, t0:t0 + 512], in_=og[:]),
                 reads=[og], writes=[('oT', u.i)], dma=True)
        load_ctx('k_d', 'v_d', 512, 4)
        lam_init = 0.8 - 0.6 * math.exp(-0.3 * l)
        for b in range(nq):
            t0 = b * 512
            load_q('q_d', t0)
            og = stageb.next()
            heads = []
            for h in range(4):
                for ss in range(u.nss):
                    pair = {}
                    for m in range(2):
                        r0 = 64 * m

                        def fin(hd, ops_, sps_, og=og, h=h, ss=ss, m=m, pair=pair):
                            q0 = ss * Nq
                            rc, dd = tmpf.next(), tmpf.next()
                            P.op('dve', lambda e: e.reciprocal(out=rc[:, 0:Nq], in_=sps_[:, 0:Nq]),
                                 reads=[sps_], writes=[rc])
                            P.op('dve', lambda e: e.tensor_tensor(out=dd[:, 0:Nq], in0=ops_[:, 0:Nq], in1=rc[:, 0:Nq],
                                                                  op=ALU.mult), reads=[ops_, rc], writes=[dd])
                            if m == 0:
                                pair['d1'] = dd
                                return
                            d1, d2 = pair['d1'], dd
                            P.op('dve', lambda e: e.scalar_tensor_tensor(out=d1[:, 0:Nq], in0=d2[:, 0:Nq],
                                                                         scalar=lamneg[:, l:l + 1], in1=d1[:, 0:Nq],
                                                                         op0=ALU.mult, op1=ALU.add),
                                 reads=[d1, d2, lamneg], writes=[d1])
                            sq = tmpb.next()
                            P.op('pool', lambda e: e.tensor_tensor(out=sq[:, 0:Nq], in0=d1[:, 0:Nq], in1=d1[:, 0:Nq],
                                                                  op=ALU.mult), reads=[d1], writes=[sq])
                            ps = ring_g1.next()
                            mm_group(ps, ps[:, 0:Nq], [(ones[:, :], sq[:, 0:Nq])], reads=[ones, sq])
                            rr = rstd_of(ps, ps[:, 0:Nq], Nq, 128.0, mult=(1.0 - lam_init))
                            P.op('dve', lambda e: e.scalar_tensor_tensor(
                                out=og[:, h, q0:q0 + Nq], in0=d1[:, 0:Nq], scalar=cv(l, 15), in1=rr[:, 0:Nq],
                                op0=ALU.mult, op1=ALU.mult), reads=[d1, rr, colv], writes=[og])
                        heads.append(dict(
                            ss=ss, k_fn=(lambda kt, h=h: kT[:, h, kt * 128:(kt + 1) * 128]),
                            q_fn=(lambda q0_, m=m, h=h: qT[:, h, m, q0_:q0_ + Nq]),
                            extra=None, v_col0=128 * h, vM=128, scale=SCALE64, fin=fin))
            run_heads(heads)
            P.op('sp', lambda e: e.dma_start(out=u.oTs[:, 4:8, t0:t0 + 512], in_=og[:]),
                 reads=[og], writes=[('oT', u.i)], dma=True)
        load_ctx('k_g', 'v_g', 128, 1)
        P.op('sp', lambda e: e.dma_start(out=kT[0:64, 1, :], in_=scr['k_g'][64:128, 0, 0:S]), reads=['k_g'],
             writes=[kT], dma=True)
        P.op('sp', lambda e: e.dma_start(out=kT[64:128, 1, :], in_=scr['k_g'][0:64, 0, 0:S]), reads=['k_g'],
             writes=[kT], dma=True)
        P.op('pool', lambda e: e.memset(vaug[:, :, 0:4, :], 1.0), writes=[vaug])
        for kvh in range(2):
            for half in range(2):
                P.op('dve' if half == 0 else 'pool', lambda e: e.tensor_copy(
                    out=vaug[:, :, 2 * kvh + half, 64 * half:64 * half + 64], in_=vv[:, :, 64 * kvh:64 * kvh + 64]),
                    reads=[vv], writes=[vaug])
        for b in range(nq):
            t0 = b * 512
            load_q('q_g', t0)
            og = stageb.next()
            heads = []
            for h in range(8):
                r0 = 64 * (h % 2)
                j = h // 2
                k0 = 64 * (h // 4)
                for ss in range(u.nss):
                    fin = fin_aug(og, r0, j, ss)
                    heads.append(dict(
                        aug=2 * (h // 4) + (h % 2),
                        ss=ss,
                        k_fn=(lambda kt, k0=k0, r0=r0: kT[:, (0 if k0 == r0 else 1), kt * 128:(kt + 1) * 128]),
                        q_fn=(lambda q0_, h=h, j=j: qT[:, j, h % 2, q0_:q0_ + Nq]),
                        extra=None, v_col0=0, vM=128, scale=SCALE64, fin=fin))
            run_heads(heads)
            P.op('sp', lambda e: e.dma_start(out=u.oTs[:, 8:12, t0:t0 + 512], in_=og[:]),
                 reads=[og], writes=[('oT', u.i)], dma=True)

    def ssm(l):
        arena.top = 0
        BT = arena.alloc([128, 2, 16, 2, 128], BF16)
        CT = arena.alloc([128, 2, 16, 2, 128], BF16)
        NB0 = 32
        Ebr = arena.alloc([128, 2, 16, NB0], F32)
        Ebi = arena.alloc([128, 2, 16, NB0], F32)
        mark = arena.top
        A = arena.alloc([128, 3, 2, 16], F32)
        for d in range(2):
            P.op('sp', lambda e, d=d: e.dma_start(out=A[:, :, d, :], in_=I['ssm_a'][l, d].rearrange("a p s -> p a s")),
                 writes=[A], dma=True)
        W = arena.alloc([128, 16, 2, 16], F32)

        def wcol(i):
            return W[:, i, :, :]
        are, aim, ldt = A[:, 0, :, :], A[:, 1, :, :], A[:, 2, :, :]

        def dv(fn, rd=(), wrt=()):
            P.op('dve', fn, reads=[W, A, prm] + list(rd), writes=list(wrt))
        P.op('act', lambda e: e.activation(out=wcol(0), in_=ldt, func=AF.Exp), reads=[A], writes=[W])
        dv(lambda e: e.tensor_tensor(out=wcol(1), in0=are, in1=wcol(0), op=ALU.mult), wrt=[W])
        dv(lambda e: e.tensor_tensor(out=wcol(2), in0=aim, in1=wcol(0), op=ALU.mult), wrt=[W])
        P.op('act', lambda e: e.activation(out=prm[:, :, :, 0], in_=wcol(1), func=AF.Exp), reads=[W], writes=[prm])
        ki = arena.alloc([128, 2, 16], I32)

        def sin_of(dst_ap, dst_buf, src_ap, shift):
            dv(lambda e: e.tensor_scalar(out=wcol(3), in0=src_ap, scalar1=shift, scalar2=None, op0=ALU.add), wrt=[W])
            P.op('dve', lambda e: e.tensor_scalar(out=ki[:, :, :], in0=wcol(3), scalar1=1.0 / (2 * math.pi),
                                                  scalar2=None, op0=ALU.mult), reads=[W], writes=[ki])
            P.op('dve', lambda e: e.tensor_copy(out=wcol(4), in_=ki[:, :, :]), reads=[ki], writes=[W])
            dv(lambda e: e.scalar_tensor_tensor(out=wcol(5), in0=wcol(4), scalar=-2 * math.pi, in1=wcol(3),
                                                op0=ALU.mult, op1=ALU.add), wrt=[W])
            P.op('act', lambda e: e.activation(out=dst_ap, in_=wcol(5), func=AF.Sin), reads=[W],
                 writes=[dst_buf])
        sin_of(wcol(6), W, wcol(2), 0.0)
        sin_of(wcol(7), W, wcol(2), math.pi / 2)
        dv(lambda e: e.tensor_tensor(out=wcol(8), in0=prm[:, :, :, 0], in1=wcol(7), op=ALU.mult), wrt=[W])
        dv(lambda e: e.tensor_tensor(out=wcol(9), in0=prm[:, :, :, 0], in1=wcol(6), op=ALU.mult), wrt=[W])
        dv(lambda e: e.tensor_scalar(out=wcol(10), in0=wcol(8), scalar1=-1.0, scalar2=None, op0=ALU.add), wrt=[W])
        dv(lambda e: e.tensor_tensor(out=wcol(11), in0=are, in1=are, op=ALU.mult), wrt=[W])
        dv(lambda e: e.tensor_tensor(out=wcol(12), in0=aim, in1=aim, op=ALU.mult), wrt=[W])
        dv(lambda e: e.tensor_tensor(out=wcol(11), in0=wcol(11), in1=wcol(12), op=ALU.add), wrt=[W])
        dv(lambda e: e.reciprocal(out=wcol(11), in_=wcol(11)), wrt=[W])
        dv(lambda e: e.tensor_tensor(out=wcol(12), in0=wcol(10), in1=are, op=ALU.mult), wrt=[W])
        dv(lambda e: e.tensor_tensor(out=wcol(13), in0=wcol(9), in1=aim, op=ALU.mult), wrt=[W])
        dv(lambda e: e.tensor_tensor(out=wcol(12), in0=wcol(12), in1=wcol(13), op=ALU.add), wrt=[W])
        dv(lambda e: e.tensor_tensor(out=prm[:, :, :, 1], in0=wcol(12), in1=wcol(11), op=ALU.mult), wrt=[prm])
        dv(lambda e: e.tensor_tensor(out=wcol(12), in0=wcol(9), in1=are, op=ALU.mult), wrt=[W])
        dv(lambda e: e.tensor_tensor(out=wcol(13), in0=wcol(10), in1=aim, op=ALU.mult), wrt=[W])
        dv(lambda e: e.tensor_tensor(out=wcol(12), in0=wcol(12), in1=wcol(13), op=ALU.subtract), wrt=[W])
        dv(lambda e: e.tensor_tensor(out=prm[:, :, :, 2], in0=wcol(12), in1=wcol(11), op=ALU.mult), wrt=[prm])
        dv(lambda e: e.tensor_scalar(out=prm[:, :, :, 3], in0=prm[:, :, :, 2], scalar1=-1.0, scalar2=None,
                                     op0=ALU.mult), wrt=[prm])
        s0 = arena.alloc([128, 2, 16, 2], F32)
        for d in range(2):
            P.op('sp', lambda e, d=d: e.dma_start(out=s0[:, d, :, :], in_=I['st_ssm'][l, d].rearrange("s p r -> p s r")),
                 writes=[s0], dma=True)
        dv(lambda e: e.tensor_tensor(out=wcol(12), in0=wcol(7), in1=s0[:, :, :, 0], op=ALU.mult), rd=[s0], wrt=[W])
        dv(lambda e: e.tensor_tensor(out=wcol(13), in0=wcol(6), in1=s0[:, :, :, 1], op=ALU.mult), rd=[s0], wrt=[W])
        dv(lambda e: e.tensor_tensor(out=prm[:, :, :, 4], in0=wcol(12), in1=wcol(13), op=ALU.subtract), wrt=[prm])
        dv(lambda e: e.tensor_tensor(out=wcol(12), in0=wcol(7), in1=s0[:, :, :, 1], op=ALU.mult), rd=[s0], wrt=[W])
        dv(lambda e: e.tensor_tensor(out=wcol(13), in0=wcol(6), in1=s0[:, :, :, 0], op=ALU.mult), rd=[s0], wrt=[W])
        dv(lambda e: e.tensor_tensor(out=prm[:, :, :, 5], in0=wcol(12), in1=wcol(13), op=ALU.add), wrt=[prm])
        P.op('dve', lambda e: e.tensor_copy(out=pwc[:, :, :, 0], in_=wcol(7)), reads=[W], writes=[pwc])
        P.op('dve', lambda e: e.tensor_scalar(out=pws[:, :, :, 0], in0=wcol(6), scalar1=-1.0, scalar2=None,
                                              op0=ALU.mult), reads=[W], writes=[pws])
        for k in range(1, NPOW):
            dv(lambda e, k=k: e.tensor_tensor(out=wcol(12), in0=pwc[:, :, :, k - 1], in1=pwc[:, :, :, k - 1],
                                              op=ALU.mult), rd=[pwc], wrt=[W])
            dv(lambda e, k=k: e.tensor_tensor(out=wcol(13), in0=pws[:, :, :, k - 1], in1=pws[:, :, :, k - 1],
                                              op=ALU.mult), rd=[pws], wrt=[W])
            dv(lambda e, k=k: e.tensor_tensor(out=wcol(14), in0=pwc[:, :, :, k - 1], in1=pws[:, :, :, k - 1],
                                              op=ALU.mult), rd=[pwc, pws], wrt=[W])
            dv(lambda e, k=k: e.tensor_tensor(out=pwc[:, :, :, k], in0=wcol(12), in1=wcol(13), op=ALU.subtract),
               wrt=[pwc])
            dv(lambda e, k=k: e.tensor_scalar(out=pws[:, :, :, k], in0=wcol(14), scalar1=2.0, scalar2=None,
                                              op0=ALU.mult), wrt=[pws])
        M = [arena.alloc([128, 2, 16, NB0 // 2], F32) for _ in range(4)]
        P.op('pool', lambda e: e.memset(Ebr[:, :, :, 0:1], 1.0), writes=[Ebr])
        P.op('pool', lambda e: e.memset(Ebi[:, :, :, 0:1], 0.0), writes=[Ebi])
        kk = 0
        while (1 << kk) < NB0:
            n = 1 << kk
            cb = pwc[:, :, :, kk:kk + 1].to_broadcast([128, 2, 16, n])
            sbb = pws[:, :, :, kk:kk + 1].to_broadcast([128, 2, 16, n])
            er0, ei0 = Ebr[:, :, :, 0:n], Ebi[:, :, :, 0:n]
            m = [M[i][:, :, :, 0:n] for i in range(4)]
            P.op('dve', lambda e: e.tensor_tensor(out=m[0], in0=er0, in1=cb, op=ALU.mult), reads=[Ebr, pwc], writes=[M[0]])
            P.op('pool', lambda e: e.tensor_tensor(out=m[1], in0=ei0, in1=sbb, op=ALU.mult), reads=[Ebi, pws], writes=[M[1]])
            P.op('dve', lambda e: e.tensor_tensor(out=m[2], in0=ei0, in1=cb, op=ALU.mult), reads=[Ebi, pwc], writes=[M[2]])
            P.op('pool', lambda e: e.tensor_tensor(out=m[3], in0=er0, in1=sbb, op=ALU.mult), reads=[Ebr, pws], writes=[M[3]])
            P.op('dve', lambda e: e.tensor_tensor(out=Ebr[:, :, :, n:2 * n], in0=m[0], in1=m[1], op=ALU.subtract),
                 reads=[M[0], M[1]], writes=[Ebr])
            P.op('dve', lambda e: e.tensor_tensor(out=Ebi[:, :, :, n:2 * n], in0=m[2], in1=m[3], op=ALU.add),
                 reads=[M[2], M[3]], writes=[Ebi])
            kk += 1
        KK0 = kk
        Yb = arena.alloc([128, 2, 16, 128], F32)
        Yw = arena.alloc([128, 2, 128], F32)
        X = arena.alloc([128, 2, 4, 512], F32)
        for d in range(2):
            P.op('pool', lambda e: e.memset(Yb[:], 0.0), writes=[Yb])
            P.op('pool', lambda e: e.memset(X[:], 0.0), writes=[X])
            for ri in range(2):
                bsrc = I['ssm_b'][l, d, ri].rearrange("(ct k gl) p n -> k gl p ct n", ct=4, k=4, gl=2)
                csrc = I['ssm_c'][l, d, ri].rearrange("(ct k gl) n p -> k gl n ct p", ct=4, k=4, gl=2)
                for k in range(4):
                    for gl in range(2):
                        c0 = 32 * k + 16 * gl
                        P.op('sp', lambda e, ri=ri, k=k, gl=gl, c0=c0, bsrc=bsrc: e.dma_start(
                            out=Yb[64 * gl:64 * gl + 64, ri, k:16:4, c0:c0 + 16], in_=bsrc[k, gl]),
                            writes=[Yb], dma=True)
                        x0 = 64 * (2 * k + gl)
                        P.op('sp', lambda e, ri=ri, c0=c0, x0=x0, csrc=csrc, k=k, gl=gl: e.dma_start(
                            out=X[c0:c0 + 16, ri, :, x0:x0 + 64], in_=csrc[k, gl]), writes=[X], dma=True)
            for s_ in range(16):
                cr, ci, nci = prm[:, d, s_, 1:2], prm[:, d, s_, 2:3], prm[:, d, s_, 3:4]
                P.op('dve', lambda e, s_=s_, ci=ci: e.tensor_scalar(out=Yw[:, 0, :], in0=Yb[:, 1, s_, :], scalar1=ci,
                                                                    scalar2=None, op0=ALU.mult),
                     reads=[Yb, prm], writes=[Yw])
                P.op('dve', lambda e, s_=s_, cr=cr: e.scalar_tensor_tensor(out=Yw[:, 0, :], in0=Yb[:, 0, s_, :],
                                                                           scalar=cr, in1=Yw[:, 0, :], op0=ALU.mult,
                                                                           op1=ALU.subtract),
                     reads=[Yb, prm, Yw], writes=[Yw])
                P.op('dve', lambda e, s_=s_, ci=ci: e.tensor_scalar(out=Yw[:, 1, :], in0=Yb[:, 0, s_, :], scalar1=ci,
                                                                    scalar2=None, op0=ALU.mult),
                     reads=[Yb, prm], writes=[Yw])
                P.op('dve', lambda e, s_=s_, cr=cr: e.scalar_tensor_tensor(out=Yw[:, 1, :], in0=Yb[:, 1, s_, :],
                                                                           scalar=cr, in1=Yw[:, 1, :], op0=ALU.mult,
                                                                           op1=ALU.add),
                     reads=[Yb, prm, Yw], writes=[Yw])
                for ri in range(2):
                    transpose_to(BT[:, d, s_, ri, :], BT, Yw[:, ri, :], Yw, 128, 128)
                    ct, k = s_ // 4, s_ % 4
                    transpose_to(CT[:, d, s_, ri, :], CT, X[:, ri, ct, 128 * k:128 * k + 128], X, 128, 128,
                                 scale=(1.0 if ri == 0 else -1.0))
        arena.top = mark
        Eres = [arena.alloc([128, TS], F32) for _ in range(2)]
        Eims = [arena.alloc([128, TS], F32) for _ in range(2)]
        bre = arena.alloc([128, TS], F32)
        bim = arena.alloc([128, TS], F32)
        uct = [arena.alloc([128, u.T], BF16) for u in units]
        yg = [arena.alloc([128, 4, u.T], BF16) for u in units]
        stS = arena.alloc([128, 2, 128], F32)
        ring2 = Ring(psb[0:2])
        ybank = {(0, 0): psb[6], (1, 0): psb[7], (2, 0): psb[2], (2, 1): psb[3], (2, 2): psb[4], (2, 3): psb[5]}

        def pk(buf, lo, hi):
            return [buf.key[i] for i in range(lo // PAGE, (hi - 1) // PAGE + 1)]
        it = 0
        for ct in range(4):
            for ui, u in enumerate(units):
                P.op('sp', lambda e: e.dma_start(out=uct[ui][:, :], in_=u.uTs[:, ct, :]),
                     reads=[('uT', u.i)], writes=[uct[ui]], dma=True)
            for d in range(2):
                for k in range(4):
                    s_ = 4 * ct + k
                    first_dk = (d == 0 and k == 0)
                    last_dk = (d == 1 and k == 3)
                    rho = prm[:, d, s_, 0:1]
                    Ere, Eim = Eres[it % 2], Eims[it % 2]
                    it += 1
                    acopy(Ere[:, 0:NB0], pk(Ere, 0, NB0), Ebr[:, d, s_, :], Ebr)
                    acopy(Eim[:, 0:NB0], pk(Eim, 0, NB0), Ebi[:, d, s_, :], Ebi)
                    for kk in range(KK0, NPOW):
                        n = 1 << kk
                        c_, sn = pwc[:, d, s_, kk:kk + 1], pws[:, d, s_, kk:kk + 1]
                        for q0 in range(0, n, 512):
                            q1 = min(n, q0 + 512)
                            w = q1 - q0
                            t1, t2 = tmpf.next(), tmpf.next()
                            P.op('act', lambda e: e.activation(out=t1[:, 0:w], in_=Eim[:, q0:q1], func=AF.Identity, scale=sn),
                                 reads=pk(Eim, q0, q1) + [pws], writes=[t1])
                            P.op('act', lambda e: e.activation(out=t2[:, 0:w], in_=Ere[:, q0:q1], func=AF.Identity, scale=sn),
                                 reads=pk(Ere, q0, q1) + [pws], writes=[t2])
                            P.op('dve', lambda e: e.scalar_tensor_tensor(
                                out=Ere[:, n + q0:n + q1], in0=Ere[:, q0:q1], scalar=c_, in1=t1[:, 0:w], op0=ALU.mult,
                                op1=ALU.subtract), reads=pk(Ere, q0, q1) + [pwc, t1], writes=pk(Ere, n + q0, n + q1))
                            P.op('dve', lambda e: e.scalar_tensor_tensor(
                                out=Eim[:, n + q0:n + q1], in0=Eim[:, q0:q1], scalar=c_, in1=t2[:, 0:w], op0=ALU.mult,
                                op1=ALU.add), reads=pk(Eim, q0, q1) + [pwc, t2], writes=pk(Eim, n + q0, n + q1))
                    for ui, u in enumerate(units):
                        Ts = u.Ts

                        def Ev(E, a, b_):
                            if d == 0:
                                return E[:, a:b_], pk(E, a, b_)
                            return E[:, 0:Ts][:, ::-1][:, a:b_], pk(E, Ts - b_, Ts - a)
                        for b in range(u.nblk):
                            psr, psi = ring2.next(), ring2.next()
                            mm_group(psr, psr[:, :], [(BT[:, d, s_, 0, :], uct[ui][:, b * 512:(b + 1) * 512])],
                                     reads=[BT, uct[ui]])
                            mm_group(psi, psi[:, :], [(BT[:, d, s_, 1, :], uct[ui][:, b * 512:(b + 1) * 512])],
                                     reads=[BT, uct[ui]])
                            ar, ai = tmpf.next(), tmpf.next()
                            acopy(ar[:, :], ar, psr[:, :], psr)
                            acopy(ai[:, :], ai, psi[:, :], psi)
                            for ss in range(u.nss):
                                w = 512 // u.nss
                                c0 = b * 512 + ss * w
                                a = c0 % Ts
                                (er, erk), (ei, eik) = Ev(Ere, a, a + w), Ev(Eim, a, a + w)
                                bk_r, bk_i = pk(bre, c0, c0 + w), pk(bim, c0, c0 + w)
                                p0 = ss * w
                                t1, t2, t3, t4 = tmpf.next(), tmpf.next(), tmpf.next(), tmpf.next()
                                P.op('dve', lambda e: e.tensor_tensor(out=t1[:, 0:w], in0=ar[:, p0:p0 + w], in1=er,
                                                                      op=ALU.mult), reads=[ar] + erk, writes=[t1])
                                P.op('dve', lambda e: e.tensor_tensor(out=t2[:, 0:w], in0=ai[:, p0:p0 + w], in1=ei,
                                                                      op=ALU.mult), reads=[ai] + eik, writes=[t2])
                                P.op('dve', lambda e: e.tensor_tensor(out=bre[:, c0:c0 + w], in0=t1[:, 0:w],
                                                                      in1=t2[:, 0:w], op=ALU.subtract),
                                     reads=[t1, t2], writes=bk_r)
                                P.op('pool', lambda e: e.tensor_tensor(out=t3[:, 0:w], in0=ar[:, p0:p0 + w], in1=ei,
                                                                       op=ALU.mult), reads=[ar] + eik, writes=[t3])
                                P.op('pool', lambda e: e.tensor_tensor(out=t4[:, 0:w], in0=ai[:, p0:p0 + w], in1=er,
                                                                       op=ALU.mult), reads=[ai] + erk, writes=[t4])
                                P.op('dve', lambda e: e.tensor_tensor(out=bim[:, c0:c0 + w], in0=t3[:, 0:w],
                                                                      in1=t4[:, 0:w], op=ALU.add),
                                     reads=[t3, t4], writes=bk_i)
                        for sq_ in range(u.T // Ts):
                            c0 = sq_ * Ts
                            for bb, col in ((bre, 4), (bim, 5)):
                                v = bb[:, c0:c0 + Ts]
                                if d == 1:
                                    v = v[:, ::-1]
                                init = prm[:, d, s_, col:col + 1] if u.sample else 0.0
                                P.op('dve', lambda e: e.tensor_tensor_scan(
                                    out=v, data0=rho.to_broadcast([128, Ts]), data1=v, initial=init, op0=ALU.mult,
                                    op1=ALU.add), reads=pk(bb, c0, c0 + Ts) + [prm], writes=pk(bb, c0, c0 + Ts))
                            if not u.sample:
                                gcol = (c0 + Ts - 1) if d == 0 else c0
                                seq = 2 * ui + sq_
                                oc = (seq * 2 + d) * 16 + s_
                                eR, eI = Ere[:, Ts - 1:Ts], Eim[:, Ts - 1:Ts]
                                ekr, eki = pk(Ere, Ts - 1, Ts), pk(Eim, Ts - 1, Ts)
                                gr, gi = bre[:, gcol:gcol + 1], bim[:, gcol:gcol + 1]
                                gkr, gki = pk(bre, gcol, gcol + 1), pk(bim, gcol, gcol + 1)
                                t1 = tmpf.next()
                                P.op('dve', lambda e: e.tensor_tensor(out=t1[:, 0:1], in0=gi, in1=eI, op=ALU.mult),
                                     reads=gki + eki, writes=[t1])
                                P.op('dve', lambda e: e.scalar_tensor_tensor(
                                    out=stS[:, 0, oc:oc + 1], in0=gr, scalar=eR, in1=t1[:, 0:1], op0=ALU.mult,
                                    op1=ALU.add), reads=gkr + ekr + [t1], writes=[stS])
                                t2 = tmpf.next()
                                P.op('dve', lambda e: e.tensor_tensor(out=t2[:, 0:1], in0=gr, in1=eI, op=ALU.mult),
                                     reads=gkr + eki, writes=[t2])
                                P.op('dve', lambda e: e.scalar_tensor_tensor(
                                    out=stS[:, 1, oc:oc + 1], in0=gi, scalar=eR, in1=t2[:, 0:1], op0=ALU.mult,
                                    op1=ALU.subtract), reads=gki + ekr + [t2], writes=[stS])
                        for b in range(u.nblk):
                            hr, hi = tmpb.next(), tmpb.next()
                            for ss in range(u.nss):
                                w = 512 // u.nss
                                c0 = b * 512 + ss * w
                                a = c0 % Ts
                                (er, erk), (ei, eik) = Ev(Ere, a, a + w), Ev(Eim, a, a + w)
                                bk_r, bk_i = pk(bre, c0, c0 + w), pk(bim, c0, c0 + w)
                                p0 = ss * w
                                t1, t2, t3, t4 = tmpf.next(), tmpf.next(), tmpf.next(), tmpf.next()
                                P.op('pool', lambda e: e.tensor_tensor(out=t1[:, 0:w], in0=bre[:, c0:c0 + w], in1=er,
                                                                       op=ALU.mult), reads=bk_r + erk, writes=[t1])
                                P.op('dve', lambda e: e.tensor_tensor(out=t2[:, 0:w], in0=bim[:, c0:c0 + w], in1=ei,
                                                                      op=ALU.mult), reads=bk_i + eik, writes=[t2])
                                P.op('dve', lambda e: e.tensor_tensor(out=hr[:, p0:p0 + w], in0=t1[:, 0:w],
                                                                      in1=t2[:, 0:w], op=ALU.add),
                                     reads=[t1, t2], writes=[hr])
                                P.op('pool', lambda e: e.tensor_tensor(out=t3[:, 0:w], in0=bim[:, c0:c0 + w], in1=er,
                                                                       op=ALU.mult), reads=bk_i + erk, writes=[t3])
                                P.op('dve', lambda e: e.tensor_tensor(out=t4[:, 0:w], in0=bre[:, c0:c0 + w], in1=ei,
                                                                      op=ALU.mult), reads=bk_r + eik, writes=[t4])
                                P.op('dve', lambda e: e.tensor_tensor(out=hi[:, p0:p0 + w], in0=t3[:, 0:w],
                                                                      in1=t4[:, 0:w], op=ALU.subtract),
                                     reads=[t3, t4], writes=[hi])
                            psy = ybank[(ui, b)]
                            P.op('pe', lambda e: e.matmul(psy[:, :], lhsT=CT[:, d, s_, 0, :], rhs=hr[:, :],
                                                          start=first_dk, stop=False),
                                 reads=[CT, hr], writes=[psy], milestone=False)
                            P.op('pe', lambda e: e.matmul(psy[:, :], lhsT=CT[:, d, s_, 1, :], rhs=hi[:, :],
                                                          start=False, stop=last_dk),
                                 reads=[CT, hi], writes=[psy])
            for ui, u in enumerate(units):
                for b in range(u.nblk):
                    sl = slice(b * 512, (b + 1) * 512)
                    psy = ybank[(ui, b)]
                    t = tmpf.next()
                    P.op('dve', lambda e: e.scalar_tensor_tensor(
                        out=t[:, :], in0=uct[ui][:, sl], scalar=cv(l, 16 + ct), in1=psy[:, :], op0=ALU.mult,
                        op1=ALU.add), reads=[uct[ui], colv, psy], writes=[t])
                    t2 = tmpf.next()
                    P.op('pool', lambda e: e.tensor_tensor(out=t2[:, :], in0=t[:, :], in1=t[:, :], op=ALU.mult),
                         reads=[t], writes=[t2])
                    P.op('dve', lambda e: e.tensor_scalar(out=t2[:, :], in0=t2[:, :], scalar1=0.044715, scalar2=1.0,
                                                          op0=ALU.mult, op1=ALU.add), reads=[t2], writes=[t2])
                    P.op('dve', lambda e: e.tensor_tensor(out=t2[:, :], in0=t2[:, :], in1=t[:, :], op=ALU.mult),
                         reads=[t, t2], writes=[t2])
                    P.op('act', lambda e: e.activation(out=t2[:, :], in_=t2[:, :], func=AF.Sigmoid,
                                                       scale=1.5957691216057308), reads=[t2], writes=[t2])
                    P.op('dve', lambda e: e.tensor_tensor(out=yg[ui][:, ct, sl], in0=t[:, :], in1=t2[:, :], op=ALU.mult),
                         reads=[t, t2], writes=[yg[ui]])
        wglu = arena.alloc([128, 4, 512], BF16)
        P.op('pool', lambda e: e.dma_start(out=wglu[:], in_=I['w_glu'][l].rearrange("(k p) c -> p k c", p=128)),
             writes=[wglu], dma=True)
        for ui, u in enumerate(units):
            for b in range(u.nblk):
                sl = slice(b * 512, (b + 1) * 512)
                og = stageb.next()
                for jj in range(4):
                    ps = ring_all.next()
                    mm_group(ps, ps[:, :], [(wglu[:, k, 128 * jj:128 * jj + 128], yg[ui][:, k, sl]) for k in range(4)],
                             reads=[wglu, yg[ui]])
                    sg = tmpf.next()
                    P.op('act', lambda e, sg=sg, ps=ps, jj=jj: e.activation(out=sg[:, :], in_=ps[:, :], func=AF.Sigmoid,
                                                                            bias=cv(l, 20 + jj), scale=1.0),
                         reads=[ps, colv], writes=[sg])
                    P.op('dve', lambda e, sg=sg, og=og, jj=jj, ui=ui, sl=sl: e.tensor_tensor(
                        out=og[:, jj, :], in0=yg[ui][:, jj, sl], in1=sg[:, :], op=ALU.mult),
                        reads=[yg[ui], sg], writes=[og])
                P.op('sp', lambda e, og=og, u=u, b=b: e.dma_start(out=u.oTs[:, 12:16, b * 512:(b + 1) * 512], in_=og[:]),
                     reads=[og], writes=[('oT', u.i)], dma=True)
        zz = arena.alloc([128, 128, 2], F32)
        for ri in range(2):
            transpose_to(zz[:, :, ri], zz, stS[:, ri, :], stS, 128, 128)
        P.op('sp', lambda e: e.dma_start(out=O['n_ssm'][l], in_=zz[:].rearrange("p a b -> p (a b)")),
             reads=[zz], dma=True)

    def stage_C_unit(l, u):
        arena.top = 0
        pth = 1 if u.sample else 0
        w_in = I['w_in'][l]
        wring = Ring([arena.alloc([128, 8, 512], BF16) for _ in range(5)])
        xT = arena.alloc([128, 8, 512], F32)
        hT = arena.alloc([128, 8, 512], BF16)
        oT = arena.alloc([128, 16, 512], BF16)
        acc = arena.alloc([128, 8, 512], F32)
        mg = arena.alloc([128, 8, 512], BF16)
        ytok = Ring([arena.alloc([128, 1024], F32) for _ in range(2)])
        ring_g = Ring(psb[0:4])
        ring_acc = Ring(psb[4:8])
        RG[0] = ring_g

        def load_w(src2d, rows, c0, ncols):
            wb = wring.next()
            kk = rows // 128
            src = src2d[:, c0:c0 + ncols].rearrange("(k p) c -> p k c", p=128)
            P.op('pool', lambda e: e.dma_start(out=wb[:, 0:kk, 0:ncols], in_=src), writes=[wb], dma=True)
            return wb

        own_mode = (u.sample and l == DEPTH - 1)
        for b in range(1 if own_mode else u.nblk):
            t0 = b * 512
            if own_mode:
                P.op('sp', lambda e: e.dma_start(out=hT[:], in_=u.hTo[:]), reads=[('hTo', u.i)], writes=[hT], dma=True)
                P.op('sp', lambda e: e.dma_start(out=xT[:], in_=u.xo[:]), reads=[('xo', u.i)], writes=[xT], dma=True)
                P.op('sp', lambda e: e.dma_start(out=oT[:, 0:12, :], in_=u.oTs[:, 0:12, 0:512]), reads=[('oT', u.i)],
                     writes=[oT], dma=True)
                for s_ in range(4):
                    od = stageb.next()
                    P.op('sp', lambda e: e.dma_start(out=od[:], in_=u.oTs[:, 12:16, 512 * s_:512 * s_ + 512]),
                         reads=[('oT', u.i)], writes=[od], dma=True)
                    for jj in range(4):
                        if s_ == 0:
                            P.op('dve', lambda e: e.tensor_scalar(out=oT[:, 12 + jj, :], in0=od[:, jj, :],
                                                                  scalar1=mask4[:, 0:1], scalar2=None, op0=ALU.mult),
                                 reads=[od, mask4], writes=[oT])
                        else:
                            P.op('dve', lambda e: e.scalar_tensor_tensor(out=oT[:, 12 + jj, :], in0=od[:, jj, :],
                                                                         scalar=mask4[:, s_:s_ + 1], in1=oT[:, 12 + jj, :],
                                                                         op0=ALU.mult, op1=ALU.add),
                                 reads=[od, mask4, oT], writes=[oT])
            else:
                P.op('sp', lambda e, t0=t0: e.dma_start(out=hT[:], in_=u.hTs[:, :, t0:t0 + 512]), reads=[('hT', u.i)],
                     writes=[hT], dma=True)
                P.op('sp', lambda e, t0=t0: e.dma_start(out=oT[:], in_=u.oTs[:, :, t0:t0 + 512]), reads=[('oT', u.i)],
                     writes=[oT], dma=True)
                P.op('sp', lambda e, t0=t0: e.dma_start(out=xT[:], in_=u.xres[:, :, t0:t0 + 512]),
                     reads=[('xres', u.i)], writes=[xT], dma=True)
            for i in range(4):
                wb = load_w(w_in, D, C_GATE + 512 * i, 512)
                for jj in range(4):
                    j = 4 * i + jj
                    ps = ring_g.next()
                    mm_group(ps, ps[:, :], [(wb[:, k, 128 * jj:128 * jj + 128], hT[:, k, :]) for k in range(8)],
                             reads=[wb, hT])
                    sg = tmpf.next()
                    P.op('act', lambda e, sg=sg, ps=ps: e.activation(out=sg[:, :], in_=ps[:, :], func=AF.Sigmoid),
                         reads=[ps], writes=[sg])
                    P.op('dve', lambda e, sg=sg, ps=ps: e.tensor_tensor(out=sg[:, :], in0=sg[:, :], in1=ps[:, :],
                                                                       op=ALU.mult), reads=[ps, sg], writes=[sg])
                    P.op('pool', lambda e, sg=sg, j=j: e.tensor_tensor(out=oT[:, j, :], in0=oT[:, j, :], in1=sg[:, :],
                                                                      op=ALU.mult), reads=[oT, sg], writes=[oT])
            for n in range(4):
                for h in range(2):
                    wbo = load_w(I['w_bo'][l, n], 512, 512 * h, 512)
                    wmg = load_w(w_in, D, C_MERGE + 1024 * n + 512 * h, 512)
                    for jj in range(4):
                        jd = 4 * h + jj
                        ps1 = ring_acc.next()
                        mm_group(ps1, ps1[:, :], [(wbo[:, k, 128 * jj:128 * jj + 128], oT[:, 4 * n + k, :])
                                                  for k in range(4)], reads=[wbo, oT])
                        ps2 = ring_g.next()
                        mm_group(ps2, ps2[:, :], [(wmg[:, k, 128 * jj:128 * jj + 128], hT[:, k, :]) for k in range(8)],
                                 reads=[wmg, hT])
                        sg = tmpf.next()
                        P.op('act', lambda e, sg=sg, ps2=ps2: e.activation(out=sg[:, :], in_=ps2[:, :], func=AF.Sigmoid),
                             reads=[ps2], writes=[sg])
                        if n == 0:
                            P.op('dve', lambda e, sg=sg, ps1=ps1, jd=jd: e.tensor_tensor(
                                out=acc[:, jd, :], in0=ps1[:, :], in1=sg[:, :], op=ALU.mult),
                                reads=[ps1, sg], writes=[acc])
                        else:
                            P.op('dve', lambda e, sg=sg, ps1=ps1: e.tensor_tensor(
                                out=sg[:, :], in0=ps1[:, :], in1=sg[:, :], op=ALU.mult), reads=[ps1, sg], writes=[sg])
                            P.op('pool', lambda e, sg=sg, jd=jd: e.tensor_tensor(
                                out=acc[:, jd, :], in0=acc[:, jd, :], in1=sg[:, :], op=ALU.add),
                                reads=[acc, sg], writes=[acc])
            for k in range(8):
                acopy(mg[:, k, :], mg, acc[:, k, :], acc)
            for h in range(2):
                wb = load_w(I['w_out'][l], D, 512 * h, 512)
                for jj in range(4):
                    jd = 4 * h + jj
                    ps = ring_g.next()
                    mm_group(ps, ps[:, :], [(wb[:, k, 128 * jj:128 * jj + 128], mg[:, k, :]) for k in range(8)],
                             reads=[wb, mg])
                    P.op('dve', lambda e, ps=ps, jd=jd: e.scalar_tensor_tensor(
                        out=xT[:, jd, :], in0=ps[:, :], scalar=modv[:, l, 16 + jd:17 + jd, pth], in1=xT[:, jd, :],
                        op0=ALU.mult, op1=ALU.add), reads=[ps, modv, xT], writes=[xT])
            if l < DEPTH - 1:
                P.op('sp', lambda e, t0=t0: e.dma_start(out=u.xres[:, :, t0:t0 + 512], in_=xT[:]),
                     reads=[xT], writes=[('xres', u.i)], dma=True)
            else:
                sqs = []
                for k in range(8):
                    s = tmpb.next()
                    P.op('act', lambda e, s=s, k=k: e.activation(out=s[:, :], in_=xT[:, k, :], func=AF.Square),
                         reads=[xT], writes=[s])
                    sqs.append(s)
                r = rstd_from_sq(sqs, float(D), ones[:, :])
                for k in range(8):
                    P.op('dve', lambda e, k=k, r=r: e.scalar_tensor_tensor(out=xT[:, k, :], in0=xT[:, k, :],
                                                                          scalar=cv(0, 24 + k), in1=r[:, :],
                                                                          op0=ALU.mult, op1=ALU.mult),
                         reads=[xT, colv, r], writes=[xT])
                for tt in range(4):
                    yt = ytok.next()
                    for k in range(8):
                        transpose_to(yt[:, 128 * k:128 * k + 128], yt, xT[:, k, tt * 128:(tt + 1) * 128], xT, 128, 128)
                    r0 = t0 + tt * 128
                    P.op('sp', lambda e, yt=yt, r0=r0: e.dma_start(out=u.yout[r0:r0 + 128, :], in_=yt[:, :]),
                         reads=[yt], dma=True)
        RG[0] = Ring(psb[0:2])

    phases = []
    for l in range(DEPTH):
        phases.append(('mod%d' % l, lambda l=l: modulation(l)))
        for u in units:
            phases.append(('A%du%d' % (l, u.i), lambda l=l, u=u: stage_A_unit(l, u)))
            phases.append(('B%du%d' % (l, u.i), lambda l=l, u=u: attention(l, u)))
        phases.append(('S%d' % l, lambda l=l: ssm(l)))
        for u in units:
            phases.append(('C%du%d' % (l, u.i), lambda l=l, u=u: stage_C_unit(l, u)))
    for name, fn in phases:
        fn()
        if stop is not None and name == stop:
            break
    P.finish()
    P.emit(st)
    st.close()
    return nc, P, arena


def _rope_tables(rot_dim):
    rows = TS // 64
    row = np.repeat(np.arange(rows, dtype=np.float32), 64)
    col = np.tile(np.arange(64, dtype=np.float32), rows)
    quarter = rot_dim // 4
    inv = (np.float32(10000.0) ** (-np.arange(quarter, dtype=np.float32) / np.float32(quarter))).astype(np.float32)
    ang = np.concatenate([row[:, None] * inv, col[:, None] * inv], axis=-1).astype(np.float32)
    return np.cos(ang).astype(np.float32), np.sin(ang).astype(np.float32)


def _host_prep(inp):
    f = np.float32
    shared = {}
    shared['w_mod'] = np.ascontiguousarray(inp['w_mod'], dtype=f)
    w_in = np.ascontiguousarray(inp['w_in'], dtype=f)
    shared['w_in'] = w_in

    def sw_idx(base, nheads, hd):
        idx = []
        for h in range(nheads):
            for d in range(hd):
                idx.append(base + h * hd + (d + hd // 2) % hd)
        return idx
    idx = sw_idx(C_KPE, 1, 32) + sw_idx(C_DQ, 8, 64) + sw_idx(C_DK, 8, 64) + sw_idx(C_GQ, 8, 64) + sw_idx(C_GK, 2, 64)
    shared['w_in_sw'] = np.ascontiguousarray(w_in[:, :, idx])
    wq = np.asarray(inp['w_mla_q_b'], dtype=f)
    cols = []
    for j in range(4):
        for hh in range(2):
            cols += [(2 * j + hh) * 96 + d for d in range(64)]
    for h in range(8):
        cols += [h * 96 + 64 + d for d in range(32)]
    for h in range(8):
        cols += [h * 96 + 64 + (d + 16) % 32 for d in range(32)]
    shared['w_qb'] = np.ascontiguousarray(wq[:, :, cols])
    wkv = np.asarray(inp['w_mla_kv_b'], dtype=f)
    cols = [h * 128 + d for h in range(8) for d in range(64)] + [h * 128 + 64 + e for h in range(8) for e in range(64)]
    shared['w_kvb'] = np.ascontiguousarray(wkv[:, :, cols])
    shared['lam_rows'] = np.ascontiguousarray(np.stack([inp['diff_lq1'], inp['diff_lk1'], inp['diff_lq2'],
                                                        inp['diff_lk2']], axis=1), dtype=f)
    shared['w_glu'] = np.ascontiguousarray(inp['ssm_glu_w'], dtype=f)
    shared['w_bo'] = np.ascontiguousarray(inp['w_branch_out'], dtype=f)
    shared['w_out'] = np.ascontiguousarray(inp['w_out'], dtype=f)
    shared['ident'] = np.eye(128, dtype=f)
    blk = np.zeros((128, 128), f)
    blk[:64, :64] = 1
    blk[64:, 64:] = 1
    shared['blk64'] = blk
    c64, s64 = _rope_tables(64)
    c32, s32 = _rope_tables(32)
    r64 = np.zeros((2, 128, TS), f)
    for p in range(128):
        d = p % 64
        r64[0, p] = c64[:, d % 32]
        r64[1, p] = -s64[:, d] if d < 32 else s64[:, d - 32]
    r32 = np.zeros((2, 32, TS), f)
    for p in range(32):
        r32[0, p] = c32[:, p % 16]
        r32[1, p] = -s32[:, p] if p < 16 else s32[:, p - 16]
    shared['rope64'] = r64
    shared['rope32'] = r32
    colv = np.zeros((128, 64), f)
    for l in range(DEPTH):
        o = 32 * l
        colv[:, o:o + 8] = np.asarray(inp['norm_g'][l]).reshape(8, 128).T
        colv[:, o + 8:o + 10] = np.asarray(inp['mla_q_norm'][l]).reshape(2, 128).T
        colv[:, o + 10] = np.asarray(inp['mla_kv_norm'][l])
        gq = np.asarray(inp['gqa_q_norm'][l])
        gk = np.asarray(inp['gqa_k_norm'][l])
        pp = np.arange(128) % 64
        colv[:, o + 11] = gq[pp]
        colv[:, o + 12] = gq[(pp + 32) % 64]
        colv[:, o + 13] = gk[pp]
        colv[:, o + 14] = gk[(pp + 32) % 64]
        colv[:, o + 15] = np.asarray(inp['diff_subln'][l])
        colv[:, o + 16:o + 20] = np.asarray(inp['ssm_d'][l]).reshape(4, 128).T
        colv[:, o + 20:o + 24] = np.asarray(inp['ssm_glu_b'][l]).reshape(4, 128).T
        colv[:, o + 24:o + 32] = np.asarray(inp['final_norm']).reshape(8, 128).T
    shared['colvecs'] = colv
    shared['bmod'] = np.ascontiguousarray(np.asarray(inp['b_mod'], dtype=f).reshape(DEPTH, 24, 128).transpose(2, 0, 1))
    a = np.zeros((DEPTH, 2, 3, 128, 16), f)
    for l in range(DEPTH):
        for d in range(2):
            a[l, d, 0] = np.asarray(inp['ssm_a_re'][l, d]).reshape(16, 128).T
            a[l, d, 1] = np.asarray(inp['ssm_a_im'][l, d]).reshape(16, 128).T
            a[l, d, 2] = np.repeat(np.asarray(inp['ssm_log_dt'][l, d]), 64).reshape(16, 128).T
    shared['ssm_a'] = a
    shared['ssm_b'] = np.ascontiguousarray(np.stack([inp['ssm_b_re'], inp['ssm_b_im']], axis=2), dtype=f)
    shared['ssm_c'] = np.ascontiguousarray(np.stack([inp['ssm_c_re'], inp['ssm_c_im']], axis=2), dtype=f)
    in_maps = []
    for i in range(8):
        b = i // 4
        m = dict(shared)
        m['xp'] = np.ascontiguousarray(np.asarray(inp['x_prompt'][4 * i:4 * i + 4], dtype=f).reshape(1024, D))
        m['xs'] = np.ascontiguousarray(inp['x_sample'][b], dtype=f)
        m['c_ckv'] = np.ascontiguousarray(inp['cache_mla_ckv'][b], dtype=f)
        m['c_krope'] = np.ascontiguousarray(inp['cache_mla_krope'][b], dtype=f)
        m['c_dk'] = np.ascontiguousarray(np.asarray(inp['cache_diff_k'][b], dtype=f).reshape(DEPTH, PAST, 512))
        m['c_dv'] = np.ascontiguousarray(np.asarray(inp['cache_diff_v'][b], dtype=f).reshape(DEPTH, PAST, 512))
        m['c_gk'] = np.ascontiguousarray(np.asarray(inp['cache_gqa_k'][b], dtype=f).reshape(DEPTH, PAST, 128))
        m['c_gv'] = np.ascontiguousarray(np.asarray(inp['cache_gqa_v'][b], dtype=f).reshape(DEPTH, PAST, 128))
        m['st_ssm'] = np.ascontiguousarray(np.asarray(inp['state_ssm'][b], dtype=f).reshape(DEPTH, 2, 16, 128, 2))
        cvec = np.zeros((128, 8, 2), f)
        cvec[:, :, 0] = np.asarray(inp['c_ctx']).reshape(8, 128).T
        cvec[:, :, 1] = np.asarray(inp['c'][b]).reshape(8, 128).T
        m['cvec'] = cvec
        j = i % 4
        mk = np.zeros((128, 4), f)
        mk[:, j] = 1.0
        m['mask4'] = mk
        m['rope64o'] = np.ascontiguousarray(shared['rope64'][:, :, 512 * j:512 * j + 512])
        m['rope32o'] = np.ascontiguousarray(shared['rope32'][:, :, 512 * j:512 * j + 512])
        in_maps.append(m)
    return in_maps


_CACHE = {}


def kernel(**inputs):
    inp = {k: np.asarray(v) for k, v in inputs.items()}
    if 'nc' not in _CACHE:
        _CACHE['nc'] = build_program()[0]
    nc = _CACHE['nc']
    in_maps = _host_prep(inp)
    res = run_bass_kernel_spmd(nc, in_maps, core_ids=list(range(8))).results
    f = np.float32
    y_prompt = np.zeros((32, TSEQ, D), f)
    y_sample = np.zeros((2, TS, D), f)
    n_ckv = np.zeros((32, DEPTH, TSEQ, 128), f)
    n_krope = np.zeros((32, DEPTH, TSEQ, 32), f)
    n_dk = np.zeros((32, DEPTH, TSEQ, 4, 2, 64), f)
    n_dv = np.zeros((32, DEPTH, TSEQ, 4, 128), f)
    n_gk = np.zeros((32, DEPTH, TSEQ, 2, 64), f)
    n_gv = np.zeros((32, DEPTH, TSEQ, 2, 64), f)
    n_ssm = np.zeros((32, DEPTH, 2, 32, 64, 2), f)
    for i in range(8):
        r = res[i]
        b, j = i // 4, i % 4
        sl = slice(4 * i, 4 * i + 4)
        y_prompt[sl] = r['yp'].reshape(4, TSEQ, D)
        y_sample[b, 512 * j:512 * j + 512] = r['ys']
        n_ckv[sl] = r['n_ckv'].reshape(DEPTH, 4, TSEQ, 128).transpose(1, 0, 2, 3)
        n_krope[sl] = r['n_krope'].reshape(DEPTH, 4, TSEQ, 32).transpose(1, 0, 2, 3)
        n_dk[sl] = r['n_dk'].reshape(DEPTH, 4, TSEQ, 4, 2, 64).transpose(1, 0, 2, 3, 4, 5)
        n_dv[sl] = r['n_dv'].reshape(DEPTH, 4, TSEQ, 4, 128).transpose(1, 0, 2, 3, 4)
        n_gk[sl] = r['n_gk'].reshape(DEPTH, 4, TSEQ, 2, 64).transpose(1, 0, 2, 3, 4)
        n_gv[sl] = r['n_gv'].reshape(DEPTH, 4, TSEQ, 2, 64).transpose(1, 0, 2, 3, 4)
        s = r['n_ssm'].reshape(DEPTH, 4, 2, 16, 2, 64, 2)
        n_ssm[sl] = s.reshape(DEPTH, 4, 2, 32, 64, 2).transpose(1, 0, 2, 3, 4, 5)
    return (y_prompt, y_sample, n_ckv, n_krope, n_dk, n_dv, n_gk, n_gv, n_ssm)
```
